# Optimizing a Trainium2 kernel written in Bass

```python
import math
import jax, jax.numpy as jnp
from jax import lax
import numpy as np

D_MODEL = 1024
BATCH = 4
SEQ = 4096
DEPTH = 4

N_MEM = 256
MIX_HALF = D_MODEL // 2
A_HEADS = 4
A_HEAD_DIM = MIX_HALF // A_HEADS
DILATED_GROUPS = ((128, 1), (512, 4), (2048, 16))
PAD_MULTIPLE = 2048
RG_WIDTH = MIX_HALF
RG_BLOCKS = 4
RG_BLOCK_DIM = RG_WIDTH // RG_BLOCKS
RG_C = 8.0
CONV_WIDTH = 4
POOL_WINDOWS = (2, 4, 8, 16)
POOL_GROUP_DIM = D_MODEL // len(POOL_WINDOWS)
XA_HEADS = 4
XA_HEAD_DIM = D_MODEL // XA_HEADS
D_FF = 4 * D_MODEL
IN_COLS = 3 * MIX_HALF + 2 * RG_WIDTH
N_EVEN = (DEPTH + 1) // 2
N_ODD = DEPTH // 2
EPS = 1e-6

kernel_name = 'hybrid_dilated_rglru_pool_trunk'


def rms_norm(x, g):
    xf = x.astype(jnp.float32)
    y = xf * lax.rsqrt(jnp.mean(xf * xf, axis=-1, keepdims=True) + EPS)
    return (y * g.astype(jnp.float32)).astype(x.dtype)


def dilated_window_attn(q, k, v, window, dilation):
    B, S, H, C = q.shape
    L = window // dilation
    nb = S // window

    def split(t):
        return t.reshape(B, nb, L, dilation, H, C)

    def with_prev(t):
        prev = jnp.pad(t, ((0, 0), (1, 0), (0, 0), (0, 0), (0, 0), (0, 0)))[:, :-1]
        return jnp.concatenate([prev, t], axis=2)

    qb = split(q)
    kc = with_prev(split(k))
    vc = with_prev(split(v))
    s = jnp.einsum('bnqrhc,bnkrhc->bnrhqk', qb, kc).astype(jnp.float32) * (C ** -0.5)
    qi = jnp.arange(L)[:, None]
    kj = jnp.arange(2 * L)[None, :]
    dist = qi + L - kj
    band = (dist >= 0) & (dist <= L)
    first = (jnp.arange(nb) == 0)[:, None, None]
    mask = band[None] & (jnp.logical_not(first) | (kj >= L)[None])
    s = jnp.where(mask[None, :, None, None], s, -jnp.inf)
    m = jnp.max(s, axis=-1, keepdims=True)
    p = jnp.exp(s - m)
    den = jnp.sum(p, axis=-1)
    o = jnp.einsum('bnrhqk,bnkrhc->bnqrhc', p, vc.astype(jnp.float32))
    den_t = den.transpose(0, 1, 4, 2, 3)
    o = o / den_t[..., None]
    lse = m[..., 0].transpose(0, 1, 4, 2, 3) + jnp.log(den_t)
    return o.reshape(B, S, H, C), lse.reshape(B, S, H)


def mixer_attn_rglru(h, w_in, q_g, k_g, conv_w, conv_b, ga_w, ga_b, gx_w, gx_b, lam, w_out):
    B, S, _ = h.shape
    proj = h @ w_in
    q, k, v, xr, gate = jnp.split(
        proj, [MIX_HALF, 2 * MIX_HALF, 3 * MIX_HALF, 3 * MIX_HALF + RG_WIDTH], axis=-1)

    q = rms_norm(q.reshape(B, S, A_HEADS, A_HEAD_DIM), q_g)
    k = rms_norm(k.reshape(B, S, A_HEADS, A_HEAD_DIM), k_g)
    v = v.reshape(B, S, A_HEADS, A_HEAD_DIM)
    Sp = -(-S // PAD_MULTIPLE) * PAD_MULTIPLE
    pad = ((0, 0), (0, Sp - S), (0, 0), (0, 0))
    qp, kp, vp = jnp.pad(q, pad), jnp.pad(k, pad), jnp.pad(v, pad)
    outs, lses = [], []
    for window, dilation in DILATED_GROUPS:
        o_g, lse_g = dilated_window_attn(qp, kp, vp, window, dilation)
        outs.append(o_g)
        lses.append(lse_g)
    wts = jax.nn.softmax(jnp.stack(lses), axis=0)
    o_a = jnp.sum(wts[..., None] * jnp.stack(outs), axis=0)[:, :S]
    o_a = o_a.reshape(B, S, MIX_HALF).astype(h.dtype)

    xpad = jnp.pad(xr, ((0, 0), (CONV_WIDTH - 1, 0), (0, 0)))
    xc = conv_b + sum(xpad[:, j:j + S] * conv_w[j] for j in range(CONV_WIDTH))
    xb = xc.reshape(B, S, RG_BLOCKS, RG_BLOCK_DIM)
    r = jax.nn.sigmoid(jnp.einsum('bsgc,gcd->bsgd', xb, ga_w).reshape(B, S, RG_WIDTH) + ga_b)
    i = jax.nn.sigmoid(jnp.einsum('bsgc,gcd->bsgd', xb, gx_w).reshape(B, S, RG_WIDTH) + gx_b)
    log_a = -RG_C * r.astype(jnp.float32) * jax.nn.softplus(-lam.astype(jnp.float32))
    a = jnp.exp(log_a)
    u = jnp.sqrt(-jnp.expm1(2.0 * log_a)) * (i * xc).astype(jnp.float32)

    def combine(e1, e2):
        a1, b1 = e1
        a2, b2 = e2
        return a1 * a2, a2 * b1 + b2

    _, hs = lax.associative_scan(combine, (a, u), axis=1)
    y_b = hs.astype(h.dtype) * jax.nn.gelu(gate)

    return jnp.concatenate([o_a, y_b], axis=-1) @ w_out


def pool_mixer(h, pool_w, scale):
    B, S, D = h.shape
    hf = h.astype(jnp.float32)
    cs = jnp.pad(jnp.cumsum(hf, axis=1), ((0, 0), (1, 0), (0, 0)))
    t = jnp.arange(S)
    diffs = []
    for g, w in enumerate(POOL_WINDOWS):
        sl = slice(g * POOL_GROUP_DIM, (g + 1) * POOL_GROUP_DIM)
        lo = jnp.maximum(t + 1 - w, 0)
        window_sum = cs[:, 1:, sl] - cs[:, lo, sl]
        count = jnp.minimum(t + 1, w).astype(jnp.float32)[None, :, None]
        diffs.append(window_sum / count - hf[..., sl])
    d = jnp.stack(diffs, axis=2)
    out = jnp.einsum('bsgc,gcd->bsgd', d, pool_w.astype(jnp.float32)).reshape(B, S, D)
    return (out * scale.astype(jnp.float32)).astype(h.dtype)


def memory_xattn(h, mem_n, w_q, w_kv, q_g, k_g, w_o):
    B, S, D = h.shape
    M = mem_n.shape[1]
    q = rms_norm((h @ w_q).reshape(B, S, XA_HEADS, XA_HEAD_DIM), q_g)
    k, v = jnp.split(mem_n @ w_kv, 2, axis=-1)
    k = rms_norm(k.reshape(B, M, XA_HEADS, XA_HEAD_DIM), k_g)
    v = v.reshape(B, M, XA_HEADS, XA_HEAD_DIM)
    s = jnp.einsum('bshc,bmhc->bhsm', q, k).astype(jnp.float32) * (XA_HEAD_DIM ** -0.5)
    p = jax.nn.softmax(s, axis=-1)
    o = jnp.einsum('bhsm,bmhc->bshc', p, v.astype(jnp.float32)).astype(h.dtype)
    return o.reshape(B, S, D) @ w_o


def sq_relu_mlp(h, w1, w2):
    return jnp.square(jax.nn.relu(h @ w1)) @ w2


def setup_inputs(seed: int = 0) -> dict:
    key = jax.random.key(seed)
    ks = jax.random.split(key, 26)
    f32 = jnp.float32

    def nrm(k, shape, fan_in):
        return jax.random.normal(k, shape, f32) * fan_in ** -0.5

    def gain(k, shape):
        return 1.0 + 0.02 * jax.random.normal(k, shape, f32)

    def bias(k, shape):
        return 0.02 * jax.random.normal(k, shape, f32)

    u = jax.random.uniform(ks[15], (N_EVEN, RG_WIDTH), f32, minval=0.9, maxval=0.999)
    s = u ** (1.0 / RG_C)
    lam = jnp.log(s) - jnp.log1p(-s)
    return {
        'x': jax.random.normal(ks[0], (BATCH, SEQ, D_MODEL), f32),
        'mem': jax.random.normal(ks[1], (BATCH, N_MEM, D_MODEL), f32),
        'mem_norm_g': gain(ks[2], (D_MODEL,)),
        'mix_norm_g': gain(ks[3], (DEPTH, D_MODEL)),
        'xattn_norm_g': gain(ks[4], (DEPTH, D_MODEL)),
        'mlp_norm_g': gain(ks[5], (DEPTH, D_MODEL)),
        'ev_w_in': nrm(ks[6], (N_EVEN, D_MODEL, IN_COLS), D_MODEL),
        'ev_q_norm_g': gain(ks[7], (N_EVEN, A_HEAD_DIM)),
        'ev_k_norm_g': gain(ks[8], (N_EVEN, A_HEAD_DIM)),
        'ev_conv_w': nrm(ks[9], (N_EVEN, CONV_WIDTH, RG_WIDTH), CONV_WIDTH),
        'ev_conv_b': bias(ks[10], (N_EVEN, RG_WIDTH)),
        'ev_gate_a_w': nrm(ks[11], (N_EVEN, RG_BLOCKS, RG_BLOCK_DIM, RG_BLOCK_DIM), RG_BLOCK_DIM),
        'ev_gate_a_b': bias(ks[12], (N_EVEN, RG_WIDTH)),
        'ev_gate_x_w': nrm(ks[13], (N_EVEN, RG_BLOCKS, RG_BLOCK_DIM, RG_BLOCK_DIM), RG_BLOCK_DIM),
        'ev_gate_x_b': bias(ks[14], (N_EVEN, RG_WIDTH)),
        'ev_lambda': lam,
        'ev_w_out': nrm(ks[16], (N_EVEN, 2 * MIX_HALF, D_MODEL), 2 * MIX_HALF),
        'od_pool_w': nrm(ks[17], (N_ODD, len(POOL_WINDOWS), POOL_GROUP_DIM, POOL_GROUP_DIM), POOL_GROUP_DIM),
        'od_scale': 0.5 + 0.05 * jax.random.normal(ks[18], (N_ODD, D_MODEL), f32),
        'xa_w_q': nrm(ks[19], (DEPTH, D_MODEL, D_MODEL), D_MODEL),
        'xa_w_kv': nrm(ks[20], (DEPTH, D_MODEL, 2 * D_MODEL), D_MODEL),
        'xa_q_norm_g': gain(ks[21], (DEPTH, XA_HEAD_DIM)),
        'xa_k_norm_g': gain(ks[22], (DEPTH, XA_HEAD_DIM)),
        'xa_w_o': nrm(ks[23], (DEPTH, D_MODEL, D_MODEL), D_MODEL),
        'mlp_w1': nrm(ks[24], (DEPTH, D_MODEL, D_FF), D_MODEL),
        'mlp_w2': nrm(ks[25], (DEPTH, D_FF, D_MODEL), D_FF),
    }


def reference(x, mem, mem_norm_g, mix_norm_g, xattn_norm_g, mlp_norm_g,
              ev_w_in, ev_q_norm_g, ev_k_norm_g, ev_conv_w, ev_conv_b,
              ev_gate_a_w, ev_gate_a_b, ev_gate_x_w, ev_gate_x_b, ev_lambda, ev_w_out,
              od_pool_w, od_scale,
              xa_w_q, xa_w_kv, xa_q_norm_g, xa_k_norm_g, xa_w_o,
              mlp_w1, mlp_w2):
    mem_n = rms_norm(mem, mem_norm_g)
    for l in range(DEPTH):
        h = rms_norm(x, mix_norm_g[l])
        if l % 2 == 0:
            e = l // 2
            x = x + mixer_attn_rglru(h, ev_w_in[e], ev_q_norm_g[e], ev_k_norm_g[e],
                                     ev_conv_w[e], ev_conv_b[e], ev_gate_a_w[e], ev_gate_a_b[e],
                                     ev_gate_x_w[e], ev_gate_x_b[e], ev_lambda[e], ev_w_out[e])
        else:
            o = l // 2
            x = x + pool_mixer(h, od_pool_w[o], od_scale[o])
        x = x + memory_xattn(rms_norm(x, xattn_norm_g[l]), mem_n, xa_w_q[l], xa_w_kv[l],
                             xa_q_norm_g[l], xa_k_norm_g[l], xa_w_o[l])
        x = x + sq_relu_mlp(rms_norm(x, mlp_norm_g[l]), mlp_w1[l], mlp_w2[l])
    return x
```

```python
from contextlib import ExitStack
import numpy as np
import concourse.bass as bass
import concourse.mybir as mybir
from concourse.bass_utils import run_bass_kernel_spmd

F32 = mybir.dt.float32
BF16 = mybir.dt.bfloat16
AF = mybir.ActivationFunctionType
ALU = mybir.AluOpType

T = 2048
NT = 4
TT = 512
HALO = 16
HW = HALO + T
KC = 8
EPS = 1e-6
MASKW = 384 + 2048 + 512
NEG = -30000.0
GROUPS = [[0, 1], [2, 3], [4, 5], [6, 7]]
SAME_ENGINE_SYNC = True


def tile_sl(tt):
    return slice(tt * TT, (tt + 1) * TT)


def param_layout():
    off = {}
    n = 0

    def add(name, w):
        nonlocal n
        off[name] = (n, w)
        n += w

    for l in range(4):
        add(f"mixg{l}", 8)
        add(f"xag{l}", 8)
        add(f"mlpg{l}", 8)
        add(f"xqg{l}", 2)
        add(f"xkg{l}", 2)
        if l % 2 == 0:
            add(f"convw{l}", 16)
            add(f"convb{l}", 4)
            add(f"gab{l}", 4)
            add(f"gxb{l}", 4)
            add(f"lam{l}", 4)
            add(f"qg{l}", 1)
            add(f"kg{l}", 1)
        else:
            add(f"scale{l}", 8)
    add("memg", 8)
    add("flag", 1)
    add("ctxbias", 1)
    add("invcnt", 8 * 16)
    return off, n


POFF, NP = param_layout()


def pack_params(inp, half):
    P = np.zeros((128, NP), np.float32)

    def put(name, arr):
        o, w = POFF[name]
        arr = np.asarray(arr, np.float32)
        assert arr.shape == (128, w), (name, arr.shape, w)
        P[:, o:o + w] = arr

    def cols(v):
        v = np.asarray(v, np.float32)
        return v.reshape(-1, 128).T

    for l in range(4):
        put(f"mixg{l}", cols(inp["mix_norm_g"][l]))
        put(f"xag{l}", cols(inp["xattn_norm_g"][l]))
        put(f"mlpg{l}", cols(inp["mlp_norm_g"][l]))
        put(f"xqg{l}", cols(inp["xa_q_norm_g"][l]))
        put(f"xkg{l}", cols(inp["xa_k_norm_g"][l]))
        if l % 2 == 0:
            e = l // 2
            cw = np.asarray(inp["ev_conv_w"][e], np.float32)
            put(f"convw{l}", np.concatenate([cols(cw[j]) for j in range(4)], axis=1))
            put(f"convb{l}", cols(inp["ev_conv_b"][e]))
            put(f"gab{l}", cols(inp["ev_gate_a_b"][e]))
            put(f"gxb{l}", cols(inp["ev_gate_x_b"][e]))
            put(f"lam{l}", cols(inp["ev_lambda"][e]))
            put(f"qg{l}", cols(inp["ev_q_norm_g"][e]))
            put(f"kg{l}", cols(inp["ev_k_norm_g"][e]))
        else:
            put(f"scale{l}", cols(inp["od_scale"][l // 2]))
    put("memg", cols(inp["mem_norm_g"]))
    put("flag", np.full((128, 1), float(half), np.float32))
    put("ctxbias", np.full((128, 1), 0.0 if half else NEG, np.float32))
    ic = np.zeros((8, 16), np.float32)
    for c in range(8):
        w = 2 ** (c // 2 + 1)
        for t in range(16):
            ic[c, t] = 1.0 / w if half else 1.0 / min(t + 1, w)
    put("invcnt", np.broadcast_to(ic.reshape(1, 128), (128, 128)))
    return P


def make_mask():
    ki = np.arange(128)[:, None]
    x = np.arange(MASKW)[None, :]
    d = x - ki - 384
    m = ((d >= 0) & (d <= 128)).astype(np.float32)
    m += ((d >= 0) & (d % 4 == 0) & (d <= 512)).astype(np.float32)
    m += ((d >= 0) & (d % 16 == 0) & (d <= 2048)).astype(np.float32)
    return m


class Trk:
    CH = 4000

    def __init__(self, nc, stack):
        self.nc = nc
        self.stack = stack
        self.eng = dict(pe=nc.tensor, act=nc.scalar, dve=nc.vector, pool=nc.gpsimd, sp=nc.sync)
        self.cnt = {e: 0 for e in self.eng}
        self.sems = {e: [] for e in self.eng}
        self.seen = {e: {f: 0 for f in self.eng} for e in self.eng}
        self.snap = {e: {} for e in self.eng}
        self.dsem = {}
        self.dcnt = {}
        self.dseen = {e: {} for e in self.eng}
        self.lastw = {}
        self.readers = {}
        self.dma_tokens = set()
        self.nops = 0
        self.stream = {e: [] for e in self.eng}

    def _sem(self, e, seq):
        k = (seq - 1) // self.CH
        while len(self.sems[e]) <= k:
            self.sems[e].append(
                self.stack.enter_context(self.nc.semaphore(f"s_{e}_{len(self.sems[e])}")))
        return self.sems[e][k], (seq - 1) % self.CH + 1

    def _wait(self, e, h):
        if h[0] == "eng":
            _, f, s = h
            if f == e and (e == "pe" or not SAME_ENGINE_SYNC):
                return
            if self.seen[e][f] >= s:
                return
            sem, v = self._sem(f, s)
            self.eng[e].wait_ge(sem, v)
            self.stream[e].append(("w", ("eng", f, (s - 1) // self.CH), v))
            self.seen[e][f] = s
            sn = self.snap[f].get(s)
            if sn and f != e:
                for g, v2 in sn.items():
                    if g != e and v2 > self.seen[e][g]:
                        self.seen[e][g] = v2
        else:
            _, key, v = h
            if self.dseen[e].get(key, 0) >= v:
                return
            self.eng[e].wait_ge(self.dsem[key], v)
            self.stream[e].append(("w", ("dma", key), v))
            self.dseen[e][key] = v

    def _deps(self, e, r, w):
        hs = []
        for t in r:
            h = self.lastw.get(t)
            if h:
                hs.append(h)
        for t in w:
            h = self.lastw.get(t)
            if h:
                hs.append(h)
            hs.extend(self.readers.get(t, ()))
        best = {}
        for h in hs:
            k = (h[0], h[1])
            if k not in best or h[2] > best[k][2]:
                best[k] = h
        for h in best.values():
            self._wait(e, h)

    def _commit(self, h, r, w):
        for t in r:
            self.readers.setdefault(t, []).append(h)
        for t in w:
            self.lastw[t] = h
            self.readers[t] = []

    def op(self, e, fn, r=(), w=()):
        self._deps(e, r, w)
        ins = fn(self.eng[e])
        self.cnt[e] += 1
        s = self.cnt[e]
        sem, v = self._sem(e, s)
        ins.then_inc(sem, 1)
        self.stream[e].append(("i", ("eng", e, (s - 1) // self.CH), 1))
        self.snap[e][s] = dict(self.seen[e])
        self._commit(("eng", e, s), r, w)
        self.nops += 1
        return ins

    def mmg(self, out, pairs, r=(), w=()):
        self._deps("pe", r, w)
        n = len(pairs)
        ins = None
        for i, (lhsT, rhs) in enumerate(pairs):
            ins = self.nc.tensor.matmul(out, lhsT, rhs, start=(i == 0), stop=(i == n - 1))
        self.cnt["pe"] += 1
        s = self.cnt["pe"]
        sem, v = self._sem("pe", s)
        ins.then_inc(sem, 1)
        self.stream["pe"].append(("i", ("eng", "pe", (s - 1) // self.CH), 1))
        self.snap["pe"][s] = dict(self.seen["pe"])
        self._commit(("eng", "pe", s), r, w)
        self.nops += n

    def mm1(self, out, lhsT, rhs, start, stop, r=(), w=()):
        self._deps("pe", r, w)
        ins = self.nc.tensor.matmul(out, lhsT, rhs, start=start, stop=stop)
        self.cnt["pe"] += 1
        s = self.cnt["pe"]
        sem, v = self._sem("pe", s)
        ins.then_inc(sem, 1)
        self.stream["pe"].append(("i", ("eng", "pe", (s - 1) // self.CH), 1))
        self.snap["pe"][s] = dict(self.seen["pe"])
        self._commit(("eng", "pe", s), r, w)
        self.nops += 1

    def dma(self, q, key, out, in_, r=(), w=()):
        if key not in self.dsem:
            self.dsem[key] = self.stack.enter_context(self.nc.semaphore(f"d_{key}"))
            self.dcnt[key] = 0
        self._deps(q, r, w)
        self.eng[q].dma_start(out=out, in_=in_).then_inc(self.dsem[key], 16)
        self.stream[q].append(("i", ("dma", key), 16))
        self.dcnt[key] += 16
        h = ("dma", key, self.dcnt[key])
        self._commit(h, r, w)
        self.dma_tokens.update(r)
        self.dma_tokens.update(w)
        return h

    def barrier(self):
        es = ["pe", "act", "dve"]
        for e in es:
            for f in es:
                if f != e and self.cnt[f] > self.seen[e][f]:
                    self._wait(e, ("eng", f, self.cnt[f]))
        for t in list(self.lastw):
            if t in self.dma_tokens:
                continue
            del self.lastw[t]
            self.readers.pop(t, None)
        for t in list(self.readers):
            if t not in self.dma_tokens and t not in self.lastw:
                del self.readers[t]

    def check_deadlock(self):
        val = {}
        ptr = {e: 0 for e in self.eng}
        prog = True
        while prog:
            prog = False
            for e in self.eng:
                st = self.stream[e]
                while ptr[e] < len(st):
                    k, key, v = st[ptr[e]]
                    if k == "w":
                        if val.get(key, 0) < v:
                            break
                    else:
                        val[key] = val.get(key, 0) + v
                    ptr[e] += 1
                    prog = True
        stuck = {e: (ptr[e], len(self.stream[e]), self.stream[e][ptr[e]], val.get(self.stream[e][ptr[e]][1], 0))
                 for e in self.eng if ptr[e] < len(self.stream[e])}
        assert not stuck, ("DEADLOCK", stuck)

    def wait_all(self, e):
        for t, h in self.lastw.items():
            self._wait(e, h)
        for t, hs in self.readers.items():
            for h in hs:
                self._wait(e, h)


def build(layers):
    nc = bass.Bass(target_bir_lowering=False)
    stack = ExitStack()

    def din(name, shape):
        return nc.dram_tensor(name, list(shape), F32, kind="ExternalInput").ap()

    x_d = din("x", (128, 8, T))
    xh_d = din("xh", (128, 8, HALO))
    mem_d = din("mem", (128, 8, 256))
    par_d = din("params", (128, NP))
    mask_d = din("mask", (128, MASKW))
    w_in_d = din("ev_w_in", (2, 1024, 2560))
    gaw_d = din("ev_gate_a_w", (2, 4, 128, 128))
    gxw_d = din("ev_gate_x_w", (2, 4, 128, 128))
    w_out_d = din("ev_w_out", (2, 1024, 1024))
    poolw_d = din("od_pool_w", (2, 4, 256, 256))
    xwq_d = din("xa_w_q", (4, 1024, 1024))
    xwkv_d = din("xa_w_kv", (4, 1024, 2048))
    xwo_d = din("xa_w_o", (4, 1024, 1024))
    w1_d = din("mlp_w1", (4, 1024, 4096))
    w2_d = din("mlp_w2", (4, 4096, 1024))
    y_d = nc.dram_tensor("y", [128, 8, T], F32, kind="ExternalOutput").ap()

    ncc_kv = sum(1 for l in layers if l % 2 == 0)
    b0 = [nc.dram_tensor(f"b0_{i}", [128, 128], F32) for i in range(len(layers))]
    g0 = [nc.dram_tensor(f"g0_{i}", [256, 128], F32) for i in range(len(layers))]
    b2 = [[nc.dram_tensor(f"b2_{i}_{h}", [256, 2048], BF16) for h in range(4)] for i in range(ncc_kv)]
    g2 = [[nc.dram_tensor(f"g2_{i}_{h}", [512, 2048], BF16) for h in range(4)] for i in range(ncc_kv)]
    b3 = [nc.dram_tensor(f"b3_{i}", [128, 4], F32) for i in range(ncc_kv)]
    g3 = [nc.dram_tensor(f"g3_{i}", [256, 4], F32) for i in range(ncc_kv)]

    def sb(name, shape, dt):
        return stack.enter_context(nc.sbuf_tensor(name, list(shape), dt))

    xres = sb("xres", (128, 8, T), F32)
    par = sb("par", (128, NP), F32)
    xhalo = sb("xhalo", (128, 8, HALO), F32)
    hprev = sb("hprev", (128, 8, HALO), F32)
    tf = sb("tf", (128, 8, 528), F32)
    sm = sb("sm", (128, 96), F32)
    hb = sb("hb", (128, 8 * HW), BF16)
    qk = sb("qk", (128, 8, T), BF16)
    vv = sb("vv", (128, 16, 512), BF16)
    NSLOT = 3
    wsl = sb("wsl", (128, NSLOT, 4096), BF16)
    bt = sb("bt", (128, 9, 512), BF16)
    yb = bt[:, 2:6, :].rearrange("p a b -> p (a b)")
    maskt = sb("maskt", (128, MASKW), BF16)
    memn = sb("memn", (128, 8, 256), BF16)
    ones = sb("ones", (128, 4, 128), BF16)
    ps = [stack.enter_context(nc.psum_tensor(f"ps{i}", [128, 512], F32)) for i in range(8)]

    hbuf = hb[:, :].rearrange("p (c t) -> p c t", c=8)
    kctx = hb[:, 0:8192].rearrange("p (h t) -> p h t", h=4)
    vctx = hb[:, 8192:16384].rearrange("p (b f) -> p b f", b=16)

    tk = Trk(nc, stack)
    op, mmg, dma = tk.op, tk.mmg, tk.dma

    def pcol(name, i=0, n=1):
        o, w = POFF[name]
        return par[:, o + i:o + i + n]

    class WStream:
        def __init__(self):
            self.blocks = []
            self.issued = 0
            self.used = 0
            self.released = set()
            self.cur = {}

        def add(self, tag, parts):
            self.blocks.append((tag, parts))

        def _issue(self, i):
            tag, parts = self.blocks[i]
            s = i % NSLOT
            for (lo, shape, src) in parts:
                n = int(np.prod(shape))
                dst = wsl[:, s, lo:lo + n]
                if len(shape) == 1:
                    pass
                elif len(shape) == 2:
                    dst = dst.rearrange("p (a b) -> p a b", a=shape[0])
                elif len(shape) == 3:
                    dst = dst.rearrange("p (a b c) -> p a b c", a=shape[0], b=shape[1])
                dma("pool", f"w{s}", dst, src, w=[("w", s)])

        def _pump(self):
            while self.issued < len(self.blocks) and (
                    self.issued < NSLOT or (self.issued - NSLOT) in self.released):
                self._issue(self.issued)
                self.issued += 1

        def next(self, tag):
            i = self.used
            assert self.blocks[i][0] == tag, (self.blocks[i][0], tag)
            self._pump()
            assert self.issued > i, ("weight slot not released", tag)
            self.used += 1
            s = i % NSLOT
            self.cur[s] = i
            return s, wsl[:, s, :]

        def release(self, s):
            self.released.add(self.cur[s])
            self._pump()

    ws = WStream()

    def wview(s, a, b, lo=0):
        return wsl[:, s, lo:lo + a * b].rearrange("p (a b) -> p a b", a=a)

    def cols_block(wd2, c0, n=512):
        return wd2.rearrange("(kc p) n -> p kc n", p=128)[:, :, c0:c0 + n]

    def rows_block(wd2, r0, nchunks):
        return wd2[r0:r0 + nchunks * 128, :].rearrange("(j p) n -> p j n", p=128)

    def schedule_layer(l):
        if l % 2 == 0:
            e = l // 2
            w = w_in_d[e]
            for c in range(4):
                ws.add(f"rg{l}_{c}", [(0, (8, 128), cols_block(w, 1536 + c * 128, 128)),
                                      (1024, (8, 128), cols_block(w, 2048 + c * 128, 128)),
                                      (2048, (128,), gaw_d[e, c]),
                                      (2176, (128,), gxw_d[e, c])])
            ws.add(f"v{l}", [(0, (8, 512), cols_block(w, 1024))])
            ws.add(f"woy{l}", [(0, (4, 1024), rows_block(w_out_d[e], 512, 4))])
            ws.add(f"k{l}", [(0, (8, 512), cols_block(w, 512))])
            ws.add(f"q{l}", [(0, (8, 512), cols_block(w, 0))])
            ws.add(f"woa{l}", [(0, (4, 1024), rows_block(w_out_d[e], 0, 4))])
        else:
            o = l // 2
            ws.add(f"pool{l}", [(i * 1024, (4, 256),
                                 poolw_d[o][:, i * 128:(i + 1) * 128, :].rearrange("g p n -> p g n"))
                                for i in range(2)])
        kv = xwkv_d[l]
        for j in range(2):
            ws.add(f"xk{l}_{j}", [(0, (8, 512), cols_block(kv, j * 512))])
        for j in range(2):
            ws.add(f"xv{l}_{j}", [(0, (8, 512), cols_block(kv, 1024 + j * 512))])
        for j in range(2):
            ws.add(f"xq{l}_{j}", [(0, (8, 512), cols_block(xwq_d[l], j * 512))])
        for j in range(2):
            ws.add(f"xo{l}_{j}", [(0, (4, 1024), rows_block(xwo_d[l], j * 512, 4))])
        for g in range(8):
            ws.add(f"w1_{l}_{g}", [(0, (8, 512), cols_block(w1_d[l], g * 512))])
            ws.add(f"w2_{l}_{g}", [(0, (4, 1024), rows_block(w2_d[l], g * 512, 4))])

    for l in layers:
        schedule_layer(l)

    def X(c, tt):
        return ("x", c, tt)

    def HB(c, tt):
        return ("hb", c, tt)

    def QK(c, tt):
        return ("qk", c, tt)

    ALLHB = [("hb", c, tt) for c in range(8) for tt in range(4)] + [("hbh",)]

    def act_fn(out, in_, func, **kw):
        return lambda e: e.activation(out=out, in_=in_, func=func, **kw)

    def emit_rstd(psum_ap, dst, r, w):
        op("act", act_fn(dst, psum_ap, AF.Ln, bias=EPSC), r=list(r) + [("epsc",)], w=w)
        op("act", act_fn(dst, dst, AF.Exp, scale=-0.5), r=w, w=w)

    def emit_recip(psum_ap, dst, r, w):
        op("act", act_fn(dst, psum_ap, AF.Ln), r=list(r), w=w)
        op("act", act_fn(dst, dst, AF.Exp, scale=-1.0), r=w, w=w)

    def stt(out, in0, scalar, in1, op0, op1):
        return lambda e: e.scalar_tensor_tensor(out, in0, scalar, in1, op0, op1)

    def tt_(out, in0, in1, opx):
        return lambda e: e.tensor_tensor(out, in0, in1, opx)

    def ts_(out, in0, s1, s2, op0, op1=None):
        if op1 is None:
            return lambda e: e.tensor_scalar(out, in0, s1, None, op0)
        return lambda e: e.tensor_scalar(out, in0, s1, s2, op0, op1)

    EPSC = sm[:, 1:2]
    ONE_D, ONE_128, ONE_256, ONE_1 = (ones[:, i, :] for i in range(4))
    sq = [bt[:, 0, :], bt[:, 1, :]]
    eb = [bt[:, 2, :], bt[:, 3, :]]
    pb = [bt[:, 4, :], bt[:, 5, :]]
    SQ = [("sq", 0), ("sq", 1)]
    EB = [("eb", 0), ("eb", 1)]
    PB = [("pb", 0), ("pb", 1)]
    PS = [("ps", i) for i in range(8)]
    TF = [("tf", i) for i in range(8)]

    for tt in range(NT):
        dma("sp", f"xin{tt}", xres[:, :, tile_sl(tt)], x_d[:, :, tile_sl(tt)],
            w=[X(c, tt) for c in range(8)])
    dma("sp", "par", par[:, :], par_d[:, :], w=[("par",)])
    dma("sp", "xh", xhalo[:, :, :], xh_d[:, :, :], w=[("xhalo",)])
    memf = tf[:, 0:8, 0:256]
    dma("sp", "mem", memf, mem_d[:, :, :], w=TF[0:8])
    dma("pool", "mask", maskt[:, :], mask_d[:, :], w=[("mask",)])
    for i, val in enumerate([1.0 / 1024, 1.0 / 128, 1.0 / 256, 1.0]):
        op("dve", lambda e, i=i, val=val: e.memset(ones[:, i, :], val), w=[("ones",)])
    op("dve", lambda e: e.memset(bt[:, 6, :], 0.0), w=[("sm0",)])
    ZT = bt[:, 6, :]
    op("dve", lambda e: e.memset(sm[:, 1:2], EPS), w=[("epsc",)])

    sqm = qk[:, 0, 0:2048].rearrange("p (c m) -> p c m", c=8)
    for c in range(8):
        op("act", act_fn(sqm[:, c, :], memf[:, c, :], AF.Square), r=TF[0:8], w=[QK(0, 0)])
    mmg(ps[0][:, 0:256], [(ONE_D, sqm[:, c, :]) for c in range(8)],
        r=[QK(0, 0), ("ones",)], w=[PS[0]])
    emit_rstd(ps[0][:, 0:256], tf[:, 4, 256:512], r=[PS[0]], w=[("mrs",)])
    for c in range(8):
        op("dve", stt(memn[:, c, :], memf[:, c, :], pcol("memg", c), tf[:, 4, 256:512],
                      ALU.mult, ALU.mult),
           r=TF[0:8] + [("mrs",), ("par",)], w=[("memn",)])
    tk.barrier()

    def halo_prep(gname, to_hbuf):
        op("dve", ts_(xhalo[:, :, :], xhalo[:, :, :], pcol("flag"), None, ALU.mult),
           r=[("par",)], w=[("xhalo",)])
        sqh = bt[:, 0, 0:128].rearrange("p (c t) -> p c t", c=8)
        op("act", act_fn(sqh, xhalo[:, :, :], AF.Square), r=[("xhalo",)], w=[SQ[0]])
        mmg(ps[7][:, 0:HALO], [(ONE_D, sqh[:, c, :]) for c in range(8)],
            r=[SQ[0], ("ones",)], w=[PS[7]])
        rh = sm[:, 16:32]
        emit_rstd(ps[7][:, 0:HALO], rh, r=[PS[7]], w=[("rh",)])
        for c in range(8):
            dst = hbuf[:, c, 0:HALO] if to_hbuf else hprev[:, c, :]
            op("dve", stt(dst, xhalo[:, c, :], pcol(gname, c), rh, ALU.mult, ALU.mult),
               r=[("xhalo",), ("rh",), ("par",)], w=[("hbh",)] if to_hbuf else [("hprev", c)])

    def norm_tile_rstd(tt, dst_tf):
        for c in range(8):
            op("act", act_fn(sq[c % 2], xres[:, c, tile_sl(tt)], AF.Square),
               r=[X(c, tt)], w=[SQ[c % 2]])
            tk.mm1(ps[7 - tt % 2][:, :], ONE_D, sq[c % 2], c == 0, c == 7, r=[SQ[c % 2], ("ones",)],
                   w=[PS[7 - tt % 2]])
        emit_rstd(ps[7 - tt % 2][:, :], tf[:, dst_tf, 0:512], r=[PS[7 - tt % 2]], w=[TF[dst_tf]])

    def norm_to_hbuf(gname):
        for tt in range(NT):
            ri = 7 - tt % 2
            norm_tile_rstd(tt, ri)
            for c in range(8):
                op("dve", stt(hbuf[:, c, HALO + tt * TT:HALO + (tt + 1) * TT],
                              xres[:, c, tile_sl(tt)], pcol(gname, c), tf[:, ri, 0:512],
                              ALU.mult, ALU.mult),
                   r=[X(c, tt), TF[ri], ("par",)], w=[HB(c, tt)])

    def hslice(c, tt):
        return hbuf[:, c, HALO + tt * TT:HALO + (tt + 1) * TT]

    def add_to_x(m, tt, psum_ap, psi):
        op("dve", tt_(xres[:, m, tile_sl(tt)], psum_ap, xres[:, m, tile_sl(tt)], ALU.add),
           r=[PS[psi]], w=[X(m, tt)])

    cc_count = [0]

    def collective(src_d, dst_d, wait_handles):
        for h in wait_handles:
            tk._wait("pool", h)
        sem = stack.enter_context(nc.semaphore(f"cc{cc_count[0]}"))
        cc_count[0] += 1
        nc.gpsimd.collective_compute(
            "AllGather", ALU.bypass, replica_groups=GROUPS,
            ins=[src_d.ap().opt()], outs=[dst_d.ap().opt()]).then_inc(sem)
        tk.stream["pool"].append(("i", ("cc", id(sem)), 1))
        return sem

    def proj_headnorm(wt, wtok, gcol, base):
        its = [(hd, tt) for hd in range(4) for tt in range(NT)]

        def front(i):
            hd, tt = its[i]
            pa = i % 3
            mmg(ps[pa][:, :], [(wt[:, kc, hd * 128:(hd + 1) * 128], hslice(kc, tt)) for kc in range(8)],
                r=[HB(kc, tt) for kc in range(8)] + [wtok], w=[PS[pa]])
            op("act", act_fn(sq[i % 2], ps[pa][:, :], AF.Square), r=[PS[pa]], w=[SQ[i % 2]])

        def back(i):
            hd, tt = its[i]
            pa, pn, ti = i % 3, 3 + i % 2, i % 2
            mmg(ps[pn][:, :], [(ONE_128, sq[i % 2])], r=[SQ[i % 2], ("ones",)], w=[PS[pn]])
            emit_rstd(ps[pn][:, :], tf[:, ti, 0:512], r=[PS[pn]], w=[TF[ti]])
            op("dve", stt(qk[:, base + hd, tile_sl(tt)], ps[pa][:, :], gcol, tf[:, ti, 0:512],
                          ALU.mult, ALU.mult), r=[PS[pa], TF[ti], ("par",)], w=[QK(base + hd, tt)])

        for i in range(len(its) + 1):
            if i < len(its):
                front(i)
            if i >= 1:
                back(i - 1)

    def even_mixer(l, li, ei):
        e = l // 2
        halo_prep(f"mixg{l}", True)
        norm_to_hbuf(f"mixg{l}")
        sp8 = sm[:, 4:8]
        op("act", act_fn(sp8, pcol(f"lam{l}", 0, 4), AF.Exp, scale=-1.0), r=[("par",)], w=[("sp8",)])
        op("act", act_fn(sp8, sp8, AF.Ln, bias=1.0), r=[("sp8",)], w=[("sp8",)])
        op("dve", ts_(sp8, sp8, -8.0, None, ALU.mult), r=[("sp8",)], w=[("sp8",)])

        xrh = sm[:, 40:56]
        hfin = sm[:, 8:12]
        hc = sm[:, 12:13]
        pc = sm[:, 13:14]
        hA = sm[:, 32:36]
        xrt = [tf[:, 0, 0:515], tf[:, 1, 0:515]]
        gxb = [tf[:, 2, 0:512], tf[:, 7, 0:512]]
        GXT = [TF[2], TF[7]]
        rits = [(c, tt) for c in range(4) for tt in range(NT)]
        rgs = {}

        def rg_views(c):
            srg = rgs[c]
            return (srg, wview(srg, 8, 128, 0), wview(srg, 8, 128, 1024),
                    wsl[:, srg, 2048:2176], wsl[:, srg, 2176:2304])

        def rfront(i):
            c, tt = rits[i]
            p = i % 2
            if tt == 0:
                rgs[c], _ = ws.next(f"rg{l}_{c}")
            srg, wxr, wgt, wga, wgx = rg_views(c)
            if tt == 0:
                mmg(ps[6][:, 0:16], [(wxr[:, kc, :], hbuf[:, kc, 0:HALO]) for kc in range(8)],
                    r=[("hbh",), ("w", srg)], w=[PS[6]])
                op("act", act_fn(xrh, ps[6][:, 0:16], AF.Copy), r=[PS[6]], w=[("xrh",)])
            mmg(ps[p][:, :], [(wxr[:, kc, :], hslice(kc, tt)) for kc in range(8)],
                r=[HB(kc, tt) for kc in range(8)] + [("w", srg)], w=[PS[p]])
            mmg(ps[2 + p][:, :], [(wgt[:, kc, :], hslice(kc, tt)) for kc in range(8)],
                r=[HB(kc, tt) for kc in range(8)] + [("w", srg)], w=[PS[2 + p]])
            if tt == 0:
                op("dve", lambda e_: e_.tensor_copy(xrt[p][:, 0:3], xrh[:, 13:16]), r=[("xrh",)], w=[TF[p]])
            else:
                op("dve", lambda e_: e_.tensor_copy(xrt[p][:, 0:3], xrt[1 - p][:, 512:515]), r=[TF[1 - p]], w=[TF[p]])
            op("act", act_fn(xrt[p][:, 3:515], ps[p][:, :], AF.Copy), r=[PS[p]], w=[TF[p]])
            op("act", act_fn(gxb[p], ps[2 + p][:, :], AF.Copy), r=[PS[2 + p]], w=[GXT[p]])

        def rback(i):
            c, tt = rits[i]
            p = i % 2
            srg, wxr, wgt, wga, wgx = rg_views(c)
            xt, gx = xrt[p], gxb[p]
            g3_ = tf[:, 3, 0:512]
            xc = tf[:, 6, 0:512]
            ra = tf[:, 4, 0:512]
            ri = tf[:, 5, 0:512]
            cw = lambda j: pcol(f"convw{l}", j * 4 + c)
            if tt == 0:
                op("dve", lambda e_: e_.memset(hc, 0.0), w=[("hc",)])
                op("dve", lambda e_: e_.memset(pc, 1.0), w=[("pc",)])
            op("act", act_fn(g3_, gx, AF.Square), r=[GXT[p]], w=[TF[3]])
            op("act", act_fn(xc, xt[:, 3:515], AF.Identity, scale=cw(3), bias=pcol(f"convb{l}", c)),
               r=[TF[p], ("par",)], w=[TF[6]])
            op("dve", ts_(g3_, g3_, 0.044715, 1.0, ALU.mult, ALU.add), r=[TF[3]], w=[TF[3]])
            op("dve", stt(xc, xt[:, 0:512], cw(0), xc, ALU.mult, ALU.add), r=[TF[p], TF[6], ("par",)], w=[TF[6]])
            op("dve", tt_(g3_, g3_, gx, ALU.mult), r=[GXT[p], TF[3]], w=[TF[3]])
            op("dve", stt(xc, xt[:, 1:513], cw(1), xc, ALU.mult, ALU.add), r=[TF[p], TF[6], ("par",)], w=[TF[6]])
            op("act", act_fn(g3_, g3_, AF.Sigmoid, scale=1.5957691216057308), r=[TF[3]], w=[TF[3]])
            op("dve", stt(xc, xt[:, 2:514], cw(2), xc, ALU.mult, ALU.add), r=[TF[p], TF[6], ("par",)], w=[TF[6]])
            op("act", act_fn(sq[0], xc, AF.Copy), r=[TF[6]], w=[SQ[0]])
            op("dve", tt_(gx, gx, g3_, ALU.mult), r=[GXT[p], TF[3]], w=[GXT[p]])
            mmg(ps[4][:, :], [(wga, sq[0])], r=[SQ[0], ("w", srg)], w=[PS[4]])
            mmg(ps[5][:, :], [(wgx, sq[0])], r=[SQ[0], ("w", srg)], w=[PS[5]])
            op("act", act_fn(ra, ps[4][:, :], AF.Sigmoid, bias=pcol(f"gab{l}", c)), r=[PS[4], ("par",)], w=[TF[4]])
            op("act", act_fn(ri, ps[5][:, :], AF.Sigmoid, bias=pcol(f"gxb{l}", c)), r=[PS[5], ("par",)], w=[TF[5]])
            op("act", act_fn(ra, ra, AF.Exp, scale=sp8[:, c:c + 1]), r=[TF[4], ("sp8",)], w=[TF[4]])
            op("dve", tt_(ri, ri, xc, ALU.mult), r=[TF[5], TF[6]], w=[TF[5]])
            op("dve", tt_(g3_, ra, ra, ALU.mult), r=[TF[4]], w=[TF[3]])
            op("act", act_fn(g3_, g3_, AF.Sqrt, scale=-1.0, bias=1.0), r=[TF[3]], w=[TF[3]])
            pp = xc
            op("dve", lambda e_: e_.tensor_tensor_scan(pp, ra, ZT, pc, ALU.mult, ALU.add),
               r=[TF[4], TF[5], ("pc",), ("sm0",)], w=[TF[6]])
            op("dve", tt_(ri, ri, g3_, ALU.mult), r=[TF[5], TF[3]], w=[TF[5]])
            op("dve", lambda e_: e_.tensor_copy(pc, pp[:, 511:512]), r=[TF[6]], w=[("pc",)])
            hh = g3_
            op("dve", lambda e_: e_.tensor_tensor_scan(hh, ra, ri, hc, ALU.mult, ALU.add),
               r=[TF[4], TF[5], ("hc",)], w=[TF[3]])
            op("dve", tt_(qk[:, c, tile_sl(tt)], pp, gx, ALU.mult), r=[TF[6], GXT[p]], w=[QK(c, tt)])
            op("dve", lambda e_: e_.tensor_copy(hc, hh[:, 511:512]), r=[TF[3]], w=[("hc",)])
            op("dve", tt_(qk[:, 4 + c, tile_sl(tt)], hh, gx, ALU.mult), r=[TF[3], GXT[p]], w=[QK(4 + c, tt)])
            if tt == NT - 1:
                op("dve", lambda e_: e_.tensor_copy(hfin[:, c:c + 1], hc), r=[("hc",)], w=[("hfin",)])
                ws.release(srg)

        for i in range(len(rits) + 1):
            if i < len(rits):
                rfront(i)
            if i >= 1:
                rback(i - 1)
        h3 = dma("pool", f"b3_{ei}", b3[ei].ap(), hfin, r=[("hfin",)])
        cc3 = collective(b3[ei], g3[ei], [h3])

        sv, _ = ws.next(f"v{l}")
        wv = wview(sv, 8, 512)
        for blk in range(16):
            pa = 4 + blk % 2
            tt = blk // 4
            mmg(ps[pa][:, :],
                [(hbuf[:, kc, HALO + blk * 128:HALO + (blk + 1) * 128], wv[:, kc, :]) for kc in range(8)],
                r=[HB(kc, tt) for kc in range(8)] + [("w", sv)], w=[PS[pa]])
            op("act", act_fn(vv[:, blk, :], ps[pa][:, :], AF.Copy), r=[PS[pa]], w=[("vv", blk)])
        ws.release(sv)

        tk.eng["pool"].wait_ge(cc3, 1)
        tk.stream["pool"].append(("w", ("cc", id(cc3)), 1))
        dma("pool", f"g3_{ei}", hA, g3[ei].ap()[0:128, :], w=[("hA",)])
        op("dve", ts_(hA, hA, pcol("flag"), None, ALU.mult), r=[("hA",), ("par",)], w=[("hA",)])
        for c in range(4):
            for tt in range(NT):
                op("dve", stt(qk[:, 4 + c, tile_sl(tt)], qk[:, c, tile_sl(tt)], hA[:, c:c + 1],
                              qk[:, 4 + c, tile_sl(tt)], ALU.mult, ALU.add),
                   r=[QK(c, tt), QK(4 + c, tt), ("hA",)], w=[QK(4 + c, tt)])
        swy, _ = ws.next(f"woy{l}")
        woy = wview(swy, 4, 1024)
        for m in range(8):
            for tt in range(NT):
                pi = 6 + (m * NT + tt) % 2
                mmg(ps[pi][:, :], [(woy[:, c, m * 128:(m + 1) * 128], qk[:, 4 + c, tile_sl(tt)]) for c in range(4)],
                    r=[QK(4 + c, tt) for c in range(4)] + [("w", swy)], w=[PS[pi]])
                add_to_x(m, tt, ps[pi][:, :], pi)
        ws.release(swy)

        sk, _ = ws.next(f"k{l}")
        wk = wview(sk, 8, 512)
        proj_headnorm(wk, ("w", sk), pcol(f"kg{l}"), 4)
        ws.release(sk)
        cc2 = []
        for hd in range(4):
            h2a = dma("pool", f"b2k_{ei}_{hd}", b2[ei][hd].ap()[0:128, :], qk[:, 4 + hd, :],
                      r=[QK(4 + hd, tt) for tt in range(4)])
            h2b = dma("pool", f"b2v_{ei}_{hd}",
                      b2[ei][hd].ap()[128:256, :].rearrange("p (b f) -> p b f", b=16),
                      vv[:, :, hd * 128:(hd + 1) * 128], r=[("vv", blk) for blk in range(16)])
            cc2.append(collective(b2[ei][hd], g2[ei][hd], [h2a, h2b]))

        sq_, _ = ws.next(f"q{l}")
        wq = wview(sq_, 8, 512)
        proj_headnorm(wq, ("w", sq_), pcol(f"qg{l}"), 0)
        ws.release(sq_)
        tk.barrier()
        for f in ("pe", "act", "dve"):
            tk._wait("pool", ("eng", f, tk.cnt[f]))
        for hd in range(4):
            tk.eng["pool"].wait_ge(cc2[hd], 1)
            tk.stream["pool"].append(("w", ("cc", id(cc2[hd])), 1))
            dma("pool", f"ctxk_{ei}_{hd}", kctx[:, hd, :], g2[ei][hd].ap()[0:128, :],
                w=[("kctx",)] + ALLHB)
            dma("pool", f"ctxv_{ei}_{hd}", vctx[:, :, hd * 128:(hd + 1) * 128],
                g2[ei][hd].ap()[128:256, :].rearrange("p (b f) -> p b f", b=16),
                w=[("vctx",)] + ALLHB)

        scale = 128.0 ** -0.5
        ebs = [bt[:, 2, :], bt[:, 3, :], bt[:, 0, :], bt[:, 7, :]]
        pbs = [bt[:, 4, :], bt[:, 5, :], bt[:, 1, :], bt[:, 8, :]]
        EBS = [("eb", 0), ("eb", 1), ("sq", 0), ("bt7",)]
        PBS = [("pb", 0), ("pb", 1), ("sq", 1), ("bt8",)]
        SBK = [0, 1, 2, 7]
        items = []
        gi = 0
        for hd in range(4):
            for qt in range(NT):
                blocks = [("ctx", kb) for kb in range(4 * qt, 16)] + [("own", kb) for kb in range(0, 4 * qt + 4)]
                for bi, (kind, kb) in enumerate(blocks):
                    items.append((hd, qt, gi, bi, len(blocks), kind, kb))
                gi += 1
        LA = 3

        def att_front(idx):
            hd, qt, g, bi, nb, kind, kb = items[idx]
            si, bi3 = SBK[idx % 4], idx % 4
            if kind == "ctx":
                kT = kctx[:, hd, kb * 128:(kb + 1) * 128]
                rk = [("kctx",)]
                d0 = 512 * qt + 2048 - 128 * kb
                bias = pcol("ctxbias")
            else:
                kT = qk[:, 4 + hd, kb * 128:(kb + 1) * 128]
                rk = [QK(4 + hd, kb // 4)]
                d0 = 512 * qt - 128 * kb
                bias = 0.0
            off = d0 + 384
            mmg(ps[si][:, :], [(kT, qk[:, hd, tile_sl(qt)])], r=rk + [QK(hd, qt)], w=[PS[si]])
            op("act", act_fn(ebs[bi3], ps[si][:, :], AF.Exp, scale=scale, bias=bias),
               r=[PS[si], ("par",)], w=[EBS[bi3]])
            op("dve", tt_(pbs[bi3], ebs[bi3], maskt[:, off:off + 512], ALU.mult),
               r=[EBS[bi3], ("mask",)], w=[PBS[bi3]])

        def att_back(idx):
            hd, qt, g, bi, nb, kind, kb = items[idx]
            bi3 = idx % 4
            po, pd = 3 + g % 2, 5 + g % 2
            if kind == "ctx":
                vs = vctx[:, kb, hd * 128:(hd + 1) * 128]
                rv = [("vctx",)]
            else:
                vs = vv[:, kb, hd * 128:(hd + 1) * 128]
                rv = [("vv", kb)]
            tk.mm1(ps[po][:, :], vs, pbs[bi3], bi == 0, bi == nb - 1, r=rv + [PBS[bi3]], w=[PS[po]])
            tk.mm1(ps[pd][:, :], ONE_1, pbs[bi3], bi == 0, bi == nb - 1, r=[("ones",), PBS[bi3]], w=[PS[pd]])
            if bi == nb - 1:
                rdi = 6 + g % 2
                rd = tf[:, rdi, 0:512]
                emit_recip(ps[pd][:, :], rd, r=[PS[pd]], w=[TF[rdi]])
                op("dve", tt_(qk[:, hd, tile_sl(qt)], ps[po][:, :], rd, ALU.mult),
                   r=[PS[po], TF[rdi]], w=[QK(hd, qt)])

        for idx in range(len(items) + LA):
            if idx < len(items):
                att_front(idx)
            if idx - LA >= 0:
                att_back(idx - LA)
        swa, _ = ws.next(f"woa{l}")
        woa = wview(swa, 4, 1024)
        for tt in range(NT):
            for m in range(8):
                pi = (m * NT + tt) % 2
                mmg(ps[pi][:, :], [(woa[:, c, m * 128:(m + 1) * 128], qk[:, c, tile_sl(tt)]) for c in range(4)],
                    r=[QK(c, tt) for c in range(4)] + [("w", swa)], w=[PS[pi]])
                add_to_x(m, tt, ps[pi][:, :], pi)
        ws.release(swa)
        tk.barrier()

    def odd_mixer(l, li):
        halo_prep(f"mixg{l}", False)
        spw, _ = ws.next(f"pool{l}")
        pw = wsl[:, spw, 0:2048].rearrange("p (i g n) -> p i g n", i=2, g=4)
        dt_ = vv[:, 0:8, :]
        for tt in range(NT):
            norm_tile_rstd(tt, 7)
            for g in range(4):
                w = 2 ** (g + 1)
                cs = (2 * g, 2 * g + 1)
                hfs = [tf[:, 0, 0:528], tf[:, 3, 0:528]]
                HFT = [TF[0], TF[3]]
                sbufs = [[(tf[:, 1, 0:528], TF[1]), (tf[:, 2, 0:528], TF[2])],
                         [(tf[:, 4, 0:528], TF[4]), (tf[:, 5, 0:528], TF[5])]]
                for q_, c in enumerate(cs):
                    op("dve", lambda e_, c=c, hf=hfs[q_]: e_.tensor_copy(hf[:, 0:16], hprev[:, c, :]),
                       r=[("hprev", c)], w=[HFT[q_]])
                for q_, c in enumerate(cs):
                    op("dve", stt(hfs[q_][:, 16:528], xres[:, c, tile_sl(tt)], pcol(f"mixg{l}", c), tf[:, 7, 0:512],
                                  ALU.mult, ALU.mult), r=[X(c, tt), TF[7], ("par",)], w=[HFT[q_]])
                for q_, c in enumerate(cs):
                    op("dve", lambda e_, c=c, hf=hfs[q_]: e_.tensor_copy(hprev[:, c, :], hf[:, 512:528]),
                       r=[HFT[q_]], w=[("hprev", c)])
                srcs = [(hfs[0], HFT[0]), (hfs[1], HFT[1])]
                for k in range(g + 1):
                    sh = 2 ** k
                    lo = 2 ** (k + 1) - 1
                    for q_ in range(2):
                        src, srct = srcs[q_]
                        dst, dstt = sbufs[q_][k % 2]
                        op("dve", tt_(dst[:, lo:528], src[:, lo:528], src[:, lo - sh:528 - sh], ALU.add),
                           r=[srct], w=[dstt])
                        srcs[q_] = (dst, dstt)
                for q_, c in enumerate(cs):
                    src, srct = srcs[q_]
                    op("dve", stt(dt_[:, c, :], src[:, 16:528], 1.0 / w, hfs[q_][:, 16:528], ALU.mult, ALU.subtract),
                       r=[srct, HFT[q_]], w=[("dt", c)])
                if tt == 0:
                    o_, _w = POFF["invcnt"]
                    for q_, c in enumerate(cs):
                        src, srct = srcs[q_]
                        t16 = sm[:, 40:56] if q_ == 0 else sm[:, 64:80]
                        op("dve", tt_(t16, src[:, 16:32], par[:, o_ + c * 16:o_ + (c + 1) * 16], ALU.mult),
                           r=[srct, ("par",)], w=[("t16", q_)])
                    for q_, c in enumerate(cs):
                        t16 = sm[:, 40:56] if q_ == 0 else sm[:, 64:80]
                        op("dve", tt_(dt_[:, c, 0:16], t16, hfs[q_][:, 16:32], ALU.subtract),
                           r=[("t16", q_), HFT[q_]], w=[("dt", c)])
            for j in range(8):
                g, jj = j // 2, j % 2
                pi = j % 2
                mmg(ps[pi][:, :], [(pw[:, i, g, jj * 128:(jj + 1) * 128], dt_[:, 2 * g + i, :]) for i in range(2)],
                    r=[("dt", 2 * g), ("dt", 2 * g + 1), ("w", spw)], w=[PS[pi]])
                op("dve", stt(xres[:, j, tile_sl(tt)], ps[pi][:, :], pcol(f"scale{l}", j),
                              xres[:, j, tile_sl(tt)], ALU.mult, ALU.add),
                   r=[PS[pi], ("par",)], w=[X(j, tt)])
        ws.release(spw)
        tk.barrier()

    def xattn(l):
        norm_to_hbuf(f"xag{l}")
        kx = vv[:, 0:4, :].rearrange("p a (b m) -> p (a b) m", b=2)
        vx = vv[:, 4:8, :].rearrange("p (b j) f -> p b (j f)", b=2)
        for j in range(2):
            sw, _ = ws.next(f"xk{l}_{j}")
            wk = wview(sw, 8, 512)
            for hh in range(2):
                h = 2 * j + hh
                for i in range(2):
                    mmg(ps[i][:, 0:256],
                        [(wk[:, kc, (2 * hh + i) * 128:(2 * hh + i + 1) * 128], memn[:, kc, :]) for kc in range(8)],
                        r=[("memn",), ("w", sw)], w=[PS[i]])
                    op("act", act_fn(sq[i][:, 0:256], ps[i][:, 0:256], AF.Square), r=[PS[i]], w=[SQ[i]])
                mmg(ps[2][:, 0:256], [(ONE_256, sq[0][:, 0:256]), (ONE_256, sq[1][:, 0:256])],
                    r=[SQ[0], SQ[1], ("ones",)], w=[PS[2]])
                emit_rstd(ps[2][:, 0:256], tf[:, 0, 0:256], r=[PS[2]], w=[TF[0]])
                for i in range(2):
                    op("dve", stt(kx[:, 2 * h + i, :], ps[i][:, 0:256], pcol(f"xkg{l}", i), tf[:, 0, 0:256],
                                  ALU.mult, ALU.mult), r=[PS[i], TF[0], ("par",)], w=[("kx",)])
            ws.release(sw)
        for j in range(2):
            sw, _ = ws.next(f"xv{l}_{j}")
            wv = wview(sw, 8, 512)
            for blk in range(2):
                pi = 3 + blk
                mmg(ps[pi][:, :], [(memn[:, kc, blk * 128:(blk + 1) * 128], wv[:, kc, :]) for kc in range(8)],
                    r=[("memn",), ("w", sw)], w=[PS[pi]])
                op("act", act_fn(vx[:, blk, j * 512:(j + 1) * 512], ps[pi][:, :], AF.Copy),
                   r=[PS[pi]], w=[("vx",)])
            ws.release(sw)
        sqx = [[bt[:, 0, :], bt[:, 1, :]], [bt[:, 2, :], bt[:, 3, :]]]
        SQX = [[("sq", 0), ("sq", 1)], [("eb", 0), ("eb", 1)]]
        for j in range(2):
            sw, _ = ws.next(f"xq{l}_{j}")
            wq = wview(sw, 8, 512)
            its = [(hh, tt) for hh in range(2) for tt in range(NT)]

            def qfront(n, its=its, wq=wq, sw=sw):
                hh, tt = its[n]
                p = n % 2
                for i in range(2):
                    bk = 2 * p + i
                    mmg(ps[bk][:, :],
                        [(wq[:, kc, (2 * hh + i) * 128:(2 * hh + i + 1) * 128], hslice(kc, tt)) for kc in range(8)],
                        r=[HB(kc, tt) for kc in range(8)] + [("w", sw)], w=[PS[bk]])
                    op("act", act_fn(sqx[p][i], ps[bk][:, :], AF.Square), r=[PS[bk]], w=[SQX[p][i]])

            def qback(n, its=its, j=j):
                hh, tt = its[n]
                h = 2 * j + hh
                p = n % 2
                pn = 4 + p
                mmg(ps[pn][:, :], [(ONE_256, sqx[p][0]), (ONE_256, sqx[p][1])],
                    r=[SQX[p][0], SQX[p][1], ("ones",)], w=[PS[pn]])
                emit_rstd(ps[pn][:, :], tf[:, p, 0:512], r=[PS[pn]], w=[TF[p]])
                for i in range(2):
                    op("dve", stt(qk[:, 2 * h + i, tile_sl(tt)], ps[2 * p + i][:, :], pcol(f"xqg{l}", i),
                                  tf[:, p, 0:512], ALU.mult, ALU.mult),
                       r=[PS[2 * p + i], TF[p], ("par",)], w=[QK(2 * h + i, tt)])

            for n in range(len(its) + 1):
                if n < len(its):
                    qfront(n)
                if n >= 1:
                    qback(n - 1)
            ws.release(sw)
        scale = 256.0 ** -0.5
        ebx = [[bt[:, 2, :], bt[:, 3, :]], [bt[:, 4, :], bt[:, 5, :]]]
        EBX = [[("eb", 0), ("eb", 1)], [("pb", 0), ("pb", 1)]]
        aits = [(h, tt) for h in range(4) for tt in range(NT)]

        def afront(n):
            h, tt = aits[n]
            p = n % 2
            for blk in range(2):
                bk = 2 * p + blk
                mmg(ps[bk][:, :],
                    [(kx[:, 2 * h + i, blk * 128:(blk + 1) * 128], qk[:, 2 * h + i, tile_sl(tt)]) for i in range(2)],
                    r=[("kx",), QK(2 * h, tt), QK(2 * h + 1, tt)], w=[PS[bk]])
                op("act", act_fn(ebx[p][blk], ps[bk][:, :], AF.Exp, scale=scale), r=[PS[bk]], w=[EBX[p][blk]])

        def aback(n):
            h, tt = aits[n]
            p = n % 2
            pd = 6 + p
            for i in range(2):
                mmg(ps[4 + i][:, :],
                    [(vx[:, blk, h * 256 + i * 128:h * 256 + (i + 1) * 128], ebx[p][blk]) for blk in range(2)],
                    r=[("vx",), EBX[p][0], EBX[p][1]], w=[PS[4 + i]])
            mmg(ps[pd][:, :], [(ONE_1, ebx[p][0]), (ONE_1, ebx[p][1])],
                r=[EBX[p][0], EBX[p][1], ("ones",)], w=[PS[pd]])
            rd = tf[:, 2 + p, 0:512]
            emit_recip(ps[pd][:, :], rd, r=[PS[pd]], w=[TF[2 + p]])
            for i in range(2):
                op("dve", tt_(qk[:, 2 * h + i, tile_sl(tt)], ps[4 + i][:, :], rd, ALU.mult),
                   r=[PS[4 + i], TF[2 + p]], w=[QK(2 * h + i, tt)])

        for n in range(len(aits) + 1):
            if n < len(aits):
                afront(n)
            if n >= 1:
                aback(n - 1)
        s0, _ = ws.next(f"xo{l}_0")
        s1, _ = ws.next(f"xo{l}_1")
        wo = [wview(s0, 4, 1024), wview(s1, 4, 1024)]
        for tt in range(NT):
            for m in range(8):
                pi = 6 + (m * NT + tt) % 2
                mmg(ps[pi][:, :],
                    [(wo[c // 4][:, c % 4, m * 128:(m + 1) * 128], qk[:, c, tile_sl(tt)]) for c in range(8)],
                    r=[QK(c, tt) for c in range(8)] + [("w", s0), ("w", s1)], w=[PS[pi]])
                add_to_x(m, tt, ps[pi][:, :], pi)
        ws.release(s0)
        ws.release(s1)

    def mlp(l):
        norm_to_hbuf(f"mlpg{l}")
        for g in range(8):
            s1, _ = ws.next(f"w1_{l}_{g}")
            s2, _ = ws.next(f"w2_{l}_{g}")
            w1 = wview(s1, 8, 512)
            w2 = wview(s2, 4, 1024)
            k = 0
            for j in range(4):
                for tt in range(NT):
                    pi = k % 4
                    ei_ = k % 2
                    k += 1
                    mmg(ps[pi][:, :], [(w1[:, kc, j * 128:(j + 1) * 128], hslice(kc, tt)) for kc in range(8)],
                        r=[HB(kc, tt) for kc in range(8)] + [("w", s1)], w=[PS[pi]])
                    op("act", act_fn(eb[ei_], ps[pi][:, :], AF.Relu), r=[PS[pi]], w=[EB[ei_]])
                    op("dve", tt_(qk[:, j, tile_sl(tt)], eb[ei_], eb[ei_], ALU.mult), r=[EB[ei_]], w=[QK(j, tt)])
            ws.release(s1)
            k = 0
            for tt in range(NT):
                for m in range(8):
                    pi = 4 + k % 4
                    k += 1
                    mmg(ps[pi][:, :], [(w2[:, j, m * 128:(m + 1) * 128], qk[:, j, tile_sl(tt)]) for j in range(4)],
                        r=[QK(j, tt) for j in range(4)] + [("w", s2)], w=[PS[pi]])
                    add_to_x(m, tt, ps[pi][:, :], pi)
            ws.release(s2)

    ei = 0
    for li, l in enumerate(layers):
        if li > 0:
            tk.eng["pool"].wait_ge(cc0, 1)
            tk.stream["pool"].append(("w", ("cc", id(cc0)), 1))
            dma("pool", f"g0_{li}", xhalo[:, :, :],
                g0[li].ap()[0:128, :].rearrange("p (c t) -> p c t", c=8), w=[("xhalo",)])
        if l % 2 == 0:
            even_mixer(l, li, ei)
            ei += 1
        else:
            odd_mixer(l, li)
        xattn(l)
        mlp(l)
        if li + 1 < len(layers):
            h0 = dma("pool", f"b0_{li + 1}", b0[li + 1].ap().rearrange("p (c t) -> p c t", c=8),
                     xres[:, :, T - HALO:T], r=[X(c, 3) for c in range(8)])
            cc0 = collective(b0[li + 1], g0[li + 1], [h0])

    for tt in range(NT):
        dma("sp", f"yout{tt}", y_d[:, :, tile_sl(tt)], xres[:, :, tile_sl(tt)],
            r=[X(c, tt) for c in range(8)])
    tk.wait_all("sp")
    assert ws.used == len(ws.blocks), (ws.used, len(ws.blocks))
    tk.check_deadlock()
    stack.close()
    return nc


WEIGHT_KEYS = ["ev_w_in", "ev_gate_a_w", "ev_gate_x_w", "ev_w_out", "od_pool_w",
               "xa_w_q", "xa_w_kv", "xa_w_o", "mlp_w1", "mlp_w2"]

_NC_CACHE = {}


def to_fm(a):
    t = a.shape[0]
    return np.ascontiguousarray(a.reshape(t, 8, 128).transpose(2, 1, 0))


def from_fm(a):
    t = a.shape[2]
    return np.ascontiguousarray(a.transpose(2, 1, 0).reshape(t, 1024))


def run_layers(inp, x_full, layers):
    key = tuple(layers)
    if key not in _NC_CACHE:
        _NC_CACHE[key] = build(list(layers))
    nc = _NC_CACHE[key]
    mask = make_mask()
    in_maps = []
    for core in range(8):
        b, half = core // 2, core % 2
        base = half * T
        m = {k: np.ascontiguousarray(np.asarray(inp[k], np.float32)) for k in WEIGHT_KEYS}
        m["x"] = to_fm(x_full[b, base:base + T])
        if half:
            m["xh"] = to_fm(x_full[b, base - HALO:base])
        else:
            m["xh"] = np.zeros((128, 8, HALO), np.float32)
        m["mem"] = to_fm(np.asarray(inp["mem"], np.float32)[b])
        m["params"] = pack_params(inp, half)
        m["mask"] = mask
        in_maps.append(m)
    res = run_bass_kernel_spmd(nc, in_maps, core_ids=list(range(8)))
    out = np.zeros_like(x_full)
    for core in range(8):
        b, half = core // 2, core % 2
        out[b, half * T:(half + 1) * T] = from_fm(np.asarray(res.results[core]["y"]))
    return out


FUSED = True


def kernel(**inp):
    x = np.ascontiguousarray(np.asarray(inp["x"], np.float32))
    if FUSED:
        return run_layers(inp, x, [0, 1, 2, 3])
    for l in range(4):
        x = run_layers(inp, x, [l])
    return x
```

```python
from contextlib import ExitStack
import numpy as np
import concourse.bass as bass
import concourse.mybir as mybir
from concourse.bass_utils import run_bass_kernel_spmd

F32 = mybir.dt.float32
BF16 = mybir.dt.bfloat16
AF = mybir.ActivationFunctionType
ALU = mybir.AluOpType

T = 2048
NT = 4
TT = 512
HALO = 16
HW = HALO + T
KC = 8
EPS = 1e-6
MASKW = 384 + 2048 + 512
NEG = -30000.0
GROUPS = [[0, 1], [2, 3], [4, 5], [6, 7]]
SAME_ENGINE_SYNC = True


def tile_sl(tt):
    return slice(tt * TT, (tt + 1) * TT)


def param_layout():
    off = {}
    n = 0

    def add(name, w):
        nonlocal n
        off[name] = (n, w)
        n += w

    for l in range(4):
        add(f"mixg{l}", 8)
        add(f"xag{l}", 8)
        add(f"mlpg{l}", 8)
        add(f"xqg{l}", 2)
        add(f"xkg{l}", 2)
        if l % 2 == 0:
            add(f"convw{l}", 16)
            add(f"convb{l}", 4)
            add(f"gab{l}", 4)
            add(f"gxb{l}", 4)
            add(f"lam{l}", 4)
            add(f"qg{l}", 1)
            add(f"kg{l}", 1)
        else:
            add(f"scale{l}", 8)
    add("memg", 8)
    add("flag", 1)
    add("ctxbias", 1)
    add("invcnt", 8 * 16)
    return off, n


POFF, NP = param_layout()


def pack_params(inp, half):
    P = np.zeros((128, NP), np.float32)

    def put(name, arr):
        o, w = POFF[name]
        arr = np.asarray(arr, np.float32)
        assert arr.shape == (128, w), (name, arr.shape, w)
        P[:, o:o + w] = arr

    def cols(v):
        v = np.asarray(v, np.float32)
        return v.reshape(-1, 128).T

    for l in range(4):
        put(f"mixg{l}", cols(inp["mix_norm_g"][l]))
        put(f"xag{l}", cols(inp["xattn_norm_g"][l]))
        put(f"mlpg{l}", cols(inp["mlp_norm_g"][l]))
        put(f"xqg{l}", cols(inp["xa_q_norm_g"][l]))
        put(f"xkg{l}", cols(inp["xa_k_norm_g"][l]))
        if l % 2 == 0:
            e = l // 2
            cw = np.asarray(inp["ev_conv_w"][e], np.float32)
            put(f"convw{l}", np.concatenate([cols(cw[j]) for j in range(4)], axis=1))
            put(f"convb{l}", cols(inp["ev_conv_b"][e]))
            put(f"gab{l}", cols(inp["ev_gate_a_b"][e]))
            put(f"gxb{l}", cols(inp["ev_gate_x_b"][e]))
            put(f"lam{l}", cols(inp["ev_lambda"][e]))
            put(f"qg{l}", cols(inp["ev_q_norm_g"][e]))
            put(f"kg{l}", cols(inp["ev_k_norm_g"][e]))
        else:
            put(f"scale{l}", cols(inp["od_scale"][l // 2]))
    put("memg", cols(inp["mem_norm_g"]))
    put("flag", np.full((128, 1), float(half), np.float32))
    put("ctxbias", np.full((128, 1), 0.0 if half else NEG, np.float32))
    ic = np.zeros((8, 16), np.float32)
    for c in range(8):
        w = 2 ** (c // 2 + 1)
        for t in range(16):
            ic[c, t] = 1.0 / w if half else 1.0 / min(t + 1, w)
    put("invcnt", np.broadcast_to(ic.reshape(1, 128), (128, 128)))
    return P


def make_mask():
    ki = np.arange(128)[:, None]
    x = np.arange(MASKW)[None, :]
    d = x - ki - 384
    m = ((d >= 0) & (d <= 128)).astype(np.float32)
    m += ((d >= 0) & (d % 4 == 0) & (d <= 512)).astype(np.float32)
    m += ((d >= 0) & (d % 16 == 0) & (d <= 2048)).astype(np.float32)
    return m


class Trk:
    CH = 4000

    def __init__(self, nc, stack):
        self.nc = nc
        self.stack = stack
        self.eng = dict(pe=nc.tensor, act=nc.scalar, dve=nc.vector, pool=nc.gpsimd, sp=nc.sync)
        self.cnt = {e: 0 for e in self.eng}
        self.sems = {e: [] for e in self.eng}
        self.seen = {e: {f: 0 for f in self.eng} for e in self.eng}
        self.snap = {e: {} for e in self.eng}
        self.dsem = {}
        self.dcnt = {}
        self.dseen = {e: {} for e in self.eng}
        self.lastw = {}
        self.readers = {}
        self.dma_tokens = set()
        self.nops = 0
        self.stream = {e: [] for e in self.eng}

    def _sem(self, e, seq):
        k = (seq - 1) // self.CH
        while len(self.sems[e]) <= k:
            self.sems[e].append(
                self.stack.enter_context(self.nc.semaphore(f"s_{e}_{len(self.sems[e])}")))
        return self.sems[e][k], (seq - 1) % self.CH + 1

    def _wait(self, e, h):
        if h[0] == "eng":
            _, f, s = h
            if f == e and (e == "pe" or not SAME_ENGINE_SYNC):
                return
            if self.seen[e][f] >= s:
                return
            sem, v = self._sem(f, s)
            self.eng[e].wait_ge(sem, v)
            self.stream[e].append(("w", ("eng", f, (s - 1) // self.CH), v))
            self.seen[e][f] = s
            sn = self.snap[f].get(s)
            if sn and f != e:
                for g, v2 in sn.items():
                    if g != e and v2 > self.seen[e][g]:
                        self.seen[e][g] = v2
        else:
            _, key, v = h
            if self.dseen[e].get(key, 0) >= v:
                return
            self.eng[e].wait_ge(self.dsem[key], v)
            self.stream[e].append(("w", ("dma", key), v))
            self.dseen[e][key] = v

    def _deps(self, e, r, w):
        hs = []
        for t in r:
            h = self.lastw.get(t)
            if h:
                hs.append(h)
        for t in w:
            h = self.lastw.get(t)
            if h:
                hs.append(h)
            hs.extend(self.readers.get(t, ()))
        best = {}
        for h in hs:
            k = (h[0], h[1])
            if k not in best or h[2] > best[k][2]:
                best[k] = h
        for h in best.values():
            self._wait(e, h)

    def _commit(self, h, r, w):
        for t in r:
            self.readers.setdefault(t, []).append(h)
        for t in w:
            self.lastw[t] = h
            self.readers[t] = []

    def op(self, e, fn, r=(), w=()):
        self._deps(e, r, w)
        ins = fn(self.eng[e])
        self.cnt[e] += 1
        s = self.cnt[e]
        sem, v = self._sem(e, s)
        ins.then_inc(sem, 1)
        self.stream[e].append(("i", ("eng", e, (s - 1) // self.CH), 1))
        self.snap[e][s] = dict(self.seen[e])
        self._commit(("eng", e, s), r, w)
        self.nops += 1
        return ins

    def mmg(self, out, pairs, r=(), w=()):
        self._deps("pe", r, w)
        n = len(pairs)
        ins = None
        for i, (lhsT, rhs) in enumerate(pairs):
            ins = self.nc.tensor.matmul(out, lhsT, rhs, start=(i == 0), stop=(i == n - 1))
        self.cnt["pe"] += 1
        s = self.cnt["pe"]
        sem, v = self._sem("pe", s)
        ins.then_inc(sem, 1)
        self.stream["pe"].append(("i", ("eng", "pe", (s - 1) // self.CH), 1))
        self.snap["pe"][s] = dict(self.seen["pe"])
        self._commit(("eng", "pe", s), r, w)
        self.nops += n

    def mm1(self, out, lhsT, rhs, start, stop, r=(), w=()):
        self._deps("pe", r, w)
        ins = self.nc.tensor.matmul(out, lhsT, rhs, start=start, stop=stop)
        self.cnt["pe"] += 1
        s = self.cnt["pe"]
        sem, v = self._sem("pe", s)
        ins.then_inc(sem, 1)
        self.stream["pe"].append(("i", ("eng", "pe", (s - 1) // self.CH), 1))
        self.snap["pe"][s] = dict(self.seen["pe"])
        self._commit(("eng", "pe", s), r, w)
        self.nops += 1

    def dma(self, q, key, out, in_, r=(), w=()):
        if key not in self.dsem:
            self.dsem[key] = self.stack.enter_context(self.nc.semaphore(f"d_{key}"))
            self.dcnt[key] = 0
        self._deps(q, r, w)
        self.eng[q].dma_start(out=out, in_=in_).then_inc(self.dsem[key], 16)
        self.stream[q].append(("i", ("dma", key), 16))
        self.dcnt[key] += 16
        h = ("dma", key, self.dcnt[key])
        self._commit(h, r, w)
        self.dma_tokens.update(r)
        self.dma_tokens.update(w)
        return h

    def barrier(self):
        es = ["pe", "act", "dve"]
        for e in es:
            for f in es:
                if f != e and self.cnt[f] > self.seen[e][f]:
                    self._wait(e, ("eng", f, self.cnt[f]))
        for t in list(self.lastw):
            if t in self.dma_tokens:
                continue
            del self.lastw[t]
            self.readers.pop(t, None)
        for t in list(self.readers):
            if t not in self.dma_tokens and t not in self.lastw:
                del self.readers[t]

    def check_deadlock(self):
        val = {}
        ptr = {e: 0 for e in self.eng}
        prog = True
        while prog:
            prog = False
            for e in self.eng:
                st = self.stream[e]
                while ptr[e] < len(st):
                    k, key, v = st[ptr[e]]
                    if k == "w":
                        if val.get(key, 0) < v:
                            break
                    else:
                        val[key] = val.get(key, 0) + v
                    ptr[e] += 1
                    prog = True
        stuck = {e: (ptr[e], len(self.stream[e]), self.stream[e][ptr[e]], val.get(self.stream[e][ptr[e]][1], 0))
                 for e in self.eng if ptr[e] < len(self.stream[e])}
        assert not stuck, ("DEADLOCK", stuck)

    def wait_all(self, e):
        for t, h in self.lastw.items():
            self._wait(e, h)
        for t, hs in self.readers.items():
            for h in hs:
                self._wait(e, h)


def build(layers):
    nc = bass.Bass(target_bir_lowering=False)
    stack = ExitStack()

    def din(name, shape):
        return nc.dram_tensor(name, list(shape), F32, kind="ExternalInput").ap()

    x_d = din("x", (128, 8, T))
    xh_d = din("xh", (128, 8, HALO))
    mem_d = din("mem", (128, 8, 256))
    par_d = din("params", (128, NP))
    mask_d = din("mask", (128, MASKW))
    w_in_d = din("ev_w_in", (2, 1024, 2560))
    gaw_d = din("ev_gate_a_w", (2, 4, 128, 128))
    gxw_d = din("ev_gate_x_w", (2, 4, 128, 128))
    w_out_d = din("ev_w_out", (2, 1024, 1024))
    poolw_d = din("od_pool_w", (2, 4, 256, 256))
    xwq_d = din("xa_w_q", (4, 1024, 1024))
    xwkv_d = din("xa_w_kv", (4, 1024, 2048))
    xwo_d = din("xa_w_o", (4, 1024, 1024))
    w1_d = din("mlp_w1", (4, 1024, 4096))
    w2_d = din("mlp_w2", (4, 4096, 1024))
    y_d = nc.dram_tensor("y", [128, 8, T], F32, kind="ExternalOutput").ap()

    ncc_kv = sum(1 for l in layers if l % 2 == 0)
    b0 = [nc.dram_tensor(f"b0_{i}", [128, 128], F32) for i in range(len(layers))]
    g0 = [nc.dram_tensor(f"g0_{i}", [256, 128], F32) for i in range(len(layers))]
    b2 = [[nc.dram_tensor(f"b2_{i}_{h}", [256, 2048], BF16) for h in range(4)] for i in range(ncc_kv)]
    g2 = [[nc.dram_tensor(f"g2_{i}_{h}", [512, 2048], BF16) for h in range(4)] for i in range(ncc_kv)]
    b3 = [nc.dram_tensor(f"b3_{i}", [128, 4], F32) for i in range(ncc_kv)]
    g3 = [nc.dram_tensor(f"g3_{i}", [256, 4], F32) for i in range(ncc_kv)]

    def sb(name, shape, dt):
        return stack.enter_context(nc.sbuf_tensor(name, list(shape), dt))

    xres = sb("xres", (128, 8, T), F32)
    par = sb("par", (128, NP), F32)
    xhalo = sb("xhalo", (128, 8, HALO), F32)
    hprev = sb("hprev", (128, 8, HALO), F32)
    tf = sb("tf", (128, 8, 528), F32)
    sm = sb("sm", (128, 96), F32)
    hb = sb("hb", (128, 8 * HW), BF16)
    qk = sb("qk", (128, 8, T), BF16)
    vv = sb("vv", (128, 16, 512), BF16)
    NSLOT = 3
    wsl = sb("wsl", (128, NSLOT, 4096), BF16)
    bt = sb("bt", (128, 9, 512), BF16)
    yb = bt[:, 2:6, :].rearrange("p a b -> p (a b)")
    maskt = sb("maskt", (128, MASKW), BF16)
    memn = sb("memn", (128, 8, 256), BF16)
    ones = sb("ones", (128, 4, 128), BF16)
    ps = [stack.enter_context(nc.psum_tensor(f"ps{i}", [128, 512], F32)) for i in range(8)]

    hbuf = hb[:, :].rearrange("p (c t) -> p c t", c=8)
    kctx = hb[:, 0:8192].rearrange("p (h t) -> p h t", h=4)
    vctx = hb[:, 8192:16384].rearrange("p (b f) -> p b f", b=16)

    tk = Trk(nc, stack)
    op, mmg, dma = tk.op, tk.mmg, tk.dma

    def pcol(name, i=0, n=1):
        o, w = POFF[name]
        return par[:, o + i:o + i + n]

    class WStream:
        def __init__(self):
            self.blocks = []
            self.issued = 0
            self.used = 0
            self.released = set()
            self.cur = {}

        def add(self, tag, parts):
            self.blocks.append((tag, parts))

        def _issue(self, i):
            tag, parts = self.blocks[i]
            s = i % NSLOT
            for (lo, shape, src) in parts:
                n = int(np.prod(shape))
                dst = wsl[:, s, lo:lo + n]
                if len(shape) == 1:
                    pass
                elif len(shape) == 2:
                    dst = dst.rearrange("p (a b) -> p a b", a=shape[0])
                elif len(shape) == 3:
                    dst = dst.rearrange("p (a b c) -> p a b c", a=shape[0], b=shape[1])
                dma("pool", f"w{s}", dst, src, w=[("w", s)])

        def _pump(self):
            while self.issued < len(self.blocks) and (
                    self.issued < NSLOT or (self.issued - NSLOT) in self.released):
                self._issue(self.issued)
                self.issued += 1

        def next(self, tag):
            i = self.used
            assert self.blocks[i][0] == tag, (self.blocks[i][0], tag)
            self._pump()
            assert self.issued > i, ("weight slot not released", tag)
            self.used += 1
            s = i % NSLOT
            self.cur[s] = i
            return s, wsl[:, s, :]

        def release(self, s):
            self.released.add(self.cur[s])
            self._pump()

    ws = WStream()

    def wview(s, a, b, lo=0):
        return wsl[:, s, lo:lo + a * b].rearrange("p (a b) -> p a b", a=a)

    def cols_block(wd2, c0, n=512):
        return wd2.rearrange("(kc p) n -> p kc n", p=128)[:, :, c0:c0 + n]

    def rows_block(wd2, r0, nchunks):
        return wd2[r0:r0 + nchunks * 128, :].rearrange("(j p) n -> p j n", p=128)

    def schedule_layer(l):
        if l % 2 == 0:
            e = l // 2
            w = w_in_d[e]
            for c in range(4):
                ws.add(f"rg{l}_{c}", [(0, (8, 128), cols_block(w, 1536 + c * 128, 128)),
                                      (1024, (8, 128), cols_block(w, 2048 + c * 128, 128)),
                                      (2048, (128,), gaw_d[e, c]),
                                      (2176, (128,), gxw_d[e, c])])
            ws.add(f"v{l}", [(0, (8, 512), cols_block(w, 1024))])
            ws.add(f"woy{l}", [(0, (4, 1024), rows_block(w_out_d[e], 512, 4))])
            ws.add(f"k{l}", [(0, (8, 512), cols_block(w, 512))])
            ws.add(f"q{l}", [(0, (8, 512), cols_block(w, 0))])
            ws.add(f"woa{l}", [(0, (4, 1024), rows_block(w_out_d[e], 0, 4))])
        else:
            o = l // 2
            ws.add(f"pool{l}", [(i * 1024, (4, 256),
                                 poolw_d[o][:, i * 128:(i + 1) * 128, :].rearrange("g p n -> p g n"))
                                for i in range(2)])
        kv = xwkv_d[l]
        for j in range(2):
            ws.add(f"xk{l}_{j}", [(0, (8, 512), cols_block(kv, j * 512))])
        for j in range(2):
            ws.add(f"xv{l}_{j}", [(0, (8, 512), cols_block(kv, 1024 + j * 512))])
        for j in range(2):
            ws.add(f"xq{l}_{j}", [(0, (8, 512), cols_block(xwq_d[l], j * 512))])
        for j in range(2):
            ws.add(f"xo{l}_{j}", [(0, (4, 1024), rows_block(xwo_d[l], j * 512, 4))])
        for g in range(8):
            ws.add(f"w1_{l}_{g}", [(0, (8, 512), cols_block(w1_d[l], g * 512))])
            ws.add(f"w2_{l}_{g}", [(0, (4, 1024), rows_block(w2_d[l], g * 512, 4))])

    for l in layers:
        schedule_layer(l)

    def X(c, tt):
        return ("x", c, tt)

    def HB(c, tt):
        return ("hb", c, tt)

    def QK(c, tt):
        return ("qk", c, tt)

    ALLHB = [("hb", c, tt) for c in range(8) for tt in range(4)] + [("hbh",)]

    def act_fn(out, in_, func, **kw):
        return lambda e: e.activation(out=out, in_=in_, func=func, **kw)

    def emit_rstd(psum_ap, dst, r, w):
        op("act", act_fn(dst, psum_ap, AF.Ln, bias=EPSC), r=list(r) + [("epsc",)], w=w)
        op("act", act_fn(dst, dst, AF.Exp, scale=-0.5), r=w, w=w)

    def emit_recip(psum_ap, dst, r, w):
        op("act", act_fn(dst, psum_ap, AF.Ln), r=list(r), w=w)
        op("act", act_fn(dst, dst, AF.Exp, scale=-1.0), r=w, w=w)

    def stt(out, in0, scalar, in1, op0, op1):
        return lambda e: e.scalar_tensor_tensor(out, in0, scalar, in1, op0, op1)

    def tt_(out, in0, in1, opx):
        return lambda e: e.tensor_tensor(out, in0, in1, opx)

    def ts_(out, in0, s1, s2, op0, op1=None):
        if op1 is None:
            return lambda e: e.tensor_scalar(out, in0, s1, None, op0)
        return lambda e: e.tensor_scalar(out, in0, s1, s2, op0, op1)

    EPSC = sm[:, 1:2]
    ONE_D, ONE_128, ONE_256, ONE_1 = (ones[:, i, :] for i in range(4))
    sq = [bt[:, 0, :], bt[:, 1, :]]
    eb = [bt[:, 2, :], bt[:, 3, :]]
    pb = [bt[:, 4, :], bt[:, 5, :]]
    SQ = [("sq", 0), ("sq", 1)]
    EB = [("eb", 0), ("eb", 1)]
    PB = [("pb", 0), ("pb", 1)]
    PS = [("ps", i) for i in range(8)]
    TF = [("tf", i) for i in range(8)]

    for tt in range(NT):
        dma("sp", f"xin{tt}", xres[:, :, tile_sl(tt)], x_d[:, :, tile_sl(tt)],
            w=[X(c, tt) for c in range(8)])
    dma("sp", "par", par[:, :], par_d[:, :], w=[("par",)])
    dma("sp", "xh", xhalo[:, :, :], xh_d[:, :, :], w=[("xhalo",)])
    memf = tf[:, 0:8, 0:256]
    dma("sp", "mem", memf, mem_d[:, :, :], w=TF[0:8])
    dma("pool", "mask", maskt[:, :], mask_d[:, :], w=[("mask",)])
    for i, val in enumerate([1.0 / 1024, 1.0 / 128, 1.0 / 256, 1.0]):
        op("dve", lambda e, i=i, val=val: e.memset(ones[:, i, :], val), w=[("ones",)])
    op("dve", lambda e: e.memset(bt[:, 6, :], 0.0), w=[("sm0",)])
    ZT = bt[:, 6, :]
    op("dve", lambda e: e.memset(sm[:, 1:2], EPS), w=[("epsc",)])

    sqm = qk[:, 0, 0:2048].rearrange("p (c m) -> p c m", c=8)
    for c in range(8):
        op("act", act_fn(sqm[:, c, :], memf[:, c, :], AF.Square), r=TF[0:8], w=[QK(0, 0)])
    mmg(ps[0][:, 0:256], [(ONE_D, sqm[:, c, :]) for c in range(8)],
        r=[QK(0, 0), ("ones",)], w=[PS[0]])
    emit_rstd(ps[0][:, 0:256], tf[:, 4, 256:512], r=[PS[0]], w=[("mrs",)])
    for c in range(8):
        op("dve", stt(memn[:, c, :], memf[:, c, :], pcol("memg", c), tf[:, 4, 256:512],
                      ALU.mult, ALU.mult),
           r=TF[0:8] + [("mrs",), ("par",)], w=[("memn",)])
    tk.barrier()

    def halo_prep(gname, to_hbuf):
        op("dve", ts_(xhalo[:, :, :], xhalo[:, :, :], pcol("flag"), None, ALU.mult),
           r=[("par",)], w=[("xhalo",)])
        sqh = bt[:, 0, 0:128].rearrange("p (c t) -> p c t", c=8)
        op("act", act_fn(sqh, xhalo[:, :, :], AF.Square), r=[("xhalo",)], w=[SQ[0]])
        mmg(ps[7][:, 0:HALO], [(ONE_D, sqh[:, c, :]) for c in range(8)],
            r=[SQ[0], ("ones",)], w=[PS[7]])
        rh = sm[:, 16:32]
        emit_rstd(ps[7][:, 0:HALO], rh, r=[PS[7]], w=[("rh",)])
        for c in range(8):
            dst = hbuf[:, c, 0:HALO] if to_hbuf else hprev[:, c, :]
            op("dve", stt(dst, xhalo[:, c, :], pcol(gname, c), rh, ALU.mult, ALU.mult),
               r=[("xhalo",), ("rh",), ("par",)], w=[("hbh",)] if to_hbuf else [("hprev", c)])

    def norm_tile_rstd(tt, dst_tf):
        for c in range(8):
            op("act", act_fn(sq[c % 2], xres[:, c, tile_sl(tt)], AF.Square),
               r=[X(c, tt)], w=[SQ[c % 2]])
            tk.mm1(ps[7 - tt % 2][:, :], ONE_D, sq[c % 2], c == 0, c == 7, r=[SQ[c % 2], ("ones",)],
                   w=[PS[7 - tt % 2]])
        emit_rstd(ps[7 - tt % 2][:, :], tf[:, dst_tf, 0:512], r=[PS[7 - tt % 2]], w=[TF[dst_tf]])

    def norm_to_hbuf(gname):
        for tt in range(NT):
            ri = 7 - tt % 2
            norm_tile_rstd(tt, ri)
            for c in range(8):
                op("dve", stt(hbuf[:, c, HALO + tt * TT:HALO + (tt + 1) * TT],
                              xres[:, c, tile_sl(tt)], pcol(gname, c), tf[:, ri, 0:512],
                              ALU.mult, ALU.mult),
                   r=[X(c, tt), TF[ri], ("par",)], w=[HB(c, tt)])

    def hslice(c, tt):
        return hbuf[:, c, HALO + tt * TT:HALO + (tt + 1) * TT]

    def add_to_x(m, tt, psum_ap, psi):
        op("dve", tt_(xres[:, m, tile_sl(tt)], psum_ap, xres[:, m, tile_sl(tt)], ALU.add),
           r=[PS[psi]], w=[X(m, tt)])

    cc_count = [0]

    def collective(src_d, dst_d, wait_handles):
        for h in wait_handles:
            tk._wait("pool", h)
        sem = stack.enter_context(nc.semaphore(f"cc{cc_count[0]}"))
        cc_count[0] += 1
        nc.gpsimd.collective_compute(
            "AllGather", ALU.bypass, replica_groups=GROUPS,
            ins=[src_d.ap().opt()], outs=[dst_d.ap().opt()]).then_inc(sem)
        tk.stream["pool"].append(("i", ("cc", id(sem)), 1))
        return sem

    def proj_headnorm(wt, wtok, gcol, base):
        its = [(hd, tt) for hd in range(4) for tt in range(NT)]

        def front(i):
            hd, tt = its[i]
            pa = i % 3
            mmg(ps[pa][:, :], [(wt[:, kc, hd * 128:(hd + 1) * 128], hslice(kc, tt)) for kc in range(8)],
                r=[HB(kc, tt) for kc in range(8)] + [wtok], w=[PS[pa]])
            op("act", act_fn(sq[i % 2], ps[pa][:, :], AF.Square), r=[PS[pa]], w=[SQ[i % 2]])

        def back(i):
            hd, tt = its[i]
            pa, pn, ti = i % 3, 3 + i % 2, i % 2
            mmg(ps[pn][:, :], [(ONE_128, sq[i % 2])], r=[SQ[i % 2], ("ones",)], w=[PS[pn]])
            emit_rstd(ps[pn][:, :], tf[:, ti, 0:512], r=[PS[pn]], w=[TF[ti]])
            op("dve", stt(qk[:, base + hd, tile_sl(tt)], ps[pa][:, :], gcol, tf[:, ti, 0:512],
                          ALU.mult, ALU.mult), r=[PS[pa], TF[ti], ("par",)], w=[QK(base + hd, tt)])

        for i in range(len(its) + 1):
            if i < len(its):
                front(i)
            if i >= 1:
                back(i - 1)

    def even_mixer(l, li, ei):
        e = l // 2
        halo_prep(f"mixg{l}", True)
        norm_to_hbuf(f"mixg{l}")
        sp8 = sm[:, 4:8]
        op("act", act_fn(sp8, pcol(f"lam{l}", 0, 4), AF.Exp, scale=-1.0), r=[("par",)], w=[("sp8",)])
        op("act", act_fn(sp8, sp8, AF.Ln, bias=1.0), r=[("sp8",)], w=[("sp8",)])
        op("dve", ts_(sp8, sp8, -8.0, None, ALU.mult), r=[("sp8",)], w=[("sp8",)])

        xrh = sm[:, 40:56]
        hfin = sm[:, 8:12]
        hc = sm[:, 12:13]
        pc = sm[:, 13:14]
        hA = sm[:, 32:36]
        xrt = [tf[:, 0, 0:515], tf[:, 1, 0:515]]
        gxb = [tf[:, 2, 0:512], tf[:, 7, 0:512]]
        GXT = [TF[2], TF[7]]
        rits = [(c, tt) for c in range(4) for tt in range(NT)]
        rgs = {}

        def rg_views(c):
            srg = rgs[c]
            return (srg, wview(srg, 8, 128, 0), wview(srg, 8, 128, 1024),
                    wsl[:, srg, 2048:2176], wsl[:, srg, 2176:2304])

        def rfront(i):
            c, tt = rits[i]
            p = i % 2
            if tt == 0:
                rgs[c], _ = ws.next(f"rg{l}_{c}")
            srg, wxr, wgt, wga, wgx = rg_views(c)
            if tt == 0:
                mmg(ps[6][:, 0:16], [(wxr[:, kc, :], hbuf[:, kc, 0:HALO]) for kc in range(8)],
                    r=[("hbh",), ("w", srg)], w=[PS[6]])
                op("act", act_fn(xrh, ps[6][:, 0:16], AF.Copy), r=[PS[6]], w=[("xrh",)])
            mmg(ps[p][:, :], [(wxr[:, kc, :], hslice(kc, tt)) for kc in range(8)],
                r=[HB(kc, tt) for kc in range(8)] + [("w", srg)], w=[PS[p]])
            mmg(ps[2 + p][:, :], [(wgt[:, kc, :], hslice(kc, tt)) for kc in range(8)],
                r=[HB(kc, tt) for kc in range(8)] + [("w", srg)], w=[PS[2 + p]])
            if tt == 0:
                op("dve", lambda e_: e_.tensor_copy(xrt[p][:, 0:3], xrh[:, 13:16]), r=[("xrh",)], w=[TF[p]])
            else:
                op("dve", lambda e_: e_.tensor_copy(xrt[p][:, 0:3], xrt[1 - p][:, 512:515]), r=[TF[1 - p]], w=[TF[p]])
            op("act", act_fn(xrt[p][:, 3:515], ps[p][:, :], AF.Copy), r=[PS[p]], w=[TF[p]])
            op("act", act_fn(gxb[p], ps[2 + p][:, :], AF.Copy), r=[PS[2 + p]], w=[GXT[p]])

        def rback(i):
            c, tt = rits[i]
            p = i % 2
            srg, wxr, wgt, wga, wgx = rg_views(c)
            xt, gx = xrt[p], gxb[p]
            g3_ = tf[:, 3, 0:512]
            xc = tf[:, 6, 0:512]
            ra = tf[:, 4, 0:512]
            ri = tf[:, 5, 0:512]
            cw = lambda j: pcol(f"convw{l}", j * 4 + c)
            if tt == 0:
                op("dve", lambda e_: e_.memset(hc, 0.0), w=[("hc",)])
                op("dve", lambda e_: e_.memset(pc, 1.0), w=[("pc",)])
            op("act", act_fn(g3_, gx, AF.Square), r=[GXT[p]], w=[TF[3]])
            op("act", act_fn(xc, xt[:, 3:515], AF.Identity, scale=cw(3), bias=pcol(f"convb{l}", c)),
               r=[TF[p], ("par",)], w=[TF[6]])
            op("dve", ts_(g3_, g3_, 0.044715, 1.0, ALU.mult, ALU.add), r=[TF[3]], w=[TF[3]])
            op("dve", stt(xc, xt[:, 0:512], cw(0), xc, ALU.mult, ALU.add), r=[TF[p], TF[6], ("par",)], w=[TF[6]])
            op("dve", tt_(g3_, g3_, gx, ALU.mult), r=[GXT[p], TF[3]], w=[TF[3]])
            op("dve", stt(xc, xt[:, 1:513], cw(1), xc, ALU.mult, ALU.add), r=[TF[p], TF[6], ("par",)], w=[TF[6]])
            op("act", act_fn(g3_, g3_, AF.Sigmoid, scale=1.5957691216057308), r=[TF[3]], w=[TF[3]])
            op("dve", stt(xc, xt[:, 2:514], cw(2), xc, ALU.mult, ALU.add), r=[TF[p], TF[6], ("par",)], w=[TF[6]])
            op("act", act_fn(sq[0], xc, AF.Copy), r=[TF[6]], w=[SQ[0]])
            op("dve", tt_(gx, gx, g3_, ALU.mult), r=[GXT[p], TF[3]], w=[GXT[p]])
            mmg(ps[4][:, :], [(wga, sq[0])], r=[SQ[0], ("w", srg)], w=[PS[4]])
            mmg(ps[5][:, :], [(wgx, sq[0])], r=[SQ[0], ("w", srg)], w=[PS[5]])
            op("act", act_fn(ra, ps[4][:, :], AF.Sigmoid, bias=pcol(f"gab{l}", c)), r=[PS[4], ("par",)], w=[TF[4]])
            op("act", act_fn(ri, ps[5][:, :], AF.Sigmoid, bias=pcol(f"gxb{l}", c)), r=[PS[5], ("par",)], w=[TF[5]])
            op("act", act_fn(ra, ra, AF.Exp, scale=sp8[:, c:c + 1]), r=[TF[4], ("sp8",)], w=[TF[4]])
            op("dve", tt_(ri, ri, xc, ALU.mult), r=[TF[5], TF[6]], w=[TF[5]])
            op("dve", tt_(g3_, ra, ra, ALU.mult), r=[TF[4]], w=[TF[3]])
            op("act", act_fn(g3_, g3_, AF.Sqrt, scale=-1.0, bias=1.0), r=[TF[3]], w=[TF[3]])
            pp = xc
            op("dve", lambda e_: e_.tensor_tensor_scan(pp, ra, ZT, pc, ALU.mult, ALU.add),
               r=[TF[4], TF[5], ("pc",), ("sm0",)], w=[TF[6]])
            op("dve", tt_(ri, ri, g3_, ALU.mult), r=[TF[5], TF[3]], w=[TF[5]])
            op("dve", lambda e_: e_.tensor_copy(pc, pp[:, 511:512]), r=[TF[6]], w=[("pc",)])
            hh = g3_
            op("dve", lambda e_: e_.tensor_tensor_scan(hh, ra, ri, hc, ALU.mult, ALU.add),
               r=[TF[4], TF[5], ("hc",)], w=[TF[3]])
            op("dve", tt_(qk[:, c, tile_sl(tt)], pp, gx, ALU.mult), r=[TF[6], GXT[p]], w=[QK(c, tt)])
            op("dve", lambda e_: e_.tensor_copy(hc, hh[:, 511:512]), r=[TF[3]], w=[("hc",)])
            op("dve", tt_(qk[:, 4 + c, tile_sl(tt)], hh, gx, ALU.mult), r=[TF[3], GXT[p]], w=[QK(4 + c, tt)])
            if tt == NT - 1:
                op("dve", lambda e_: e_.tensor_copy(hfin[:, c:c + 1], hc), r=[("hc",)], w=[("hfin",)])
                ws.release(srg)

        for i in range(len(rits) + 1):
            if i < len(rits):
                rfront(i)
            if i >= 1:
                rback(i - 1)
        h3 = dma("pool", f"b3_{ei}", b3[ei].ap(), hfin, r=[("hfin",)])
        cc3 = collective(b3[ei], g3[ei], [h3])

        sv, _ = ws.next(f"v{l}")
        wv = wview(sv, 8, 512)
        for blk in range(16):
            pa = 4 + blk % 2
            tt = blk // 4
            mmg(ps[pa][:, :],
                [(hbuf[:, kc, HALO + blk * 128:HALO + (blk + 1) * 128], wv[:, kc, :]) for kc in range(8)],
                r=[HB(kc, tt) for kc in range(8)] + [("w", sv)], w=[PS[pa]])
            op("act", act_fn(vv[:, blk, :], ps[pa][:, :], AF.Copy), r=[PS[pa]], w=[("vv", blk)])
        ws.release(sv)

        tk.eng["pool"].wait_ge(cc3, 1)
        tk.stream["pool"].append(("w", ("cc", id(cc3)), 1))
        dma("pool", f"g3_{ei}", hA, g3[ei].ap()[0:128, :], w=[("hA",)])
        op("dve", ts_(hA, hA, pcol("flag"), None, ALU.mult), r=[("hA",), ("par",)], w=[("hA",)])
        for c in range(4):
            for tt in range(NT):
                op("dve", stt(qk[:, 4 + c, tile_sl(tt)], qk[:, c, tile_sl(tt)], hA[:, c:c + 1],
                              qk[:, 4 + c, tile_sl(tt)], ALU.mult, ALU.add),
                   r=[QK(c, tt), QK(4 + c, tt), ("hA",)], w=[QK(4 + c, tt)])
        swy, _ = ws.next(f"woy{l}")
        woy = wview(swy, 4, 1024)
        for m in range(8):
            for tt in range(NT):
                pi = 6 + (m * NT + tt) % 2
                mmg(ps[pi][:, :], [(woy[:, c, m * 128:(m + 1) * 128], qk[:, 4 + c, tile_sl(tt)]) for c in range(4)],
                    r=[QK(4 + c, tt) for c in range(4)] + [("w", swy)], w=[PS[pi]])
                add_to_x(m, tt, ps[pi][:, :], pi)
        ws.release(swy)

        sk, _ = ws.next(f"k{l}")
        wk = wview(sk, 8, 512)
        proj_headnorm(wk, ("w", sk), pcol(f"kg{l}"), 4)
        ws.release(sk)
        cc2 = []
        for hd in range(4):
            h2a = dma("pool", f"b2k_{ei}_{hd}", b2[ei][hd].ap()[0:128, :], qk[:, 4 + hd, :],
                      r=[QK(4 + hd, tt) for tt in range(4)])
            h2b = dma("pool", f"b2v_{ei}_{hd}",
                      b2[ei][hd].ap()[128:256, :].rearrange("p (b f) -> p b f", b=16),
                      vv[:, :, hd * 128:(hd + 1) * 128], r=[("vv", blk) for blk in range(16)])
            cc2.append(collective(b2[ei][hd], g2[ei][hd], [h2a, h2b]))

        sq_, _ = ws.next(f"q{l}")
        wq = wview(sq_, 8, 512)
        proj_headnorm(wq, ("w", sq_), pcol(f"qg{l}"), 0)
        ws.release(sq_)
        tk.barrier()
        for f in ("pe", "act", "dve"):
            tk._wait("pool", ("eng", f, tk.cnt[f]))
        for hd in range(4):
            tk.eng["pool"].wait_ge(cc2[hd], 1)
            tk.stream["pool"].append(("w", ("cc", id(cc2[hd])), 1))
            dma("pool", f"ctxk_{ei}_{hd}", kctx[:, hd, :], g2[ei][hd].ap()[0:128, :],
                w=[("kctx",)] + ALLHB)
            dma("pool", f"ctxv_{ei}_{hd}", vctx[:, :, hd * 128:(hd + 1) * 128],
                g2[ei][hd].ap()[128:256, :].rearrange("p (b f) -> p b f", b=16),
                w=[("vctx",)] + ALLHB)

        scale = 128.0 ** -0.5
        ebs = [bt[:, 2, :], bt[:, 3, :], bt[:, 0, :], bt[:, 7, :]]
        pbs = [bt[:, 4, :], bt[:, 5, :], bt[:, 1, :], bt[:, 8, :]]
        EBS = [("eb", 0), ("eb", 1), ("sq", 0), ("bt7",)]
        PBS = [("pb", 0), ("pb", 1), ("sq", 1), ("bt8",)]
        SBK = [0, 1, 2, 7]
        items = []
        gi = 0
        for hd in range(4):
            for qt in range(NT):
                blocks = [("ctx", kb) for kb in range(4 * qt, 16)] + [("own", kb) for kb in range(0, 4 * qt + 4)]
                for bi, (kind, kb) in enumerate(blocks):
                    items.append((hd, qt, gi, bi, len(blocks), kind, kb))
                gi += 1
        LA = 3

        def att_front(idx):
            hd, qt, g, bi, nb, kind, kb = items[idx]
            si, bi3 = SBK[idx % 4], idx % 4
            if kind == "ctx":
                kT = kctx[:, hd, kb * 128:(kb + 1) * 128]
                rk = [("kctx",)]
                d0 = 512 * qt + 2048 - 128 * kb
                bias = pcol("ctxbias")
            else:
                kT = qk[:, 4 + hd, kb * 128:(kb + 1) * 128]
                rk = [QK(4 + hd, kb // 4)]
                d0 = 512 * qt - 128 * kb
                bias = 0.0
            off = d0 + 384
            mmg(ps[si][:, :], [(kT, qk[:, hd, tile_sl(qt)])], r=rk + [QK(hd, qt)], w=[PS[si]])
            op("act", act_fn(ebs[bi3], ps[si][:, :], AF.Exp, scale=scale, bias=bias),
               r=[PS[si], ("par",)], w=[EBS[bi3]])
            op("dve", tt_(pbs[bi3], ebs[bi3], maskt[:, off:off + 512], ALU.mult),
               r=[EBS[bi3], ("mask",)], w=[PBS[bi3]])

        def att_back(idx):
            hd, qt, g, bi, nb, kind, kb = items[idx]
            bi3 = idx % 4
            po, pd = 3 + g % 2, 5 + g % 2
            if kind == "ctx":
                vs = vctx[:, kb, hd * 128:(hd + 1) * 128]
                rv = [("vctx",)]
            else:
                vs = vv[:, kb, hd * 128:(hd + 1) * 128]
                rv = [("vv", kb)]
            tk.mm1(ps[po][:, :], vs, pbs[bi3], bi == 0, bi == nb - 1, r=rv + [PBS[bi3]], w=[PS[po]])
            tk.mm1(ps[pd][:, :], ONE_1, pbs[bi3], bi == 0, bi == nb - 1, r=[("ones",), PBS[bi3]], w=[PS[pd]])
            if bi == nb - 1:
                rdi = 6 + g % 2
                rd = tf[:, rdi, 0:512]
                emit_recip(ps[pd][:, :], rd, r=[PS[pd]], w=[TF[rdi]])
                op("dve", tt_(qk[:, hd, tile_sl(qt)], ps[po][:, :], rd, ALU.mult),
                   r=[PS[po], TF[rdi]], w=[QK(hd, qt)])

        for idx in range(len(items) + LA):
            if idx < len(items):
                att_front(idx)
            if idx - LA >= 0:
                att_back(idx - LA)
        swa, _ = ws.next(f"woa{l}")
        woa = wview(swa, 4, 1024)
        for tt in range(NT):
            for m in range(8):
                pi = (tt * 8 + m) % 2
                mmg(ps[pi][:, :], [(woa[:, c, m * 128:(m + 1) * 128], qk[:, c, tile_sl(tt)]) for c in range(4)],
                    r=[QK(c, tt) for c in range(4)] + [("w", swa)], w=[PS[pi]])
                add_to_x(m, tt, ps[pi][:, :], pi)
        ws.release(swa)
        tk.barrier()

    def odd_mixer(l, li):
        halo_prep(f"mixg{l}", False)
        spw, _ = ws.next(f"pool{l}")
        pw = wsl[:, spw, 0:2048].rearrange("p (i g n) -> p i g n", i=2, g=4)
        dt_ = vv[:, 0:8, :]
        for tt in range(NT):
            norm_tile_rstd(tt, 7)
            for g in range(4):
                w = 2 ** (g + 1)
                cs = (2 * g, 2 * g + 1)
                hfs = [tf[:, 0, 0:528], tf[:, 3, 0:528]]
                HFT = [TF[0], TF[3]]
                sbufs = [[(tf[:, 1, 0:528], TF[1]), (tf[:, 2, 0:528], TF[2])],
                         [(tf[:, 4, 0:528], TF[4]), (tf[:, 5, 0:528], TF[5])]]
                for q_, c in enumerate(cs):
                    op("dve", lambda e_, c=c, hf=hfs[q_]: e_.tensor_copy(hf[:, 0:16], hprev[:, c, :]),
                       r=[("hprev", c)], w=[HFT[q_]])
                for q_, c in enumerate(cs):
                    op("dve", stt(hfs[q_][:, 16:528], xres[:, c, tile_sl(tt)], pcol(f"mixg{l}", c), tf[:, 7, 0:512],
                                  ALU.mult, ALU.mult), r=[X(c, tt), TF[7], ("par",)], w=[HFT[q_]])
                for q_, c in enumerate(cs):
                    op("dve", lambda e_, c=c, hf=hfs[q_]: e_.tensor_copy(hprev[:, c, :], hf[:, 512:528]),
                       r=[HFT[q_]], w=[("hprev", c)])
                srcs = [(hfs[0], HFT[0]), (hfs[1], HFT[1])]
                for k in range(g + 1):
                    sh = 2 ** k
                    lo = 2 ** (k + 1) - 1
                    for q_ in range(2):
                        src, srct = srcs[q_]
                        dst, dstt = sbufs[q_][k % 2]
                        op("dve", tt_(dst[:, lo:528], src[:, lo:528], src[:, lo - sh:528 - sh], ALU.add),
                           r=[srct], w=[dstt])
                        srcs[q_] = (dst, dstt)
                for q_, c in enumerate(cs):
                    src, srct = srcs[q_]
                    op("dve", stt(dt_[:, c, :], src[:, 16:528], 1.0 / w, hfs[q_][:, 16:528], ALU.mult, ALU.subtract),
                       r=[srct, HFT[q_]], w=[("dt", c)])
                if tt == 0:
                    o_, _w = POFF["invcnt"]
                    for q_, c in enumerate(cs):
                        src, srct = srcs[q_]
                        t16 = sm[:, 40:56] if q_ == 0 else sm[:, 64:80]
                        op("dve", tt_(t16, src[:, 16:32], par[:, o_ + c * 16:o_ + (c + 1) * 16], ALU.mult),
                           r=[srct, ("par",)], w=[("t16", q_)])
                    for q_, c in enumerate(cs):
                        t16 = sm[:, 40:56] if q_ == 0 else sm[:, 64:80]
                        op("dve", tt_(dt_[:, c, 0:16], t16, hfs[q_][:, 16:32], ALU.subtract),
                           r=[("t16", q_), HFT[q_]], w=[("dt", c)])
            for j in range(8):
                g, jj = j // 2, j % 2
                pi = j % 2
                mmg(ps[pi][:, :], [(pw[:, i, g, jj * 128:(jj + 1) * 128], dt_[:, 2 * g + i, :]) for i in range(2)],
                    r=[("dt", 2 * g), ("dt", 2 * g + 1), ("w", spw)], w=[PS[pi]])
                op("dve", stt(xres[:, j, tile_sl(tt)], ps[pi][:, :], pcol(f"scale{l}", j),
                              xres[:, j, tile_sl(tt)], ALU.mult, ALU.add),
                   r=[PS[pi], ("par",)], w=[X(j, tt)])
        ws.release(spw)
        tk.barrier()

    def xattn(l):
        norm_to_hbuf(f"xag{l}")
        kx = vv[:, 0:4, :].rearrange("p a (b m) -> p (a b) m", b=2)
        vx = vv[:, 4:8, :].rearrange("p (b j) f -> p b (j f)", b=2)
        for j in range(2):
            sw, _ = ws.next(f"xk{l}_{j}")
            wk = wview(sw, 8, 512)
            for hh in range(2):
                h = 2 * j + hh
                for i in range(2):
                    mmg(ps[i][:, 0:256],
                        [(wk[:, kc, (2 * hh + i) * 128:(2 * hh + i + 1) * 128], memn[:, kc, :]) for kc in range(8)],
                        r=[("memn",), ("w", sw)], w=[PS[i]])
                    op("act", act_fn(sq[i][:, 0:256], ps[i][:, 0:256], AF.Square), r=[PS[i]], w=[SQ[i]])
                mmg(ps[2][:, 0:256], [(ONE_256, sq[0][:, 0:256]), (ONE_256, sq[1][:, 0:256])],
                    r=[SQ[0], SQ[1], ("ones",)], w=[PS[2]])
                emit_rstd(ps[2][:, 0:256], tf[:, 0, 0:256], r=[PS[2]], w=[TF[0]])
                for i in range(2):
                    op("dve", stt(kx[:, 2 * h + i, :], ps[i][:, 0:256], pcol(f"xkg{l}", i), tf[:, 0, 0:256],
                                  ALU.mult, ALU.mult), r=[PS[i], TF[0], ("par",)], w=[("kx",)])
            ws.release(sw)
        for j in range(2):
            sw, _ = ws.next(f"xv{l}_{j}")
            wv = wview(sw, 8, 512)
            for blk in range(2):
                pi = 3 + blk
                mmg(ps[pi][:, :], [(memn[:, kc, blk * 128:(blk + 1) * 128], wv[:, kc, :]) for kc in range(8)],
                    r=[("memn",), ("w", sw)], w=[PS[pi]])
                op("act", act_fn(vx[:, blk, j * 512:(j + 1) * 512], ps[pi][:, :], AF.Copy),
                   r=[PS[pi]], w=[("vx",)])
            ws.release(sw)
        sqx = [[bt[:, 0, :], bt[:, 1, :]], [bt[:, 2, :], bt[:, 3, :]], [bt[:, 7, :], bt[:, 8, :]]]
        SQX = [[("sq", 0), ("sq", 1)], [("eb", 0), ("eb", 1)], [("bt7",), ("bt8",)]]
        for j in range(2):
            sw, _ = ws.next(f"xq{l}_{j}")
            wq = wview(sw, 8, 512)
            its = [(hh, tt) for hh in range(2) for tt in range(NT)]

            def qfront(n, its=its, wq=wq, sw=sw):
                hh, tt = its[n]
                p = n % 3
                for i in range(2):
                    bk = 2 * p + i
                    mmg(ps[bk][:, :],
                        [(wq[:, kc, (2 * hh + i) * 128:(2 * hh + i + 1) * 128], hslice(kc, tt)) for kc in range(8)],
                        r=[HB(kc, tt) for kc in range(8)] + [("w", sw)], w=[PS[bk]])
                    op("act", act_fn(sqx[p][i], ps[bk][:, :], AF.Square), r=[PS[bk]], w=[SQX[p][i]])

            def qback(n, its=its, j=j):
                hh, tt = its[n]
                h = 2 * j + hh
                p = n % 3
                pn = 6 + n % 2
                mmg(ps[pn][:, :], [(ONE_256, sqx[p][0]), (ONE_256, sqx[p][1])],
                    r=[SQX[p][0], SQX[p][1], ("ones",)], w=[PS[pn]])
                emit_rstd(ps[pn][:, :], tf[:, p, 0:512], r=[PS[pn]], w=[TF[p]])
                for i in range(2):
                    op("dve", stt(qk[:, 2 * h + i, tile_sl(tt)], ps[2 * p + i][:, :], pcol(f"xqg{l}", i),
                                  tf[:, p, 0:512], ALU.mult, ALU.mult),
                       r=[PS[2 * p + i], TF[p], ("par",)], w=[QK(2 * h + i, tt)])

            for n in range(len(its) + 1):
                if n < len(its):
                    qfront(n)
                if n >= 1:
                    qback(n - 1)
            ws.release(sw)
        scale = 256.0 ** -0.5
        ebx = [[bt[:, 2, :], bt[:, 3, :]], [bt[:, 4, :], bt[:, 5, :]]]
        EBX = [[("eb", 0), ("eb", 1)], [("pb", 0), ("pb", 1)]]
        aits = [(h, tt) for h in range(4) for tt in range(NT)]

        def afront(n):
            h, tt = aits[n]
            p = n % 2
            for blk in range(2):
                bk = 2 * p + blk
                mmg(ps[bk][:, :],
                    [(kx[:, 2 * h + i, blk * 128:(blk + 1) * 128], qk[:, 2 * h + i, tile_sl(tt)]) for i in range(2)],
                    r=[("kx",), QK(2 * h, tt), QK(2 * h + 1, tt)], w=[PS[bk]])
                op("act", act_fn(ebx[p][blk], ps[bk][:, :], AF.Exp, scale=scale), r=[PS[bk]], w=[EBX[p][blk]])

        def aback(n):
            h, tt = aits[n]
            p = n % 2
            pd = 6 + p
            for i in range(2):
                mmg(ps[4 + i][:, :],
                    [(vx[:, blk, h * 256 + i * 128:h * 256 + (i + 1) * 128], ebx[p][blk]) for blk in range(2)],
                    r=[("vx",), EBX[p][0], EBX[p][1]], w=[PS[4 + i]])
            mmg(ps[pd][:, :], [(ONE_1, ebx[p][0]), (ONE_1, ebx[p][1])],
                r=[EBX[p][0], EBX[p][1], ("ones",)], w=[PS[pd]])
            rd = tf[:, 2 + p, 0:512]
            emit_recip(ps[pd][:, :], rd, r=[PS[pd]], w=[TF[2 + p]])
            for i in range(2):
                op("dve", tt_(qk[:, 2 * h + i, tile_sl(tt)], ps[4 + i][:, :], rd, ALU.mult),
                   r=[PS[4 + i], TF[2 + p]], w=[QK(2 * h + i, tt)])

        for n in range(len(aits) + 1):
            if n < len(aits):
                afront(n)
            if n >= 1:
                aback(n - 1)
        s0, _ = ws.next(f"xo{l}_0")
        s1, _ = ws.next(f"xo{l}_1")
        wo = [wview(s0, 4, 1024), wview(s1, 4, 1024)]
        for tt in range(NT):
            for m in range(8):
                pi = 6 + (tt * 8 + m) % 2
                mmg(ps[pi][:, :],
                    [(wo[c // 4][:, c % 4, m * 128:(m + 1) * 128], qk[:, c, tile_sl(tt)]) for c in range(8)],
                    r=[QK(c, tt) for c in range(8)] + [("w", s0), ("w", s1)], w=[PS[pi]])
                add_to_x(m, tt, ps[pi][:, :], pi)
        ws.release(s0)
        ws.release(s1)

    def mlp(l):
        norm_to_hbuf(f"mlpg{l}")
        for g in range(8):
            s1, _ = ws.next(f"w1_{l}_{g}")
            s2, _ = ws.next(f"w2_{l}_{g}")
            w1 = wview(s1, 8, 512)
            w2 = wview(s2, 4, 1024)
            k = 0
            for j in range(4):
                for tt in range(NT):
                    pi = k % 4
                    ei_ = k % 2
                    k += 1
                    mmg(ps[pi][:, :], [(w1[:, kc, j * 128:(j + 1) * 128], hslice(kc, tt)) for kc in range(8)],
                        r=[HB(kc, tt) for kc in range(8)] + [("w", s1)], w=[PS[pi]])
                    op("act", act_fn(eb[ei_], ps[pi][:, :], AF.Relu), r=[PS[pi]], w=[EB[ei_]])
                    op("dve", tt_(qk[:, j, tile_sl(tt)], eb[ei_], eb[ei_], ALU.mult), r=[EB[ei_]], w=[QK(j, tt)])
            ws.release(s1)
            k = 0
            for tt in range(NT):
                for m in range(8):
                    pi = 4 + k % 4
                    k += 1
                    mmg(ps[pi][:, :], [(w2[:, j, m * 128:(m + 1) * 128], qk[:, j, tile_sl(tt)]) for j in range(4)],
                        r=[QK(j, tt) for j in range(4)] + [("w", s2)], w=[PS[pi]])
                    add_to_x(m, tt, ps[pi][:, :], pi)
            ws.release(s2)

    ei = 0
    for li, l in enumerate(layers):
        if li > 0:
            tk.eng["pool"].wait_ge(cc0, 1)
            tk.stream["pool"].append(("w", ("cc", id(cc0)), 1))
            dma("pool", f"g0_{li}", xhalo[:, :, :],
                g0[li].ap()[0:128, :].rearrange("p (c t) -> p c t", c=8), w=[("xhalo",)])
        if l % 2 == 0:
            even_mixer(l, li, ei)
            ei += 1
        else:
            odd_mixer(l, li)
        xattn(l)
        mlp(l)
        if li + 1 < len(layers):
            h0 = dma("pool", f"b0_{li + 1}", b0[li + 1].ap().rearrange("p (c t) -> p c t", c=8),
                     xres[:, :, T - HALO:T], r=[X(c, 3) for c in range(8)])
            cc0 = collective(b0[li + 1], g0[li + 1], [h0])

    for tt in range(NT):
        dma("sp", f"yout{tt}", y_d[:, :, tile_sl(tt)], xres[:, :, tile_sl(tt)],
            r=[X(c, tt) for c in range(8)])
    tk.wait_all("sp")
    assert ws.used == len(ws.blocks), (ws.used, len(ws.blocks))
    tk.check_deadlock()
    stack.close()
    return nc


WEIGHT_KEYS = ["ev_w_in", "ev_gate_a_w", "ev_gate_x_w", "ev_w_out", "od_pool_w",
               "xa_w_q", "xa_w_kv", "xa_w_o", "mlp_w1", "mlp_w2"]

_NC_CACHE = {}


def to_fm(a):
    t = a.shape[0]
    return np.ascontiguousarray(a.reshape(t, 8, 128).transpose(2, 1, 0))


def from_fm(a):
    t = a.shape[2]
    return np.ascontiguousarray(a.transpose(2, 1, 0).reshape(t, 1024))


def run_layers(inp, x_full, layers):
    key = tuple(layers)
    if key not in _NC_CACHE:
        _NC_CACHE[key] = build(list(layers))
    nc = _NC_CACHE[key]
    mask = make_mask()
    in_maps = []
    for core in range(8):
        b, half = core // 2, core % 2
        base = half * T
        m = {k: np.ascontiguousarray(np.asarray(inp[k], np.float32)) for k in WEIGHT_KEYS}
        m["x"] = to_fm(x_full[b, base:base + T])
        if half:
            m["xh"] = to_fm(x_full[b, base - HALO:base])
        else:
            m["xh"] = np.zeros((128, 8, HALO), np.float32)
        m["mem"] = to_fm(np.asarray(inp["mem"], np.float32)[b])
        m["params"] = pack_params(inp, half)
        m["mask"] = mask
        in_maps.append(m)
    res = run_bass_kernel_spmd(nc, in_maps, core_ids=list(range(8)))
    out = np.zeros_like(x_full)
    for core in range(8):
        b, half = core // 2, core % 2
        out[b, half * T:(half + 1) * T] = from_fm(np.asarray(res.results[core]["y"]))
    return out


FUSED = True


def kernel(**inp):
    x = np.ascontiguousarray(np.asarray(inp["x"], np.float32))
    if FUSED:
        return run_layers(inp, x, [0, 1, 2, 3])
    for l in range(4):
        x = run_layers(inp, x, [l])
    return x
```

```python
from contextlib import ExitStack
import numpy as np
import concourse.bass as bass
import concourse.mybir as mybir
from concourse.bass_utils import run_bass_kernel_spmd

F32 = mybir.dt.float32
BF16 = mybir.dt.bfloat16
AF = mybir.ActivationFunctionType
ALU = mybir.AluOpType

T = 2048
NT = 4
TT = 512
HALO = 16
HW = HALO + T
KC = 8
EPS = 1e-6
MASKW = 384 + 2048 + 512
NEG = -30000.0
GROUPS = [[0, 1], [2, 3], [4, 5], [6, 7]]
SAME_ENGINE_SYNC = True


def tile_sl(tt):
    return slice(tt * TT, (tt + 1) * TT)


def param_layout():
    off = {}
    n = 0

    def add(name, w):
        nonlocal n
        off[name] = (n, w)
        n += w

    for l in range(4):
        add(f"mixg{l}", 8)
        add(f"xag{l}", 8)
        add(f"mlpg{l}", 8)
        add(f"xqg{l}", 2)
        add(f"xkg{l}", 2)
        if l % 2 == 0:
            add(f"convw{l}", 16)
            add(f"convb{l}", 4)
            add(f"gab{l}", 4)
            add(f"gxb{l}", 4)
            add(f"lam{l}", 4)
            add(f"qg{l}", 1)
            add(f"kg{l}", 1)
        else:
            add(f"scale{l}", 8)
    add("memg", 8)
    add("flag", 1)
    add("ctxbias", 1)
    add("invcnt", 8 * 16)
    return off, n


POFF, NP = param_layout()


def pack_params(inp, half):
    P = np.zeros((128, NP), np.float32)

    def put(name, arr):
        o, w = POFF[name]
        arr = np.asarray(arr, np.float32)
        assert arr.shape == (128, w), (name, arr.shape, w)
        P[:, o:o + w] = arr

    def cols(v):
        v = np.asarray(v, np.float32)
        return v.reshape(-1, 128).T

    for l in range(4):
        put(f"mixg{l}", cols(inp["mix_norm_g"][l]))
        put(f"xag{l}", cols(inp["xattn_norm_g"][l]))
        put(f"mlpg{l}", cols(inp["mlp_norm_g"][l]))
        put(f"xqg{l}", cols(inp["xa_q_norm_g"][l]))
        put(f"xkg{l}", cols(inp["xa_k_norm_g"][l]))
        if l % 2 == 0:
            e = l // 2
            cw = np.asarray(inp["ev_conv_w"][e], np.float32)
            put(f"convw{l}", np.concatenate([cols(cw[j]) for j in range(4)], axis=1))
            put(f"convb{l}", cols(inp["ev_conv_b"][e]))
            put(f"gab{l}", cols(inp["ev_gate_a_b"][e]))
            put(f"gxb{l}", cols(inp["ev_gate_x_b"][e]))
            put(f"lam{l}", cols(inp["ev_lambda"][e]))
            put(f"qg{l}", cols(inp["ev_q_norm_g"][e]))
            put(f"kg{l}", cols(inp["ev_k_norm_g"][e]))
        else:
            put(f"scale{l}", cols(inp["od_scale"][l // 2]))
    put("memg", cols(inp["mem_norm_g"]))
    put("flag", np.full((128, 1), float(half), np.float32))
    put("ctxbias", np.full((128, 1), 0.0 if half else NEG, np.float32))
    ic = np.zeros((8, 16), np.float32)
    for c in range(8):
        w = 2 ** (c // 2 + 1)
        for t in range(16):
            ic[c, t] = 1.0 / w if half else 1.0 / min(t + 1, w)
    put("invcnt", np.broadcast_to(ic.reshape(1, 128), (128, 128)))
    return P


def make_mask():
    ki = np.arange(128)[:, None]
    x = np.arange(MASKW)[None, :]
    d = x - ki - 384
    m = ((d >= 0) & (d <= 128)).astype(np.float32)
    m += ((d >= 0) & (d % 4 == 0) & (d <= 512)).astype(np.float32)
    m += ((d >= 0) & (d % 16 == 0) & (d <= 2048)).astype(np.float32)
    return m


class Trk:
    CH = 4000

    def __init__(self, nc, stack):
        self.nc = nc
        self.stack = stack
        self.eng = dict(pe=nc.tensor, act=nc.scalar, dve=nc.vector, pool=nc.gpsimd, sp=nc.sync)
        self.cnt = {e: 0 for e in self.eng}
        self.sems = {e: [] for e in self.eng}
        self.seen = {e: {f: 0 for f in self.eng} for e in self.eng}
        self.snap = {e: {} for e in self.eng}
        self.dsem = {}
        self.dcnt = {}
        self.dseen = {e: {} for e in self.eng}
        self.lastw = {}
        self.readers = {}
        self.dma_tokens = set()
        self.nops = 0
        self.stream = {e: [] for e in self.eng}

    def _sem(self, e, seq):
        k = (seq - 1) // self.CH
        while len(self.sems[e]) <= k:
            self.sems[e].append(
                self.stack.enter_context(self.nc.semaphore(f"s_{e}_{len(self.sems[e])}")))
        return self.sems[e][k], (seq - 1) % self.CH + 1

    def _wait(self, e, h):
        if h[0] == "eng":
            _, f, s = h
            if f == e and (e == "pe" or not SAME_ENGINE_SYNC):
                return
            if self.seen[e][f] >= s:
                return
            sem, v = self._sem(f, s)
            self.eng[e].wait_ge(sem, v)
            self.stream[e].append(("w", ("eng", f, (s - 1) // self.CH), v))
            self.seen[e][f] = s
            sn = self.snap[f].get(s)
            if sn and f != e:
                for g, v2 in sn.items():
                    if g != e and v2 > self.seen[e][g]:
                        self.seen[e][g] = v2
        else:
            _, key, v = h
            if self.dseen[e].get(key, 0) >= v:
                return
            self.eng[e].wait_ge(self.dsem[key], v)
            self.stream[e].append(("w", ("dma", key), v))
            self.dseen[e][key] = v

    def _deps(self, e, r, w):
        hs = []
        for t in r:
            h = self.lastw.get(t)
            if h:
                hs.append(h)
        for t in w:
            h = self.lastw.get(t)
            if h:
                hs.append(h)
            hs.extend(self.readers.get(t, ()))
        best = {}
        for h in hs:
            k = (h[0], h[1])
            if k not in best or h[2] > best[k][2]:
                best[k] = h
        for h in best.values():
            self._wait(e, h)

    def _commit(self, h, r, w):
        for t in r:
            self.readers.setdefault(t, []).append(h)
        for t in w:
            self.lastw[t] = h
            self.readers[t] = []

    def op(self, e, fn, r=(), w=()):
        self._deps(e, r, w)
        ins = fn(self.eng[e])
        self.cnt[e] += 1
        s = self.cnt[e]
        sem, v = self._sem(e, s)
        ins.then_inc(sem, 1)
        self.stream[e].append(("i", ("eng", e, (s - 1) // self.CH), 1))
        self.snap[e][s] = dict(self.seen[e])
        self._commit(("eng", e, s), r, w)
        self.nops += 1
        return ins

    def mmg(self, out, pairs, r=(), w=()):
        self._deps("pe", r, w)
        n = len(pairs)
        ins = None
        for i, (lhsT, rhs) in enumerate(pairs):
            ins = self.nc.tensor.matmul(out, lhsT, rhs, start=(i == 0), stop=(i == n - 1))
        self.cnt["pe"] += 1
        s = self.cnt["pe"]
        sem, v = self._sem("pe", s)
        ins.then_inc(sem, 1)
        self.stream["pe"].append(("i", ("eng", "pe", (s - 1) // self.CH), 1))
        self.snap["pe"][s] = dict(self.seen["pe"])
        self._commit(("eng", "pe", s), r, w)
        self.nops += n

    def mm1(self, out, lhsT, rhs, start, stop, r=(), w=()):
        self._deps("pe", r, w)
        ins = self.nc.tensor.matmul(out, lhsT, rhs, start=start, stop=stop)
        self.cnt["pe"] += 1
        s = self.cnt["pe"]
        sem, v = self._sem("pe", s)
        ins.then_inc(sem, 1)
        self.stream["pe"].append(("i", ("eng", "pe", (s - 1) // self.CH), 1))
        self.snap["pe"][s] = dict(self.seen["pe"])
        self._commit(("eng", "pe", s), r, w)
        self.nops += 1

    def dma(self, q, key, out, in_, r=(), w=()):
        if key not in self.dsem:
            self.dsem[key] = self.stack.enter_context(self.nc.semaphore(f"d_{key}"))
            self.dcnt[key] = 0
        self._deps(q, r, w)
        self.eng[q].dma_start(out=out, in_=in_).then_inc(self.dsem[key], 16)
        self.stream[q].append(("i", ("dma", key), 16))
        self.dcnt[key] += 16
        h = ("dma", key, self.dcnt[key])
        self._commit(h, r, w)
        self.dma_tokens.update(r)
        self.dma_tokens.update(w)
        return h

    def barrier(self):
        es = ["pe", "act", "dve"]
        for e in es:
            for f in es:
                if f != e and self.cnt[f] > self.seen[e][f]:
                    self._wait(e, ("eng", f, self.cnt[f]))
        for t in list(self.lastw):
            if t in self.dma_tokens:
                continue
            del self.lastw[t]
            self.readers.pop(t, None)
        for t in list(self.readers):
            if t not in self.dma_tokens and t not in self.lastw:
                del self.readers[t]

    def check_deadlock(self):
        val = {}
        ptr = {e: 0 for e in self.eng}
        prog = True
        while prog:
            prog = False
            for e in self.eng:
                st = self.stream[e]
                while ptr[e] < len(st):
                    k, key, v = st[ptr[e]]
                    if k == "w":
                        if val.get(key, 0) < v:
                            break
                    else:
                        val[key] = val.get(key, 0) + v
                    ptr[e] += 1
                    prog = True
        stuck = {e: (ptr[e], len(self.stream[e]), self.stream[e][ptr[e]], val.get(self.stream[e][ptr[e]][1], 0))
                 for e in self.eng if ptr[e] < len(self.stream[e])}
        assert not stuck, ("DEADLOCK", stuck)

    def wait_all(self, e):
        for t, h in self.lastw.items():
            self._wait(e, h)
        for t, hs in self.readers.items():
            for h in hs:
                self._wait(e, h)


def build(layers):
    nc = bass.Bass(target_bir_lowering=False)
    stack = ExitStack()

    def din(name, shape):
        return nc.dram_tensor(name, list(shape), F32, kind="ExternalInput").ap()

    x_d = din("x", (128, 8, T))
    xh_d = din("xh", (128, 8, HALO))
    mem_d = din("mem", (128, 8, 256))
    par_d = din("params", (128, NP))
    mask_d = din("mask", (128, MASKW))
    w_in_d = din("ev_w_in", (2, 1024, 2560))
    gaw_d = din("ev_gate_a_w", (2, 4, 128, 128))
    gxw_d = din("ev_gate_x_w", (2, 4, 128, 128))
    w_out_d = din("ev_w_out", (2, 1024, 1024))
    poolw_d = din("od_pool_w", (2, 4, 256, 256))
    xwq_d = din("xa_w_q", (4, 1024, 1024))
    xwkv_d = din("xa_w_kv", (4, 1024, 2048))
    xwo_d = din("xa_w_o", (4, 1024, 1024))
    w1_d = din("mlp_w1", (4, 1024, 4096))
    w2_d = din("mlp_w2", (4, 4096, 1024))
    y_d = nc.dram_tensor("y", [128, 8, T], F32, kind="ExternalOutput").ap()

    ncc_kv = sum(1 for l in layers if l % 2 == 0)
    b0 = [nc.dram_tensor(f"b0_{i}", [128, 128], F32) for i in range(len(layers))]
    g0 = [nc.dram_tensor(f"g0_{i}", [256, 128], F32) for i in range(len(layers))]
    b2 = [[nc.dram_tensor(f"b2_{i}_{h}", [256, 2048], BF16) for h in range(4)] for i in range(ncc_kv)]
    g2 = [[nc.dram_tensor(f"g2_{i}_{h}", [512, 2048], BF16) for h in range(4)] for i in range(ncc_kv)]
    b3 = [nc.dram_tensor(f"b3_{i}", [128, 4], F32) for i in range(ncc_kv)]
    g3 = [nc.dram_tensor(f"g3_{i}", [256, 4], F32) for i in range(ncc_kv)]

    def sb(name, shape, dt):
        return stack.enter_context(nc.sbuf_tensor(name, list(shape), dt))

    xres = sb("xres", (128, 8, T), F32)
    par = sb("par", (128, NP), F32)
    xhalo = sb("xhalo", (128, 8, HALO), F32)
    hprev = sb("hprev", (128, 8, HALO), F32)
    tf = sb("tf", (128, 8, 528), F32)
    sm = sb("sm", (128, 96), F32)
    hb = sb("hb", (128, 8 * HW), BF16)
    qk = sb("qk", (128, 8, T), BF16)
    vv = sb("vv", (128, 16, 512), BF16)
    NSLOT = 3
    wsl = sb("wsl", (128, NSLOT, 4096), BF16)
    bt = sb("bt", (128, 9, 512), BF16)
    yb = bt[:, 2:6, :].rearrange("p a b -> p (a b)")
    maskt = sb("maskt", (128, MASKW), BF16)
    memn = sb("memn", (128, 8, 256), BF16)
    ones = sb("ones", (128, 4, 128), BF16)
    ps = [stack.enter_context(nc.psum_tensor(f"ps{i}", [128, 512], F32)) for i in range(8)]

    hbuf = hb[:, :].rearrange("p (c t) -> p c t", c=8)
    kctx = hb[:, 0:8192].rearrange("p (h t) -> p h t", h=4)
    vctx = hb[:, 8192:16384].rearrange("p (b f) -> p b f", b=16)

    tk = Trk(nc, stack)
    op, mmg, dma = tk.op, tk.mmg, tk.dma

    def pcol(name, i=0, n=1):
        o, w = POFF[name]
        return par[:, o + i:o + i + n]

    class WStream:
        def __init__(self):
            self.blocks = []
            self.issued = 0
            self.used = 0
            self.released = set()
            self.cur = {}

        def add(self, tag, parts):
            self.blocks.append((tag, parts))

        def _issue(self, i):
            tag, parts = self.blocks[i]
            s = i % NSLOT
            for (lo, shape, src) in parts:
                n = int(np.prod(shape))
                dst = wsl[:, s, lo:lo + n]
                if len(shape) == 1:
                    pass
                elif len(shape) == 2:
                    dst = dst.rearrange("p (a b) -> p a b", a=shape[0])
                elif len(shape) == 3:
                    dst = dst.rearrange("p (a b c) -> p a b c", a=shape[0], b=shape[1])
                dma("pool", f"w{s}", dst, src, w=[("w", s)])

        def _pump(self):
            while self.issued < len(self.blocks) and (
                    self.issued < NSLOT or (self.issued - NSLOT) in self.released):
                self._issue(self.issued)
                self.issued += 1

        def next(self, tag):
            i = self.used
            assert self.blocks[i][0] == tag, (self.blocks[i][0], tag)
            self._pump()
            assert self.issued > i, ("weight slot not released", tag)
            self.used += 1
            s = i % NSLOT
            self.cur[s] = i
            return s, wsl[:, s, :]

        def release(self, s):
            self.released.add(self.cur[s])
            self._pump()

    ws = WStream()

    def wview(s, a, b, lo=0):
        return wsl[:, s, lo:lo + a * b].rearrange("p (a b) -> p a b", a=a)

    def cols_block(wd2, c0, n=512):
        return wd2.rearrange("(kc p) n -> p kc n", p=128)[:, :, c0:c0 + n]

    def rows_block(wd2, r0, nchunks):
        return wd2[r0:r0 + nchunks * 128, :].rearrange("(j p) n -> p j n", p=128)

    def schedule_layer(l):
        if l % 2 == 0:
            e = l // 2
            w = w_in_d[e]
            for c in range(4):
                ws.add(f"rg{l}_{c}", [(0, (8, 128), cols_block(w, 1536 + c * 128, 128)),
                                      (1024, (8, 128), cols_block(w, 2048 + c * 128, 128)),
                                      (2048, (128,), gaw_d[e, c]),
                                      (2176, (128,), gxw_d[e, c])])
            ws.add(f"v{l}", [(0, (8, 512), cols_block(w, 1024))])
            ws.add(f"woy{l}", [(0, (4, 1024), rows_block(w_out_d[e], 512, 4))])
            ws.add(f"k{l}", [(0, (8, 512), cols_block(w, 512))])
            ws.add(f"q{l}", [(0, (8, 512), cols_block(w, 0))])
            ws.add(f"woa{l}", [(0, (4, 1024), rows_block(w_out_d[e], 0, 4))])
        else:
            o = l // 2
            ws.add(f"pool{l}", [(i * 1024, (4, 256),
                                 poolw_d[o][:, i * 128:(i + 1) * 128, :].rearrange("g p n -> p g n"))
                                for i in range(2)])
        kv = xwkv_d[l]
        for j in range(2):
            ws.add(f"xk{l}_{j}", [(0, (8, 512), cols_block(kv, j * 512))])
        for j in range(2):
            ws.add(f"xv{l}_{j}", [(0, (8, 512), cols_block(kv, 1024 + j * 512))])
        for j in range(2):
            ws.add(f"xq{l}_{j}", [(0, (8, 512), cols_block(xwq_d[l], j * 512))])
        for j in range(2):
            ws.add(f"xo{l}_{j}", [(0, (4, 1024), rows_block(xwo_d[l], j * 512, 4))])
        for g in range(8):
            ws.add(f"w1_{l}_{g}", [(0, (8, 512), cols_block(w1_d[l], g * 512))])
            ws.add(f"w2_{l}_{g}", [(0, (4, 1024), rows_block(w2_d[l], g * 512, 4))])

    for l in layers:
        schedule_layer(l)

    def X(c, tt):
        return ("x", c, tt)

    def HB(c, tt):
        return ("hb", c, tt)

    def QK(c, tt):
        return ("qk", c, tt)

    ALLHB = [("hb", c, tt) for c in range(8) for tt in range(4)] + [("hbh",)]

    def act_fn(out, in_, func, **kw):
        return lambda e: e.activation(out=out, in_=in_, func=func, **kw)

    def emit_rstd(psum_ap, dst, r, w):
        op("act", act_fn(dst, psum_ap, AF.Ln, bias=EPSC), r=list(r) + [("epsc",)], w=w)
        op("act", act_fn(dst, dst, AF.Exp, scale=-0.5), r=w, w=w)

    def emit_recip(psum_ap, dst, r, w):
        op("act", act_fn(dst, psum_ap, AF.Ln), r=list(r), w=w)
        op("act", act_fn(dst, dst, AF.Exp, scale=-1.0), r=w, w=w)

    def stt(out, in0, scalar, in1, op0, op1):
        return lambda e: e.scalar_tensor_tensor(out, in0, scalar, in1, op0, op1)

    def tt_(out, in0, in1, opx):
        return lambda e: e.tensor_tensor(out, in0, in1, opx)

    def ts_(out, in0, s1, s2, op0, op1=None):
        if op1 is None:
            return lambda e: e.tensor_scalar(out, in0, s1, None, op0)
        return lambda e: e.tensor_scalar(out, in0, s1, s2, op0, op1)

    EPSC = sm[:, 1:2]
    ONE_D, ONE_128, ONE_256, ONE_1 = (ones[:, i, :] for i in range(4))
    sq = [bt[:, 0, :], bt[:, 1, :]]
    eb = [bt[:, 2, :], bt[:, 3, :]]
    pb = [bt[:, 4, :], bt[:, 5, :]]
    SQ = [("sq", 0), ("sq", 1)]
    EB = [("eb", 0), ("eb", 1)]
    PB = [("pb", 0), ("pb", 1)]
    PS = [("ps", i) for i in range(8)]
    TF = [("tf", i) for i in range(8)]

    for tt in range(NT):
        dma("sp", f"xin{tt}", xres[:, :, tile_sl(tt)], x_d[:, :, tile_sl(tt)],
            w=[X(c, tt) for c in range(8)])
    dma("sp", "par", par[:, :], par_d[:, :], w=[("par",)])
    dma("sp", "xh", xhalo[:, :, :], xh_d[:, :, :], w=[("xhalo",)])
    memf = tf[:, 0:8, 0:256]
    dma("sp", "mem", memf, mem_d[:, :, :], w=TF[0:8])
    dma("pool", "mask", maskt[:, :], mask_d[:, :], w=[("mask",)])
    for i, val in enumerate([1.0 / 1024, 1.0 / 128, 1.0 / 256, 1.0]):
        op("dve", lambda e, i=i, val=val: e.memset(ones[:, i, :], val), w=[("ones",)])
    op("dve", lambda e: e.memset(bt[:, 6, :], 0.0), w=[("sm0",)])
    ZT = bt[:, 6, :]
    op("dve", lambda e: e.memset(sm[:, 1:2], EPS), w=[("epsc",)])

    sqm = qk[:, 0, 0:2048].rearrange("p (c m) -> p c m", c=8)
    for c in range(8):
        op("act", act_fn(sqm[:, c, :], memf[:, c, :], AF.Square), r=TF[0:8], w=[QK(0, 0)])
    mmg(ps[0][:, 0:256], [(ONE_D, sqm[:, c, :]) for c in range(8)],
        r=[QK(0, 0), ("ones",)], w=[PS[0]])
    emit_rstd(ps[0][:, 0:256], tf[:, 4, 256:512], r=[PS[0]], w=[("mrs",)])
    for c in range(8):
        op("dve", stt(memn[:, c, :], memf[:, c, :], pcol("memg", c), tf[:, 4, 256:512],
                      ALU.mult, ALU.mult),
           r=TF[0:8] + [("mrs",), ("par",)], w=[("memn",)])
    tk.barrier()

    def halo_prep(gname, to_hbuf):
        op("dve", ts_(xhalo[:, :, :], xhalo[:, :, :], pcol("flag"), None, ALU.mult),
           r=[("par",)], w=[("xhalo",)])
        sqh = bt[:, 0, 0:128].rearrange("p (c t) -> p c t", c=8)
        op("act", act_fn(sqh, xhalo[:, :, :], AF.Square), r=[("xhalo",)], w=[SQ[0]])
        mmg(ps[7][:, 0:HALO], [(ONE_D, sqh[:, c, :]) for c in range(8)],
            r=[SQ[0], ("ones",)], w=[PS[7]])
        rh = sm[:, 16:32]
        emit_rstd(ps[7][:, 0:HALO], rh, r=[PS[7]], w=[("rh",)])
        for c in range(8):
            dst = hbuf[:, c, 0:HALO] if to_hbuf else hprev[:, c, :]
            op("dve", stt(dst, xhalo[:, c, :], pcol(gname, c), rh, ALU.mult, ALU.mult),
               r=[("xhalo",), ("rh",), ("par",)], w=[("hbh",)] if to_hbuf else [("hprev", c)])

    def norm_tile_rstd(tt, dst_tf):
        for c in range(8):
            op("act", act_fn(sq[c % 2], xres[:, c, tile_sl(tt)], AF.Square),
               r=[X(c, tt)], w=[SQ[c % 2]])
            tk.mm1(ps[7 - tt % 2][:, :], ONE_D, sq[c % 2], c == 0, c == 7, r=[SQ[c % 2], ("ones",)],
                   w=[PS[7 - tt % 2]])
        emit_rstd(ps[7 - tt % 2][:, :], tf[:, dst_tf, 0:512], r=[PS[7 - tt % 2]], w=[TF[dst_tf]])

    def norm_to_hbuf(gname):
        for tt in range(NT):
            ri = 7 - tt % 2
            norm_tile_rstd(tt, ri)
            for c in range(8):
                op("dve", stt(hbuf[:, c, HALO + tt * TT:HALO + (tt + 1) * TT],
                              xres[:, c, tile_sl(tt)], pcol(gname, c), tf[:, ri, 0:512],
                              ALU.mult, ALU.mult),
                   r=[X(c, tt), TF[ri], ("par",)], w=[HB(c, tt)])

    def hslice(c, tt):
        return hbuf[:, c, HALO + tt * TT:HALO + (tt + 1) * TT]

    def add_to_x(m, tt, psum_ap, psi):
        op("dve", tt_(xres[:, m, tile_sl(tt)], psum_ap, xres[:, m, tile_sl(tt)], ALU.add),
           r=[PS[psi]], w=[X(m, tt)])

    cc_count = [0]

    def collective(src_d, dst_d, wait_handles):
        for h in wait_handles:
            tk._wait("pool", h)
        sem = stack.enter_context(nc.semaphore(f"cc{cc_count[0]}"))
        cc_count[0] += 1
        nc.gpsimd.collective_compute(
            "AllGather", ALU.bypass, replica_groups=GROUPS,
            ins=[src_d.ap().opt()], outs=[dst_d.ap().opt()]).then_inc(sem)
        tk.stream["pool"].append(("i", ("cc", id(sem)), 1))
        return sem

    def proj_headnorm(wt, wtok, gcol, base):
        its = [(hd, tt) for hd in range(4) for tt in range(NT)]

        def front(i):
            hd, tt = its[i]
            pa = i % 3
            mmg(ps[pa][:, :], [(wt[:, kc, hd * 128:(hd + 1) * 128], hslice(kc, tt)) for kc in range(8)],
                r=[HB(kc, tt) for kc in range(8)] + [wtok], w=[PS[pa]])
            op("act", act_fn(sq[i % 2], ps[pa][:, :], AF.Square), r=[PS[pa]], w=[SQ[i % 2]])

        def back(i):
            hd, tt = its[i]
            pa, pn, ti = i % 3, 3 + i % 2, i % 2
            mmg(ps[pn][:, :], [(ONE_128, sq[i % 2])], r=[SQ[i % 2], ("ones",)], w=[PS[pn]])
            emit_rstd(ps[pn][:, :], tf[:, ti, 0:512], r=[PS[pn]], w=[TF[ti]])
            op("dve", stt(qk[:, base + hd, tile_sl(tt)], ps[pa][:, :], gcol, tf[:, ti, 0:512],
                          ALU.mult, ALU.mult), r=[PS[pa], TF[ti], ("par",)], w=[QK(base + hd, tt)])

        for i in range(len(its) + 1):
            if i < len(its):
                front(i)
            if i >= 1:
                back(i - 1)

    def even_mixer(l, li, ei):
        e = l // 2
        halo_prep(f"mixg{l}", True)
        norm_to_hbuf(f"mixg{l}")
        sp8 = sm[:, 4:8]
        op("act", act_fn(sp8, pcol(f"lam{l}", 0, 4), AF.Exp, scale=-1.0), r=[("par",)], w=[("sp8",)])
        op("act", act_fn(sp8, sp8, AF.Ln, bias=1.0), r=[("sp8",)], w=[("sp8",)])
        op("dve", ts_(sp8, sp8, -8.0, None, ALU.mult), r=[("sp8",)], w=[("sp8",)])

        hfin = sm[:, 8:12]
        hA = sm[:, 32:36]
        tk.barrier()
        vf = vv[:, :, :].rearrange("p a b -> p (a b)").bitcast(F32)
        VR = lambda r: vf[:, r * 528:(r + 1) * 528]
        CH = [
            dict(xrt=[tf[:, 0, 0:515], tf[:, 1, 0:515]], XT=[TF[0], TF[1]],
                 gx=tf[:, 2, 0:512], GX=TF[2], g3=tf[:, 3, 0:512], G3=TF[3],
                 ra=tf[:, 4, 0:512], RA=TF[4], ri=tf[:, 5, 0:512], RI=TF[5],
                 xc=tf[:, 6, 0:512], XC=TF[6], pxr=0, pgt=1, pga=4, pgx=5, sq=0,
                 hc=sm[:, 12:13], HC=("hc", 0), pc=sm[:, 13:14], PC=("pc", 0),
                 xrh=sm[:, 40:56], XRH=("xrh", 0)),
            dict(xrt=[VR(0)[:, 0:515], VR(1)[:, 0:515]], XT=[("vf", 0), ("vf", 1)],
                 gx=VR(2)[:, 0:512], GX=("vf", 2), g3=VR(3)[:, 0:512], G3=("vf", 3),
                 ra=VR(4)[:, 0:512], RA=("vf", 4), ri=VR(5)[:, 0:512], RI=("vf", 5),
                 xc=VR(6)[:, 0:512], XC=("vf", 6), pxr=2, pgt=3, pga=6, pgx=7, sq=1,
                 hc=sm[:, 14:15], HC=("hc", 1), pc=sm[:, 15:16], PC=("pc", 1),
                 xrh=sm[:, 64:80], XRH=("xrh", 1)),
        ]

        def chain_ops(q, c, tt, srg):
            B = CH[q]
            p = tt % 2
            xt, XTp = B["xrt"][p], B["XT"][p]
            xprev, XTq = B["xrt"][1 - p], B["XT"][1 - p]
            gx, g3_, ra, ri, xc = B["gx"], B["g3"], B["ra"], B["ri"], B["xc"]
            GX, G3, RA, RI, XC = B["GX"], B["G3"], B["RA"], B["RI"], B["XC"]
            hc, pc, HC, PC = B["hc"], B["pc"], B["HC"], B["PC"]
            sqb, SQB = sq[B["sq"]], SQ[B["sq"]]
            pxr, pgt, pga, pgx = B["pxr"], B["pgt"], B["pga"], B["pgx"]
            wxr = wview(srg, 8, 128, 0)
            wgt = wview(srg, 8, 128, 1024)
            wga = wsl[:, srg, 2048:2176]
            wgx = wsl[:, srg, 2176:2304]
            W = ("w", srg)
            cw = lambda j: pcol(f"convw{l}", j * 4 + c)
            hbr = [HB(kc, tt) for kc in range(8)]
            L = []
            A = L.append
            if tt == 0:
                A(lambda: mmg(ps[pxr][:, 0:16], [(wxr[:, kc, :], hbuf[:, kc, 0:HALO]) for kc in range(8)],
                              r=[("hbh",), W], w=[PS[pxr]]))
                A(lambda: op("act", act_fn(B["xrh"], ps[pxr][:, 0:16], AF.Copy), r=[PS[pxr]], w=[B["XRH"]]))
                A(lambda: op("dve", lambda e_: e_.memset(hc, 0.0), w=[HC]))
                A(lambda: op("dve", lambda e_: e_.memset(pc, 1.0), w=[PC]))
            A(lambda: mmg(ps[pxr][:, :], [(wxr[:, kc, :], hslice(kc, tt)) for kc in range(8)], r=hbr + [W], w=[PS[pxr]]))
            A(lambda: mmg(ps[pgt][:, :], [(wgt[:, kc, :], hslice(kc, tt)) for kc in range(8)], r=hbr + [W], w=[PS[pgt]]))
            if tt == 0:
                A(lambda: op("dve", lambda e_: e_.tensor_copy(xt[:, 0:3], B["xrh"][:, 13:16]), r=[B["XRH"]], w=[XTp]))
            else:
                A(lambda: op("dve", lambda e_: e_.tensor_copy(xt[:, 0:3], xprev[:, 512:515]), r=[XTq], w=[XTp]))
            A(lambda: op("act", act_fn(xt[:, 3:515], ps[pxr][:, :], AF.Copy), r=[PS[pxr]], w=[XTp]))
            A(lambda: op("act", act_fn(gx, ps[pgt][:, :], AF.Copy), r=[PS[pgt]], w=[GX]))
            A(lambda: op("act", act_fn(g3_, gx, AF.Square), r=[GX], w=[G3]))
            A(lambda: op("act", act_fn(xc, xt[:, 3:515], AF.Identity, scale=cw(3), bias=pcol(f"convb{l}", c)),
                         r=[XTp, ("par",)], w=[XC]))
            A(lambda: op("dve", ts_(g3_, g3_, 0.044715, 1.0, ALU.mult, ALU.add), r=[G3], w=[G3]))
            A(lambda: op("dve", stt(xc, xt[:, 0:512], cw(0), xc, ALU.mult, ALU.add), r=[XTp, XC, ("par",)], w=[XC]))
            A(lambda: op("dve", tt_(g3_, g3_, gx, ALU.mult), r=[GX, G3], w=[G3]))
            A(lambda: op("dve", stt(xc, xt[:, 1:513], cw(1), xc, ALU.mult, ALU.add), r=[XTp, XC, ("par",)], w=[XC]))
            A(lambda: op("act", act_fn(g3_, g3_, AF.Sigmoid, scale=1.5957691216057308), r=[G3], w=[G3]))
            A(lambda: op("dve", stt(xc, xt[:, 2:514], cw(2), xc, ALU.mult, ALU.add), r=[XTp, XC, ("par",)], w=[XC]))
            A(lambda: op("act", act_fn(sqb, xc, AF.Copy), r=[XC], w=[SQB]))
            A(lambda: op("dve", tt_(gx, gx, g3_, ALU.mult), r=[GX, G3], w=[GX]))
            A(lambda: mmg(ps[pga][:, :], [(wga, sqb)], r=[SQB, W], w=[PS[pga]]))
            A(lambda: mmg(ps[pgx][:, :], [(wgx, sqb)], r=[SQB, W], w=[PS[pgx]]))
            A(lambda: op("act", act_fn(ra, ps[pga][:, :], AF.Sigmoid, bias=pcol(f"gab{l}", c)),
                         r=[PS[pga], ("par",)], w=[RA]))
            A(lambda: op("act", act_fn(ri, ps[pgx][:, :], AF.Sigmoid, bias=pcol(f"gxb{l}", c)),
                         r=[PS[pgx], ("par",)], w=[RI]))
            A(lambda: op("act", act_fn(ra, ra, AF.Exp, scale=sp8[:, c:c + 1]), r=[RA, ("sp8",)], w=[RA]))
            A(lambda: op("dve", tt_(ri, ri, xc, ALU.mult), r=[RI, XC], w=[RI]))
            A(lambda: op("dve", tt_(g3_, ra, ra, ALU.mult), r=[RA], w=[G3]))
            A(lambda: op("act", act_fn(g3_, g3_, AF.Sqrt, scale=-1.0, bias=1.0), r=[G3], w=[G3]))
            A(lambda: op("dve", lambda e_: e_.tensor_tensor_scan(xc, ra, ZT, pc, ALU.mult, ALU.add),
                         r=[RA, RI, PC, ("sm0",)], w=[XC]))
            A(lambda: op("dve", tt_(ri, ri, g3_, ALU.mult), r=[RI, G3], w=[RI]))
            A(lambda: op("dve", lambda e_: e_.tensor_copy(pc, xc[:, 511:512]), r=[XC], w=[PC]))
            A(lambda: op("dve", lambda e_: e_.tensor_tensor_scan(g3_, ra, ri, hc, ALU.mult, ALU.add),
                         r=[RA, RI, HC], w=[G3]))
            A(lambda: op("dve", tt_(qk[:, c, tile_sl(tt)], xc, gx, ALU.mult), r=[XC, GX], w=[QK(c, tt)]))
            A(lambda: op("dve", lambda e_: e_.tensor_copy(hc, g3_[:, 511:512]), r=[G3], w=[HC]))
            A(lambda: op("dve", tt_(qk[:, 4 + c, tile_sl(tt)], g3_, gx, ALU.mult), r=[G3, GX], w=[QK(4 + c, tt)]))
            if tt == NT - 1:
                A(lambda: op("dve", lambda e_: e_.tensor_copy(hfin[:, c:c + 1], hc), r=[HC], w=[("hfin",)]))
            return L

        for pair in ((0, 1), (2, 3)):
            srgs = [ws.next(f"rg{l}_{c}")[0] for c in pair]
            for tt in range(NT):
                lists = [chain_ops(q, c, tt, srgs[q]) for q, c in enumerate(pair)]
                for k in range(max(len(x_) for x_ in lists)):
                    for x_ in lists:
                        if k < len(x_):
                            x_[k]()
            for sg in srgs:
                ws.release(sg)
        h3 = dma("pool", f"b3_{ei}", b3[ei].ap(), hfin, r=[("hfin",)])
        cc3 = collective(b3[ei], g3[ei], [h3])
        tk.barrier()

        sv, _ = ws.next(f"v{l}")
        wv = wview(sv, 8, 512)
        for blk in range(16):
            pa = 4 + blk % 2
            tt = blk // 4
            mmg(ps[pa][:, :],
                [(hbuf[:, kc, HALO + blk * 128:HALO + (blk + 1) * 128], wv[:, kc, :]) for kc in range(8)],
                r=[HB(kc, tt) for kc in range(8)] + [("w", sv)], w=[PS[pa]])
            op("act", act_fn(vv[:, blk, :], ps[pa][:, :], AF.Copy), r=[PS[pa]], w=[("vv", blk)])
        ws.release(sv)

        tk.eng["pool"].wait_ge(cc3, 1)
        tk.stream["pool"].append(("w", ("cc", id(cc3)), 1))
        dma("pool", f"g3_{ei}", hA, g3[ei].ap()[0:128, :], w=[("hA",)])
        op("dve", ts_(hA, hA, pcol("flag"), None, ALU.mult), r=[("hA",), ("par",)], w=[("hA",)])
        for c in range(4):
            for tt in range(NT):
                op("dve", stt(qk[:, 4 + c, tile_sl(tt)], qk[:, c, tile_sl(tt)], hA[:, c:c + 1],
                              qk[:, 4 + c, tile_sl(tt)], ALU.mult, ALU.add),
                   r=[QK(c, tt), QK(4 + c, tt), ("hA",)], w=[QK(4 + c, tt)])
        swy, _ = ws.next(f"woy{l}")
        woy = wview(swy, 4, 1024)
        for m in range(8):
            for tt in range(NT):
                pi = 6 + (m * NT + tt) % 2
                mmg(ps[pi][:, :], [(woy[:, c, m * 128:(m + 1) * 128], qk[:, 4 + c, tile_sl(tt)]) for c in range(4)],
                    r=[QK(4 + c, tt) for c in range(4)] + [("w", swy)], w=[PS[pi]])
                add_to_x(m, tt, ps[pi][:, :], pi)
        ws.release(swy)

        sk, _ = ws.next(f"k{l}")
        wk = wview(sk, 8, 512)
        proj_headnorm(wk, ("w", sk), pcol(f"kg{l}"), 4)
        ws.release(sk)
        cc2 = []
        for hd in range(4):
            h2a = dma("pool", f"b2k_{ei}_{hd}", b2[ei][hd].ap()[0:128, :], qk[:, 4 + hd, :],
                      r=[QK(4 + hd, tt) for tt in range(4)])
            h2b = dma("pool", f"b2v_{ei}_{hd}",
                      b2[ei][hd].ap()[128:256, :].rearrange("p (b f) -> p b f", b=16),
                      vv[:, :, hd * 128:(hd + 1) * 128], r=[("vv", blk) for blk in range(16)])
            cc2.append(collective(b2[ei][hd], g2[ei][hd], [h2a, h2b]))

        sq_, _ = ws.next(f"q{l}")
        wq = wview(sq_, 8, 512)
        proj_headnorm(wq, ("w", sq_), pcol(f"qg{l}"), 0)
        ws.release(sq_)
        tk.barrier()
        for f in ("pe", "act", "dve"):
            tk._wait("pool", ("eng", f, tk.cnt[f]))
        for hd in range(4):
            tk.eng["pool"].wait_ge(cc2[hd], 1)
            tk.stream["pool"].append(("w", ("cc", id(cc2[hd])), 1))
            dma("pool", f"ctxk_{ei}_{hd}", kctx[:, hd, :], g2[ei][hd].ap()[0:128, :],
                w=[("kctx",)] + ALLHB)
            dma("pool", f"ctxv_{ei}_{hd}", vctx[:, :, hd * 128:(hd + 1) * 128],
                g2[ei][hd].ap()[128:256, :].rearrange("p (b f) -> p b f", b=16),
                w=[("vctx",)] + ALLHB)

        scale = 128.0 ** -0.5
        ebs = [bt[:, 2, :], bt[:, 3, :], bt[:, 0, :], bt[:, 7, :]]
        pbs = [bt[:, 4, :], bt[:, 5, :], bt[:, 1, :], bt[:, 8, :]]
        EBS = [("eb", 0), ("eb", 1), ("sq", 0), ("bt7",)]
        PBS = [("pb", 0), ("pb", 1), ("sq", 1), ("bt8",)]
        SBK = [0, 1, 2, 7]
        items = []
        gi = 0
        for hd in range(4):
            for qt in range(NT):
                blocks = [("ctx", kb) for kb in range(4 * qt, 16)] + [("own", kb) for kb in range(0, 4 * qt + 4)]
                for bi, (kind, kb) in enumerate(blocks):
                    items.append((hd, qt, gi, bi, len(blocks), kind, kb))
                gi += 1
        LA = 3

        def att_front(idx):
            hd, qt, g, bi, nb, kind, kb = items[idx]
            si, bi3 = SBK[idx % 4], idx % 4
            if kind == "ctx":
                kT = kctx[:, hd, kb * 128:(kb + 1) * 128]
                rk = [("kctx",)]
                d0 = 512 * qt + 2048 - 128 * kb
                bias = pcol("ctxbias")
            else:
                kT = qk[:, 4 + hd, kb * 128:(kb + 1) * 128]
                rk = [QK(4 + hd, kb // 4)]
                d0 = 512 * qt - 128 * kb
                bias = 0.0
            off = d0 + 384
            mmg(ps[si][:, :], [(kT, qk[:, hd, tile_sl(qt)])], r=rk + [QK(hd, qt)], w=[PS[si]])
            op("act", act_fn(ebs[bi3], ps[si][:, :], AF.Exp, scale=scale, bias=bias),
               r=[PS[si], ("par",)], w=[EBS[bi3]])
            op("dve", tt_(pbs[bi3], ebs[bi3], maskt[:, off:off + 512], ALU.mult),
               r=[EBS[bi3], ("mask",)], w=[PBS[bi3]])

        def att_back(idx):
            hd, qt, g, bi, nb, kind, kb = items[idx]
            bi3 = idx % 4
            po, pd = 3 + g % 2, 5 + g % 2
            if kind == "ctx":
                vs = vctx[:, kb, hd * 128:(hd + 1) * 128]
                rv = [("vctx",)]
            else:
                vs = vv[:, kb, hd * 128:(hd + 1) * 128]
                rv = [("vv", kb)]
            tk.mm1(ps[po][:, :], vs, pbs[bi3], bi == 0, bi == nb - 1, r=rv + [PBS[bi3]], w=[PS[po]])
            tk.mm1(ps[pd][:, :], ONE_1, pbs[bi3], bi == 0, bi == nb - 1, r=[("ones",), PBS[bi3]], w=[PS[pd]])
            if bi == nb - 1:
                rdi = 6 + g % 2
                rd = tf[:, rdi, 0:512]
                emit_recip(ps[pd][:, :], rd, r=[PS[pd]], w=[TF[rdi]])
                op("dve", tt_(qk[:, hd, tile_sl(qt)], ps[po][:, :], rd, ALU.mult),
                   r=[PS[po], TF[rdi]], w=[QK(hd, qt)])

        for idx in range(len(items) + LA):
            if idx < len(items):
                att_front(idx)
            if idx - LA >= 0:
                att_back(idx - LA)
        swa, _ = ws.next(f"woa{l}")
        woa = wview(swa, 4, 1024)
        for tt in range(NT):
            for m in range(8):
                pi = (tt * 8 + m) % 2
                mmg(ps[pi][:, :], [(woa[:, c, m * 128:(m + 1) * 128], qk[:, c, tile_sl(tt)]) for c in range(4)],
                    r=[QK(c, tt) for c in range(4)] + [("w", swa)], w=[PS[pi]])
                add_to_x(m, tt, ps[pi][:, :], pi)
        ws.release(swa)
        tk.barrier()

    def odd_mixer(l, li):
        halo_prep(f"mixg{l}", False)
        spw, _ = ws.next(f"pool{l}")
        pw = wsl[:, spw, 0:2048].rearrange("p (i g n) -> p i g n", i=2, g=4)
        dt_ = vv[:, 0:8, :]
        for tt in range(NT):
            norm_tile_rstd(tt, 7)
            for g in range(4):
                w = 2 ** (g + 1)
                cs = (2 * g, 2 * g + 1)
                hfs = [tf[:, 0, 0:528], tf[:, 3, 0:528]]
                HFT = [TF[0], TF[3]]
                sbufs = [[(tf[:, 1, 0:528], TF[1]), (tf[:, 2, 0:528], TF[2])],
                         [(tf[:, 4, 0:528], TF[4]), (tf[:, 5, 0:528], TF[5])]]
                for q_, c in enumerate(cs):
                    op("dve", lambda e_, c=c, hf=hfs[q_]: e_.tensor_copy(hf[:, 0:16], hprev[:, c, :]),
                       r=[("hprev", c)], w=[HFT[q_]])
                for q_, c in enumerate(cs):
                    op("dve", stt(hfs[q_][:, 16:528], xres[:, c, tile_sl(tt)], pcol(f"mixg{l}", c), tf[:, 7, 0:512],
                                  ALU.mult, ALU.mult), r=[X(c, tt), TF[7], ("par",)], w=[HFT[q_]])
                for q_, c in enumerate(cs):
                    op("dve", lambda e_, c=c, hf=hfs[q_]: e_.tensor_copy(hprev[:, c, :], hf[:, 512:528]),
                       r=[HFT[q_]], w=[("hprev", c)])
                srcs = [(hfs[0], HFT[0]), (hfs[1], HFT[1])]
                for k in range(g + 1):
                    sh = 2 ** k
                    lo = 2 ** (k + 1) - 1
                    for q_ in range(2):
                        src, srct = srcs[q_]
                        dst, dstt = sbufs[q_][k % 2]
                        op("dve", tt_(dst[:, lo:528], src[:, lo:528], src[:, lo - sh:528 - sh], ALU.add),
                           r=[srct], w=[dstt])
                        srcs[q_] = (dst, dstt)
                for q_, c in enumerate(cs):
                    src, srct = srcs[q_]
                    op("dve", stt(dt_[:, c, :], src[:, 16:528], 1.0 / w, hfs[q_][:, 16:528], ALU.mult, ALU.subtract),
                       r=[srct, HFT[q_]], w=[("dt", c)])
                if tt == 0:
                    o_, _w = POFF["invcnt"]
                    for q_, c in enumerate(cs):
                        src, srct = srcs[q_]
                        t16 = sm[:, 40:56] if q_ == 0 else sm[:, 64:80]
                        op("dve", tt_(t16, src[:, 16:32], par[:, o_ + c * 16:o_ + (c + 1) * 16], ALU.mult),
                           r=[srct, ("par",)], w=[("t16", q_)])
                    for q_, c in enumerate(cs):
                        t16 = sm[:, 40:56] if q_ == 0 else sm[:, 64:80]
                        op("dve", tt_(dt_[:, c, 0:16], t16, hfs[q_][:, 16:32], ALU.subtract),
                           r=[("t16", q_), HFT[q_]], w=[("dt", c)])
            for j in range(8):
                g, jj = j // 2, j % 2
                pi = j % 2
                mmg(ps[pi][:, :], [(pw[:, i, g, jj * 128:(jj + 1) * 128], dt_[:, 2 * g + i, :]) for i in range(2)],
                    r=[("dt", 2 * g), ("dt", 2 * g + 1), ("w", spw)], w=[PS[pi]])
                op("dve", stt(xres[:, j, tile_sl(tt)], ps[pi][:, :], pcol(f"scale{l}", j),
                              xres[:, j, tile_sl(tt)], ALU.mult, ALU.add),
                   r=[PS[pi], ("par",)], w=[X(j, tt)])
        ws.release(spw)
        tk.barrier()

    def xattn(l):
        norm_to_hbuf(f"xag{l}")
        kx = vv[:, 0:4, :].rearrange("p a (b m) -> p (a b) m", b=2)
        vx = vv[:, 4:8, :].rearrange("p (b j) f -> p b (j f)", b=2)
        for j in range(2):
            sw, _ = ws.next(f"xk{l}_{j}")
            wk = wview(sw, 8, 512)
            for hh in range(2):
                h = 2 * j + hh
                for i in range(2):
                    mmg(ps[i][:, 0:256],
                        [(wk[:, kc, (2 * hh + i) * 128:(2 * hh + i + 1) * 128], memn[:, kc, :]) for kc in range(8)],
                        r=[("memn",), ("w", sw)], w=[PS[i]])
                    op("act", act_fn(sq[i][:, 0:256], ps[i][:, 0:256], AF.Square), r=[PS[i]], w=[SQ[i]])
                mmg(ps[2][:, 0:256], [(ONE_256, sq[0][:, 0:256]), (ONE_256, sq[1][:, 0:256])],
                    r=[SQ[0], SQ[1], ("ones",)], w=[PS[2]])
                emit_rstd(ps[2][:, 0:256], tf[:, 0, 0:256], r=[PS[2]], w=[TF[0]])
                for i in range(2):
                    op("dve", stt(kx[:, 2 * h + i, :], ps[i][:, 0:256], pcol(f"xkg{l}", i), tf[:, 0, 0:256],
                                  ALU.mult, ALU.mult), r=[PS[i], TF[0], ("par",)], w=[("kx",)])
            ws.release(sw)
        for j in range(2):
            sw, _ = ws.next(f"xv{l}_{j}")
            wv = wview(sw, 8, 512)
            for blk in range(2):
                pi = 3 + blk
                mmg(ps[pi][:, :], [(memn[:, kc, blk * 128:(blk + 1) * 128], wv[:, kc, :]) for kc in range(8)],
                    r=[("memn",), ("w", sw)], w=[PS[pi]])
                op("act", act_fn(vx[:, blk, j * 512:(j + 1) * 512], ps[pi][:, :], AF.Copy),
                   r=[PS[pi]], w=[("vx",)])
            ws.release(sw)
        sqx = [[bt[:, 0, :], bt[:, 1, :]], [bt[:, 2, :], bt[:, 3, :]], [bt[:, 7, :], bt[:, 8, :]]]
        SQX = [[("sq", 0), ("sq", 1)], [("eb", 0), ("eb", 1)], [("bt7",), ("bt8",)]]
        for j in range(2):
            sw, _ = ws.next(f"xq{l}_{j}")
            wq = wview(sw, 8, 512)
            its = [(hh, tt) for hh in range(2) for tt in range(NT)]

            def qfront(n, its=its, wq=wq, sw=sw):
                hh, tt = its[n]
                p = n % 3
                for i in range(2):
                    bk = 2 * p + i
                    mmg(ps[bk][:, :],
                        [(wq[:, kc, (2 * hh + i) * 128:(2 * hh + i + 1) * 128], hslice(kc, tt)) for kc in range(8)],
                        r=[HB(kc, tt) for kc in range(8)] + [("w", sw)], w=[PS[bk]])
                    op("act", act_fn(sqx[p][i], ps[bk][:, :], AF.Square), r=[PS[bk]], w=[SQX[p][i]])

            def qback(n, its=its, j=j):
                hh, tt = its[n]
                h = 2 * j + hh
                p = n % 3
                pn = 6 + n % 2
                mmg(ps[pn][:, :], [(ONE_256, sqx[p][0]), (ONE_256, sqx[p][1])],
                    r=[SQX[p][0], SQX[p][1], ("ones",)], w=[PS[pn]])
                emit_rstd(ps[pn][:, :], tf[:, p, 0:512], r=[PS[pn]], w=[TF[p]])
                for i in range(2):
                    op("dve", stt(qk[:, 2 * h + i, tile_sl(tt)], ps[2 * p + i][:, :], pcol(f"xqg{l}", i),
                                  tf[:, p, 0:512], ALU.mult, ALU.mult),
                       r=[PS[2 * p + i], TF[p], ("par",)], w=[QK(2 * h + i, tt)])

            for n in range(len(its) + 1):
                if n < len(its):
                    qfront(n)
                if n >= 1:
                    qback(n - 1)
            ws.release(sw)
        scale = 256.0 ** -0.5
        ebx = [[bt[:, 2, :], bt[:, 3, :]], [bt[:, 4, :], bt[:, 5, :]]]
        EBX = [[("eb", 0), ("eb", 1)], [("pb", 0), ("pb", 1)]]
        aits = [(h, tt) for h in range(4) for tt in range(NT)]

        def afront(n):
            h, tt = aits[n]
            p = n % 2
            for blk in range(2):
                bk = 2 * p + blk
                mmg(ps[bk][:, :],
                    [(kx[:, 2 * h + i, blk * 128:(blk + 1) * 128], qk[:, 2 * h + i, tile_sl(tt)]) for i in range(2)],
                    r=[("kx",), QK(2 * h, tt), QK(2 * h + 1, tt)], w=[PS[bk]])
                op("act", act_fn(ebx[p][blk], ps[bk][:, :], AF.Exp, scale=scale), r=[PS[bk]], w=[EBX[p][blk]])

        def aback(n):
            h, tt = aits[n]
            p = n % 2
            pd = 6 + p
            for i in range(2):
                mmg(ps[4 + i][:, :],
                    [(vx[:, blk, h * 256 + i * 128:h * 256 + (i + 1) * 128], ebx[p][blk]) for blk in range(2)],
                    r=[("vx",), EBX[p][0], EBX[p][1]], w=[PS[4 + i]])
            mmg(ps[pd][:, :], [(ONE_1, ebx[p][0]), (ONE_1, ebx[p][1])],
                r=[EBX[p][0], EBX[p][1], ("ones",)], w=[PS[pd]])
            rd = tf[:, 2 + p, 0:512]
            emit_recip(ps[pd][:, :], rd, r=[PS[pd]], w=[TF[2 + p]])
            for i in range(2):
                op("dve", tt_(qk[:, 2 * h + i, tile_sl(tt)], ps[4 + i][:, :], rd, ALU.mult),
                   r=[PS[4 + i], TF[2 + p]], w=[QK(2 * h + i, tt)])

        for n in range(len(aits) + 1):
            if n < len(aits):
                afront(n)
            if n >= 1:
                aback(n - 1)
        s0, _ = ws.next(f"xo{l}_0")
        s1, _ = ws.next(f"xo{l}_1")
        wo = [wview(s0, 4, 1024), wview(s1, 4, 1024)]
        for tt in range(NT):
            for m in range(8):
                pi = 6 + (tt * 8 + m) % 2
                mmg(ps[pi][:, :],
                    [(wo[c // 4][:, c % 4, m * 128:(m + 1) * 128], qk[:, c, tile_sl(tt)]) for c in range(8)],
                    r=[QK(c, tt) for c in range(8)] + [("w", s0), ("w", s1)], w=[PS[pi]])
                add_to_x(m, tt, ps[pi][:, :], pi)
        ws.release(s0)
        ws.release(s1)

    def mlp(l):
        norm_to_hbuf(f"mlpg{l}")
        for g in range(8):
            s1, _ = ws.next(f"w1_{l}_{g}")
            s2, _ = ws.next(f"w2_{l}_{g}")
            w1 = wview(s1, 8, 512)
            w2 = wview(s2, 4, 1024)
            k = 0
            for j in range(4):
                for tt in range(NT):
                    pi = k % 4
                    ei_ = k % 2
                    k += 1
                    mmg(ps[pi][:, :], [(w1[:, kc, j * 128:(j + 1) * 128], hslice(kc, tt)) for kc in range(8)],
                        r=[HB(kc, tt) for kc in range(8)] + [("w", s1)], w=[PS[pi]])
                    op("act", act_fn(eb[ei_], ps[pi][:, :], AF.Relu), r=[PS[pi]], w=[EB[ei_]])
                    op("dve", tt_(qk[:, j, tile_sl(tt)], eb[ei_], eb[ei_], ALU.mult), r=[EB[ei_]], w=[QK(j, tt)])
            ws.release(s1)
            k = 0
            for tt in range(NT):
                for m in range(8):
                    pi = 4 + k % 4
                    k += 1
                    mmg(ps[pi][:, :], [(w2[:, j, m * 128:(m + 1) * 128], qk[:, j, tile_sl(tt)]) for j in range(4)],
                        r=[QK(j, tt) for j in range(4)] + [("w", s2)], w=[PS[pi]])
                    add_to_x(m, tt, ps[pi][:, :], pi)
            ws.release(s2)

    ei = 0
    for li, l in enumerate(layers):
        if li > 0:
            tk.eng["pool"].wait_ge(cc0, 1)
            tk.stream["pool"].append(("w", ("cc", id(cc0)), 1))
            dma("pool", f"g0_{li}", xhalo[:, :, :],
                g0[li].ap()[0:128, :].rearrange("p (c t) -> p c t", c=8), w=[("xhalo",)])
        if l % 2 == 0:
            even_mixer(l, li, ei)
            ei += 1
        else:
            odd_mixer(l, li)
        xattn(l)
        mlp(l)
        if li + 1 < len(layers):
            h0 = dma("pool", f"b0_{li + 1}", b0[li + 1].ap().rearrange("p (c t) -> p c t", c=8),
                     xres[:, :, T - HALO:T], r=[X(c, 3) for c in range(8)])
            cc0 = collective(b0[li + 1], g0[li + 1], [h0])

    for tt in range(NT):
        dma("sp", f"yout{tt}", y_d[:, :, tile_sl(tt)], xres[:, :, tile_sl(tt)],
            r=[X(c, tt) for c in range(8)])
    tk.wait_all("sp")
    assert ws.used == len(ws.blocks), (ws.used, len(ws.blocks))
    tk.check_deadlock()
    stack.close()
    return nc


WEIGHT_KEYS = ["ev_w_in", "ev_gate_a_w", "ev_gate_x_w", "ev_w_out", "od_pool_w",
               "xa_w_q", "xa_w_kv", "xa_w_o", "mlp_w1", "mlp_w2"]

_NC_CACHE = {}


def to_fm(a):
    t = a.shape[0]
    return np.ascontiguousarray(a.reshape(t, 8, 128).transpose(2, 1, 0))


def from_fm(a):
    t = a.shape[2]
    return np.ascontiguousarray(a.transpose(2, 1, 0).reshape(t, 1024))


def run_layers(inp, x_full, layers):
    key = tuple(layers)
    if key not in _NC_CACHE:
        _NC_CACHE[key] = build(list(layers))
    nc = _NC_CACHE[key]
    mask = make_mask()
    in_maps = []
    for core in range(8):
        b, half = core // 2, core % 2
        base = half * T
        m = {k: np.ascontiguousarray(np.asarray(inp[k], np.float32)) for k in WEIGHT_KEYS}
        m["x"] = to_fm(x_full[b, base:base + T])
        if half:
            m["xh"] = to_fm(x_full[b, base - HALO:base])
        else:
            m["xh"] = np.zeros((128, 8, HALO), np.float32)
        m["mem"] = to_fm(np.asarray(inp["mem"], np.float32)[b])
        m["params"] = pack_params(inp, half)
        m["mask"] = mask
        in_maps.append(m)
    res = run_bass_kernel_spmd(nc, in_maps, core_ids=list(range(8)))
    out = np.zeros_like(x_full)
    for core in range(8):
        b, half = core // 2, core % 2
        out[b, half * T:(half + 1) * T] = from_fm(np.asarray(res.results[core]["y"]))
    return out


FUSED = True


def kernel(**inp):
    x = np.ascontiguousarray(np.asarray(inp["x"], np.float32))
    if FUSED:
        return run_layers(inp, x, [0, 1, 2, 3])
    for l in range(4):
        x = run_layers(inp, x, [l])
    return x
```

```python
from contextlib import ExitStack
import numpy as np
import concourse.bass as bass
import concourse.mybir as mybir
from concourse.bass_utils import run_bass_kernel_spmd

F32 = mybir.dt.float32
BF16 = mybir.dt.bfloat16
AF = mybir.ActivationFunctionType
ALU = mybir.AluOpType

T = 2048
NT = 4
TT = 512
HALO = 16
HW = HALO + T
KC = 8
EPS = 1e-6
MASKW = 384 + 2048 + 512
NEG = -30000.0
GROUPS = [[0, 1], [2, 3], [4, 5], [6, 7]]
SAME_ENGINE_SYNC = True


def tile_sl(tt):
    return slice(tt * TT, (tt + 1) * TT)


def param_layout():
    off = {}
    n = 0

    def add(name, w):
        nonlocal n
        off[name] = (n, w)
        n += w

    for l in range(4):
        add(f"mixg{l}", 8)
        add(f"xag{l}", 8)
        add(f"mlpg{l}", 8)
        add(f"xqg{l}", 2)
        add(f"xkg{l}", 2)
        if l % 2 == 0:
            add(f"convw{l}", 16)
            add(f"convb{l}", 4)
            add(f"gab{l}", 4)
            add(f"gxb{l}", 4)
            add(f"lam{l}", 4)
            add(f"qg{l}", 1)
            add(f"kg{l}", 1)
        else:
            add(f"scale{l}", 8)
    add("memg", 8)
    add("flag", 1)
    add("ctxbias", 1)
    add("invcnt", 8 * 16)
    return off, n


POFF, NP = param_layout()


def pack_params(inp, half):
    P = np.zeros((128, NP), np.float32)

    def put(name, arr):
        o, w = POFF[name]
        arr = np.asarray(arr, np.float32)
        assert arr.shape == (128, w), (name, arr.shape, w)
        P[:, o:o + w] = arr

    def cols(v):
        v = np.asarray(v, np.float32)
        return v.reshape(-1, 128).T

    for l in range(4):
        put(f"mixg{l}", cols(inp["mix_norm_g"][l]))
        put(f"xag{l}", cols(inp["xattn_norm_g"][l]))
        put(f"mlpg{l}", cols(inp["mlp_norm_g"][l]))
        put(f"xqg{l}", cols(inp["xa_q_norm_g"][l]))
        put(f"xkg{l}", cols(inp["xa_k_norm_g"][l]))
        if l % 2 == 0:
            e = l // 2
            cw = np.asarray(inp["ev_conv_w"][e], np.float32)
            put(f"convw{l}", np.concatenate([cols(cw[j]) for j in range(4)], axis=1))
            put(f"convb{l}", cols(inp["ev_conv_b"][e]))
            put(f"gab{l}", cols(inp["ev_gate_a_b"][e]))
            put(f"gxb{l}", cols(inp["ev_gate_x_b"][e]))
            put(f"lam{l}", cols(inp["ev_lambda"][e]))
            put(f"qg{l}", cols(inp["ev_q_norm_g"][e]))
            put(f"kg{l}", cols(inp["ev_k_norm_g"][e]))
        else:
            put(f"scale{l}", cols(inp["od_scale"][l // 2]))
    put("memg", cols(inp["mem_norm_g"]))
    put("flag", np.full((128, 1), float(half), np.float32))
    put("ctxbias", np.full((128, 1), 0.0 if half else NEG, np.float32))
    ic = np.zeros((8, 16), np.float32)
    for c in range(8):
        w = 2 ** (c // 2 + 1)
        for t in range(16):
            ic[c, t] = 1.0 / w if half else 1.0 / min(t + 1, w)
    put("invcnt", np.broadcast_to(ic.reshape(1, 128), (128, 128)))
    return P


def make_mask():
    ki = np.arange(128)[:, None]
    x = np.arange(MASKW)[None, :]
    d = x - ki - 384
    m = ((d >= 0) & (d <= 128)).astype(np.float32)
    m += ((d >= 0) & (d % 4 == 0) & (d <= 512)).astype(np.float32)
    m += ((d >= 0) & (d % 16 == 0) & (d <= 2048)).astype(np.float32)
    return m


class Trk:
    CH = 4000

    def __init__(self, nc, stack):
        self.nc = nc
        self.stack = stack
        self.eng = dict(pe=nc.tensor, act=nc.scalar, dve=nc.vector, pool=nc.gpsimd, sp=nc.sync)
        self.cnt = {e: 0 for e in self.eng}
        self.sems = {e: [] for e in self.eng}
        self.seen = {e: {f: 0 for f in self.eng} for e in self.eng}
        self.snap = {e: {} for e in self.eng}
        self.dsem = {}
        self.dcnt = {}
        self.dseen = {e: {} for e in self.eng}
        self.lastw = {}
        self.readers = {}
        self.dma_tokens = set()
        self.nops = 0
        self.stream = {e: [] for e in self.eng}

    def _sem(self, e, seq):
        k = (seq - 1) // self.CH
        while len(self.sems[e]) <= k:
            self.sems[e].append(
                self.stack.enter_context(self.nc.semaphore(f"s_{e}_{len(self.sems[e])}")))
        return self.sems[e][k], (seq - 1) % self.CH + 1

    def _wait(self, e, h):
        if h[0] == "eng":
            _, f, s = h
            if f == e and (e == "pe" or not SAME_ENGINE_SYNC):
                return
            if self.seen[e][f] >= s:
                return
            sem, v = self._sem(f, s)
            self.eng[e].wait_ge(sem, v)
            self.stream[e].append(("w", ("eng", f, (s - 1) // self.CH), v))
            self.seen[e][f] = s
            sn = self.snap[f].get(s)
            if sn and f != e:
                for g, v2 in sn.items():
                    if g != e and v2 > self.seen[e][g]:
                        self.seen[e][g] = v2
        else:
            _, key, v = h
            if self.dseen[e].get(key, 0) >= v:
                return
            self.eng[e].wait_ge(self.dsem[key], v)
            self.stream[e].append(("w", ("dma", key), v))
            self.dseen[e][key] = v

    def _deps(self, e, r, w):
        hs = []
        for t in r:
            h = self.lastw.get(t)
            if h:
                hs.append(h)
        for t in w:
            h = self.lastw.get(t)
            if h:
                hs.append(h)
            hs.extend(self.readers.get(t, ()))
        best = {}
        for h in hs:
            k = (h[0], h[1])
            if k not in best or h[2] > best[k][2]:
                best[k] = h
        for h in best.values():
            self._wait(e, h)

    def _commit(self, h, r, w):
        for t in r:
            self.readers.setdefault(t, []).append(h)
        for t in w:
            self.lastw[t] = h
            self.readers[t] = []

    def op(self, e, fn, r=(), w=()):
        self._deps(e, r, w)
        ins = fn(self.eng[e])
        self.cnt[e] += 1
        s = self.cnt[e]
        sem, v = self._sem(e, s)
        ins.then_inc(sem, 1)
        self.stream[e].append(("i", ("eng", e, (s - 1) // self.CH), 1))
        self.snap[e][s] = dict(self.seen[e])
        self._commit(("eng", e, s), r, w)
        self.nops += 1
        return ins

    def mmg(self, out, pairs, r=(), w=()):
        self._deps("pe", r, w)
        n = len(pairs)
        ins = None
        for i, (lhsT, rhs) in enumerate(pairs):
            ins = self.nc.tensor.matmul(out, lhsT, rhs, start=(i == 0), stop=(i == n - 1))
        self.cnt["pe"] += 1
        s = self.cnt["pe"]
        sem, v = self._sem("pe", s)
        ins.then_inc(sem, 1)
        self.stream["pe"].append(("i", ("eng", "pe", (s - 1) // self.CH), 1))
        self.snap["pe"][s] = dict(self.seen["pe"])
        self._commit(("eng", "pe", s), r, w)
        self.nops += n

    def mm1(self, out, lhsT, rhs, start, stop, r=(), w=()):
        self._deps("pe", r, w)
        ins = self.nc.tensor.matmul(out, lhsT, rhs, start=start, stop=stop)
        self.cnt["pe"] += 1
        s = self.cnt["pe"]
        sem, v = self._sem("pe", s)
        ins.then_inc(sem, 1)
        self.stream["pe"].append(("i", ("eng", "pe", (s - 1) // self.CH), 1))
        self.snap["pe"][s] = dict(self.seen["pe"])
        self._commit(("eng", "pe", s), r, w)
        self.nops += 1

    def dma(self, q, key, out, in_, r=(), w=()):
        if key not in self.dsem:
            self.dsem[key] = self.stack.enter_context(self.nc.semaphore(f"d_{key}"))
            self.dcnt[key] = 0
        self._deps(q, r, w)
        self.eng[q].dma_start(out=out, in_=in_).then_inc(self.dsem[key], 16)
        self.stream[q].append(("i", ("dma", key), 16))
        self.dcnt[key] += 16
        h = ("dma", key, self.dcnt[key])
        self._commit(h, r, w)
        self.dma_tokens.update(r)
        self.dma_tokens.update(w)
        return h

    def barrier(self):
        es = ["pe", "act", "dve"]
        for e in es:
            for f in es:
                if f != e and self.cnt[f] > self.seen[e][f]:
                    self._wait(e, ("eng", f, self.cnt[f]))
        for t in list(self.lastw):
            if t in self.dma_tokens:
                continue
            del self.lastw[t]
            self.readers.pop(t, None)
        for t in list(self.readers):
            if t not in self.dma_tokens and t not in self.lastw:
                del self.readers[t]

    def check_deadlock(self):
        val = {}
        ptr = {e: 0 for e in self.eng}
        prog = True
        while prog:
            prog = False
            for e in self.eng:
                st = self.stream[e]
                while ptr[e] < len(st):
                    k, key, v = st[ptr[e]]
                    if k == "w":
                        if val.get(key, 0) < v:
                            break
                    else:
                        val[key] = val.get(key, 0) + v
                    ptr[e] += 1
                    prog = True
        stuck = {e: (ptr[e], len(self.stream[e]), self.stream[e][ptr[e]], val.get(self.stream[e][ptr[e]][1], 0))
                 for e in self.eng if ptr[e] < len(self.stream[e])}
        assert not stuck, ("DEADLOCK", stuck)

    def wait_all(self, e):
        for t, h in self.lastw.items():
            self._wait(e, h)
        for t, hs in self.readers.items():
            for h in hs:
                self._wait(e, h)


def build(layers):
    nc = bass.Bass(target_bir_lowering=False)
    stack = ExitStack()

    def din(name, shape):
        return nc.dram_tensor(name, list(shape), F32, kind="ExternalInput").ap()

    x_d = din("x", (128, 8, T))
    xh_d = din("xh", (128, 8, HALO))
    mem_d = din("mem", (128, 8, 256))
    par_d = din("params", (128, NP))
    mask_d = din("mask", (128, MASKW))
    w_in_d = din("ev_w_in", (2, 1024, 2560))
    gaw_d = din("ev_gate_a_w", (2, 4, 128, 128))
    gxw_d = din("ev_gate_x_w", (2, 4, 128, 128))
    w_out_d = din("ev_w_out", (2, 1024, 1024))
    poolw_d = din("od_pool_w", (2, 4, 256, 256))
    xwq_d = din("xa_w_q", (4, 1024, 1024))
    xwkv_d = din("xa_w_kv", (4, 1024, 2048))
    xwo_d = din("xa_w_o", (4, 1024, 1024))
    w1_d = din("mlp_w1", (4, 1024, 4096))
    w2_d = din("mlp_w2", (4, 4096, 1024))
    y_d = nc.dram_tensor("y", [128, 8, T], F32, kind="ExternalOutput").ap()

    ncc_kv = sum(1 for l in layers if l % 2 == 0)
    b0 = [nc.dram_tensor(f"b0_{i}", [128, 128], F32) for i in range(len(layers))]
    g0 = [nc.dram_tensor(f"g0_{i}", [256, 128], F32) for i in range(len(layers))]
    b2 = [[nc.dram_tensor(f"b2_{i}_{h}", [256, 2048], BF16) for h in range(4)] for i in range(ncc_kv)]
    g2 = [[nc.dram_tensor(f"g2_{i}_{h}", [512, 2048], BF16) for h in range(4)] for i in range(ncc_kv)]
    b3 = [nc.dram_tensor(f"b3_{i}", [128, 4], F32) for i in range(ncc_kv)]
    g3 = [nc.dram_tensor(f"g3_{i}", [256, 4], F32) for i in range(ncc_kv)]

    def sb(name, shape, dt):
        return stack.enter_context(nc.sbuf_tensor(name, list(shape), dt))

    xres = sb("xres", (128, 8, T), F32)
    par = sb("par", (128, NP), F32)
    xhalo = sb("xhalo", (128, 8, HALO), F32)
    hprev = sb("hprev", (128, 8, HALO), F32)
    tf = sb("tf", (128, 8, 528), F32)
    sm = sb("sm", (128, 96), F32)
    hb = sb("hb", (128, 8 * HW), BF16)
    qk = sb("qk", (128, 8, T), BF16)
    vv = sb("vv", (128, 16, 512), BF16)
    NSLOT = 3
    wsl = sb("wsl", (128, NSLOT, 4096), BF16)
    bt = sb("bt", (128, 9, 512), BF16)
    yb = bt[:, 2:6, :].rearrange("p a b -> p (a b)")
    maskt = sb("maskt", (128, MASKW), BF16)
    memn = sb("memn", (128, 8, 256), BF16)
    ones = sb("ones", (128, 4, 128), BF16)
    ps = [stack.enter_context(nc.psum_tensor(f"ps{i}", [128, 512], F32)) for i in range(8)]

    hbuf = hb[:, :].rearrange("p (c t) -> p c t", c=8)
    kctx = hb[:, 0:8192].rearrange("p (h t) -> p h t", h=4)
    vctx = hb[:, 8192:16384].rearrange("p (b f) -> p b f", b=16)

    tk = Trk(nc, stack)
    op, mmg, dma = tk.op, tk.mmg, tk.dma

    def pcol(name, i=0, n=1):
        o, w = POFF[name]
        return par[:, o + i:o + i + n]

    class WStream:
        def __init__(self):
            self.blocks = []
            self.issued = 0
            self.used = 0
            self.released = set()
            self.cur = {}

        def add(self, tag, parts):
            self.blocks.append((tag, parts))

        def _issue(self, i):
            tag, parts = self.blocks[i]
            s = i % NSLOT
            for (lo, shape, src) in parts:
                n = int(np.prod(shape))
                dst = wsl[:, s, lo:lo + n]
                if len(shape) == 1:
                    pass
                elif len(shape) == 2:
                    dst = dst.rearrange("p (a b) -> p a b", a=shape[0])
                elif len(shape) == 3:
                    dst = dst.rearrange("p (a b c) -> p a b c", a=shape[0], b=shape[1])
                dma("pool", f"w{s}", dst, src, w=[("w", s)])

        def _pump(self):
            while self.issued < len(self.blocks) and (
                    self.issued < NSLOT or (self.issued - NSLOT) in self.released):
                self._issue(self.issued)
                self.issued += 1

        def next(self, tag):
            i = self.used
            assert self.blocks[i][0] == tag, (self.blocks[i][0], tag)
            self._pump()
            assert self.issued > i, ("weight slot not released", tag)
            self.used += 1
            s = i % NSLOT
            self.cur[s] = i
            return s, wsl[:, s, :]

        def release(self, s):
            self.released.add(self.cur[s])
            self._pump()

    ws = WStream()

    def wview(s, a, b, lo=0):
        return wsl[:, s, lo:lo + a * b].rearrange("p (a b) -> p a b", a=a)

    def cols_block(wd2, c0, n=512):
        return wd2.rearrange("(kc p) n -> p kc n", p=128)[:, :, c0:c0 + n]

    def rows_block(wd2, r0, nchunks):
        return wd2[r0:r0 + nchunks * 128, :].rearrange("(j p) n -> p j n", p=128)

    def schedule_layer(l):
        if l % 2 == 0:
            e = l // 2
            w = w_in_d[e]
            for c in range(4):
                ws.add(f"rg{l}_{c}", [(0, (8, 128), cols_block(w, 1536 + c * 128, 128)),
                                      (1024, (8, 128), cols_block(w, 2048 + c * 128, 128)),
                                      (2048, (128,), gaw_d[e, c]),
                                      (2176, (128,), gxw_d[e, c])])
            ws.add(f"v{l}", [(0, (8, 512), cols_block(w, 1024))])
            ws.add(f"woy{l}", [(0, (4, 1024), rows_block(w_out_d[e], 512, 4))])
            ws.add(f"k{l}", [(0, (8, 512), cols_block(w, 512))])
            ws.add(f"q{l}", [(0, (8, 512), cols_block(w, 0))])
            ws.add(f"woa{l}", [(0, (4, 1024), rows_block(w_out_d[e], 0, 4))])
        else:
            o = l // 2
            ws.add(f"pool{l}", [(i * 1024, (4, 256),
                                 poolw_d[o][:, i * 128:(i + 1) * 128, :].rearrange("g p n -> p g n"))
                                for i in range(2)])
        kv = xwkv_d[l]
        for j in range(2):
            ws.add(f"xk{l}_{j}", [(0, (8, 512), cols_block(kv, j * 512))])
        for j in range(2):
            ws.add(f"xv{l}_{j}", [(0, (8, 512), cols_block(kv, 1024 + j * 512))])
        for j in range(2):
            ws.add(f"xq{l}_{j}", [(0, (8, 512), cols_block(xwq_d[l], j * 512))])
        for j in range(2):
            ws.add(f"xo{l}_{j}", [(0, (4, 1024), rows_block(xwo_d[l], j * 512, 4))])
        for g in range(8):
            ws.add(f"w1_{l}_{g}", [(0, (8, 512), cols_block(w1_d[l], g * 512))])
            ws.add(f"w2_{l}_{g}", [(0, (4, 1024), rows_block(w2_d[l], g * 512, 4))])

    for l in layers:
        schedule_layer(l)

    def X(c, tt):
        return ("x", c, tt)

    def HB(c, tt):
        return ("hb", c, tt)

    def QK(c, tt):
        return ("qk", c, tt)

    ALLHB = [("hb", c, tt) for c in range(8) for tt in range(4)] + [("hbh",)]

    def act_fn(out, in_, func, **kw):
        return lambda e: e.activation(out=out, in_=in_, func=func, **kw)

    def emit_rstd(psum_ap, dst, r, w):
        op("act", act_fn(dst, psum_ap, AF.Ln, bias=EPSC), r=list(r) + [("epsc",)], w=w)
        op("act", act_fn(dst, dst, AF.Exp, scale=-0.5), r=w, w=w)

    def emit_recip(psum_ap, dst, r, w):
        op("act", act_fn(dst, psum_ap, AF.Ln), r=list(r), w=w)
        op("act", act_fn(dst, dst, AF.Exp, scale=-1.0), r=w, w=w)

    def stt(out, in0, scalar, in1, op0, op1):
        return lambda e: e.scalar_tensor_tensor(out, in0, scalar, in1, op0, op1)

    def tt_(out, in0, in1, opx):
        return lambda e: e.tensor_tensor(out, in0, in1, opx)

    def ts_(out, in0, s1, s2, op0, op1=None):
        if op1 is None:
            return lambda e: e.tensor_scalar(out, in0, s1, None, op0)
        return lambda e: e.tensor_scalar(out, in0, s1, s2, op0, op1)

    EPSC = sm[:, 1:2]
    ONE_D, ONE_128, ONE_256, ONE_1 = (ones[:, i, :] for i in range(4))
    sq = [bt[:, 0, :], bt[:, 1, :]]
    eb = [bt[:, 2, :], bt[:, 3, :]]
    pb = [bt[:, 4, :], bt[:, 5, :]]
    SQ = [("sq", 0), ("sq", 1)]
    EB = [("eb", 0), ("eb", 1)]
    PB = [("pb", 0), ("pb", 1)]
    PS = [("ps", i) for i in range(8)]
    TF = [("tf", i) for i in range(8)]

    for tt in range(NT):
        dma("sp", f"xin{tt}", xres[:, :, tile_sl(tt)], x_d[:, :, tile_sl(tt)],
            w=[X(c, tt) for c in range(8)])
    dma("sp", "par", par[:, :], par_d[:, :], w=[("par",)])
    dma("sp", "xh", xhalo[:, :, :], xh_d[:, :, :], w=[("xhalo",)])
    memf = tf[:, 0:8, 0:256]
    dma("sp", "mem", memf, mem_d[:, :, :], w=TF[0:8])
    dma("pool", "mask", maskt[:, :], mask_d[:, :], w=[("mask",)])
    for i, val in enumerate([1.0 / 1024, 1.0 / 128, 1.0 / 256, 1.0]):
        op("dve", lambda e, i=i, val=val: e.memset(ones[:, i, :], val), w=[("ones",)])
    op("dve", lambda e: e.memset(bt[:, 6, :], 0.0), w=[("sm0",)])
    ZT = bt[:, 6, :]
    op("dve", lambda e: e.memset(sm[:, 1:2], EPS), w=[("epsc",)])

    sqm = qk[:, 0, 0:2048].rearrange("p (c m) -> p c m", c=8)
    for c in range(8):
        op("act", act_fn(sqm[:, c, :], memf[:, c, :], AF.Square), r=TF[0:8], w=[QK(0, 0)])
    mmg(ps[0][:, 0:256], [(ONE_D, sqm[:, c, :]) for c in range(8)],
        r=[QK(0, 0), ("ones",)], w=[PS[0]])
    emit_rstd(ps[0][:, 0:256], tf[:, 4, 256:512], r=[PS[0]], w=[("mrs",)])
    for c in range(8):
        op("dve", stt(memn[:, c, :], memf[:, c, :], pcol("memg", c), tf[:, 4, 256:512],
                      ALU.mult, ALU.mult),
           r=TF[0:8] + [("mrs",), ("par",)], w=[("memn",)])
    tk.barrier()

    def halo_prep(gname, to_hbuf):
        op("dve", ts_(xhalo[:, :, :], xhalo[:, :, :], pcol("flag"), None, ALU.mult),
           r=[("par",)], w=[("xhalo",)])
        sqh = bt[:, 0, 0:128].rearrange("p (c t) -> p c t", c=8)
        op("act", act_fn(sqh, xhalo[:, :, :], AF.Square), r=[("xhalo",)], w=[SQ[0]])
        mmg(ps[7][:, 0:HALO], [(ONE_D, sqh[:, c, :]) for c in range(8)],
            r=[SQ[0], ("ones",)], w=[PS[7]])
        rh = sm[:, 16:32]
        emit_rstd(ps[7][:, 0:HALO], rh, r=[PS[7]], w=[("rh",)])
        for c in range(8):
            dst = hbuf[:, c, 0:HALO] if to_hbuf else hprev[:, c, :]
            op("dve", stt(dst, xhalo[:, c, :], pcol(gname, c), rh, ALU.mult, ALU.mult),
               r=[("xhalo",), ("rh",), ("par",)], w=[("hbh",)] if to_hbuf else [("hprev", c)])

    def norm_tile_rstd(tt, dst_tf):
        for c in range(8):
            op("act", act_fn(sq[c % 2], xres[:, c, tile_sl(tt)], AF.Square),
               r=[X(c, tt)], w=[SQ[c % 2]])
            tk.mm1(ps[7 - tt % 2][:, :], ONE_D, sq[c % 2], c == 0, c == 7, r=[SQ[c % 2], ("ones",)],
                   w=[PS[7 - tt % 2]])
        emit_rstd(ps[7 - tt % 2][:, :], tf[:, dst_tf, 0:512], r=[PS[7 - tt % 2]], w=[TF[dst_tf]])

    def norm_to_hbuf(gname):
        for tt in range(NT):
            ri = 7 - tt % 2
            norm_tile_rstd(tt, ri)
            for c in range(8):
                op("dve", stt(hbuf[:, c, HALO + tt * TT:HALO + (tt + 1) * TT],
                              xres[:, c, tile_sl(tt)], pcol(gname, c), tf[:, ri, 0:512],
                              ALU.mult, ALU.mult),
                   r=[X(c, tt), TF[ri], ("par",)], w=[HB(c, tt)])

    def hslice(c, tt):
        return hbuf[:, c, HALO + tt * TT:HALO + (tt + 1) * TT]

    def add_to_x(m, tt, psum_ap, psi):
        op("dve", tt_(xres[:, m, tile_sl(tt)], psum_ap, xres[:, m, tile_sl(tt)], ALU.add),
           r=[PS[psi]], w=[X(m, tt)])

    cc_count = [0]

    def collective(src_d, dst_d, wait_handles):
        for h in wait_handles:
            tk._wait("pool", h)
        sem = stack.enter_context(nc.semaphore(f"cc{cc_count[0]}"))
        cc_count[0] += 1
        nc.gpsimd.collective_compute(
            "AllGather", ALU.bypass, replica_groups=GROUPS,
            ins=[src_d.ap().opt()], outs=[dst_d.ap().opt()]).then_inc(sem)
        tk.stream["pool"].append(("i", ("cc", id(sem)), 1))
        return sem

    def proj_headnorm(wt, wtok, gcol, base):
        its = [(hd, tt) for hd in range(4) for tt in range(NT)]

        def front(i):
            hd, tt = its[i]
            pa = i % 3
            mmg(ps[pa][:, :], [(wt[:, kc, hd * 128:(hd + 1) * 128], hslice(kc, tt)) for kc in range(8)],
                r=[HB(kc, tt) for kc in range(8)] + [wtok], w=[PS[pa]])
            op("act", act_fn(sq[i % 2], ps[pa][:, :], AF.Square), r=[PS[pa]], w=[SQ[i % 2]])

        def back(i):
            hd, tt = its[i]
            pa, pn, ti = i % 3, 3 + i % 2, i % 2
            mmg(ps[pn][:, :], [(ONE_128, sq[i % 2])], r=[SQ[i % 2], ("ones",)], w=[PS[pn]])
            emit_rstd(ps[pn][:, :], tf[:, ti, 0:512], r=[PS[pn]], w=[TF[ti]])
            op("dve", stt(qk[:, base + hd, tile_sl(tt)], ps[pa][:, :], gcol, tf[:, ti, 0:512],
                          ALU.mult, ALU.mult), r=[PS[pa], TF[ti], ("par",)], w=[QK(base + hd, tt)])

        for i in range(len(its) + 1):
            if i < len(its):
                front(i)
            if i >= 1:
                back(i - 1)

    def even_mixer(l, li, ei):
        e = l // 2
        halo_prep(f"mixg{l}", True)
        norm_to_hbuf(f"mixg{l}")
        sp8 = sm[:, 4:8]
        op("act", act_fn(sp8, pcol(f"lam{l}", 0, 4), AF.Exp, scale=-1.0), r=[("par",)], w=[("sp8",)])
        op("act", act_fn(sp8, sp8, AF.Ln, bias=1.0), r=[("sp8",)], w=[("sp8",)])
        op("dve", ts_(sp8, sp8, -8.0, None, ALU.mult), r=[("sp8",)], w=[("sp8",)])

        hfin = sm[:, 8:12]
        hA = sm[:, 32:36]
        tk.barrier()
        vf = vv[:, :, :].rearrange("p a b -> p (a b)").bitcast(F32)
        VR = lambda r: vf[:, r * 528:(r + 1) * 528]
        CH = [
            dict(xrt=[tf[:, 0, 0:515], tf[:, 1, 0:515]], XT=[TF[0], TF[1]],
                 gx=tf[:, 2, 0:512], GX=TF[2], g3=tf[:, 3, 0:512], G3=TF[3],
                 ra=tf[:, 4, 0:512], RA=TF[4], ri=tf[:, 5, 0:512], RI=TF[5],
                 xc=tf[:, 6, 0:512], XC=TF[6], pxr=0, pgt=1, pga=4, pgx=5, sq=0,
                 hc=sm[:, 12:13], HC=("hc", 0), pc=sm[:, 13:14], PC=("pc", 0),
                 xrh=sm[:, 40:56], XRH=("xrh", 0)),
            dict(xrt=[VR(0)[:, 0:515], VR(1)[:, 0:515]], XT=[("vf", 0), ("vf", 1)],
                 gx=VR(2)[:, 0:512], GX=("vf", 2), g3=VR(3)[:, 0:512], G3=("vf", 3),
                 ra=VR(4)[:, 0:512], RA=("vf", 4), ri=VR(5)[:, 0:512], RI=("vf", 5),
                 xc=VR(6)[:, 0:512], XC=("vf", 6), pxr=2, pgt=3, pga=6, pgx=7, sq=1,
                 hc=sm[:, 14:15], HC=("hc", 1), pc=sm[:, 15:16], PC=("pc", 1),
                 xrh=sm[:, 64:80], XRH=("xrh", 1)),
        ]

        def chain_ops(q, c, tt, srg):
            B = CH[q]
            p = tt % 2
            xt, XTp = B["xrt"][p], B["XT"][p]
            xprev, XTq = B["xrt"][1 - p], B["XT"][1 - p]
            gx, g3_, ra, ri, xc = B["gx"], B["g3"], B["ra"], B["ri"], B["xc"]
            GX, G3, RA, RI, XC = B["GX"], B["G3"], B["RA"], B["RI"], B["XC"]
            hc, pc, HC, PC = B["hc"], B["pc"], B["HC"], B["PC"]
            sqb, SQB = sq[B["sq"]], SQ[B["sq"]]
            pxr, pgt, pga, pgx = B["pxr"], B["pgt"], B["pga"], B["pgx"]
            wxr = wview(srg, 8, 128, 0)
            wgt = wview(srg, 8, 128, 1024)
            wga = wsl[:, srg, 2048:2176]
            wgx = wsl[:, srg, 2176:2304]
            W = ("w", srg)
            cw = lambda j: pcol(f"convw{l}", j * 4 + c)
            hbr = [HB(kc, tt) for kc in range(8)]
            L = []
            A = L.append
            if tt == 0:
                A(lambda: mmg(ps[pxr][:, 0:16], [(wxr[:, kc, :], hbuf[:, kc, 0:HALO]) for kc in range(8)],
                              r=[("hbh",), W], w=[PS[pxr]]))
                A(lambda: op("act", act_fn(B["xrh"], ps[pxr][:, 0:16], AF.Copy), r=[PS[pxr]], w=[B["XRH"]]))
                A(lambda: op("dve", lambda e_: e_.memset(hc, 0.0), w=[HC]))
                A(lambda: op("dve", lambda e_: e_.memset(pc, 1.0), w=[PC]))
            A(lambda: mmg(ps[pxr][:, :], [(wxr[:, kc, :], hslice(kc, tt)) for kc in range(8)], r=hbr + [W], w=[PS[pxr]]))
            A(lambda: mmg(ps[pgt][:, :], [(wgt[:, kc, :], hslice(kc, tt)) for kc in range(8)], r=hbr + [W], w=[PS[pgt]]))
            if tt == 0:
                A(lambda: op("dve", lambda e_: e_.tensor_copy(xt[:, 0:3], B["xrh"][:, 13:16]), r=[B["XRH"]], w=[XTp]))
            else:
                A(lambda: op("dve", lambda e_: e_.tensor_copy(xt[:, 0:3], xprev[:, 512:515]), r=[XTq], w=[XTp]))
            A(lambda: op("act", act_fn(xt[:, 3:515], ps[pxr][:, :], AF.Copy), r=[PS[pxr]], w=[XTp]))
            A(lambda: op("act", act_fn(gx, ps[pgt][:, :], AF.Copy), r=[PS[pgt]], w=[GX]))
            A(lambda: op("act", act_fn(g3_, gx, AF.Square), r=[GX], w=[G3]))
            A(lambda: op("act", act_fn(xc, xt[:, 3:515], AF.Identity, scale=cw(3), bias=pcol(f"convb{l}", c)),
                         r=[XTp, ("par",)], w=[XC]))
            A(lambda: op("dve", ts_(g3_, g3_, 0.044715, 1.0, ALU.mult, ALU.add), r=[G3], w=[G3]))
            A(lambda: op("dve", stt(xc, xt[:, 0:512], cw(0), xc, ALU.mult, ALU.add), r=[XTp, XC, ("par",)], w=[XC]))
            A(lambda: op("dve", tt_(g3_, g3_, gx, ALU.mult), r=[GX, G3], w=[G3]))
            A(lambda: op("dve", stt(xc, xt[:, 1:513], cw(1), xc, ALU.mult, ALU.add), r=[XTp, XC, ("par",)], w=[XC]))
            A(lambda: op("act", act_fn(g3_, g3_, AF.Sigmoid, scale=1.5957691216057308), r=[G3], w=[G3]))
            A(lambda: op("dve", stt(xc, xt[:, 2:514], cw(2), xc, ALU.mult, ALU.add), r=[XTp, XC, ("par",)], w=[XC]))
            A(lambda: op("act", act_fn(sqb, xc, AF.Copy), r=[XC], w=[SQB]))
            A(lambda: op("dve", tt_(gx, gx, g3_, ALU.mult), r=[GX, G3], w=[GX]))
            A(lambda: mmg(ps[pga][:, :], [(wga, sqb)], r=[SQB, W], w=[PS[pga]]))
            A(lambda: mmg(ps[pgx][:, :], [(wgx, sqb)], r=[SQB, W], w=[PS[pgx]]))
            A(lambda: op("act", act_fn(ra, ps[pga][:, :], AF.Sigmoid, bias=pcol(f"gab{l}", c)),
                         r=[PS[pga], ("par",)], w=[RA]))
            A(lambda: op("act", act_fn(ri, ps[pgx][:, :], AF.Sigmoid, bias=pcol(f"gxb{l}", c)),
                         r=[PS[pgx], ("par",)], w=[RI]))
            A(lambda: op("act", act_fn(ra, ra, AF.Exp, scale=sp8[:, c:c + 1]), r=[RA, ("sp8",)], w=[RA]))
            A(lambda: op("dve", tt_(ri, ri, xc, ALU.mult), r=[RI, XC], w=[RI]))
            A(lambda: op("dve", tt_(g3_, ra, ra, ALU.mult), r=[RA], w=[G3]))
            A(lambda: op("act", act_fn(g3_, g3_, AF.Sqrt, scale=-1.0, bias=1.0), r=[G3], w=[G3]))
            A(lambda: op("dve", lambda e_: e_.tensor_tensor_scan(xc, ra, ZT, pc, ALU.mult, ALU.add),
                         r=[RA, RI, PC, ("sm0",)], w=[XC]))
            A(lambda: op("dve", tt_(ri, ri, g3_, ALU.mult), r=[RI, G3], w=[RI]))
            A(lambda: op("dve", lambda e_: e_.tensor_copy(pc, xc[:, 511:512]), r=[XC], w=[PC]))
            A(lambda: op("dve", lambda e_: e_.tensor_tensor_scan(g3_, ra, ri, hc, ALU.mult, ALU.add),
                         r=[RA, RI, HC], w=[G3]))
            A(lambda: op("dve", tt_(qk[:, c, tile_sl(tt)], xc, gx, ALU.mult), r=[XC, GX], w=[QK(c, tt)]))
            A(lambda: op("dve", lambda e_: e_.tensor_copy(hc, g3_[:, 511:512]), r=[G3], w=[HC]))
            A(lambda: op("dve", tt_(qk[:, 4 + c, tile_sl(tt)], g3_, gx, ALU.mult), r=[G3, GX], w=[QK(4 + c, tt)]))
            if tt == NT - 1:
                A(lambda: op("dve", lambda e_: e_.tensor_copy(hfin[:, c:c + 1], hc), r=[HC], w=[("hfin",)]))
            return L

        for pair in ((0, 1), (2, 3)):
            srgs = [ws.next(f"rg{l}_{c}")[0] for c in pair]
            for tt in range(NT):
                lists = [chain_ops(q, c, tt, srgs[q]) for q, c in enumerate(pair)]
                for k in range(max(len(x_) for x_ in lists)):
                    for x_ in lists:
                        if k < len(x_):
                            x_[k]()
            for sg in srgs:
                ws.release(sg)
        h3 = dma("pool", f"b3_{ei}", b3[ei].ap(), hfin, r=[("hfin",)])
        cc3 = collective(b3[ei], g3[ei], [h3])
        tk.barrier()

        sv, _ = ws.next(f"v{l}")
        wv = wview(sv, 8, 512)
        for blk in range(16):
            pa = 4 + blk % 2
            tt = blk // 4
            mmg(ps[pa][:, :],
                [(hbuf[:, kc, HALO + blk * 128:HALO + (blk + 1) * 128], wv[:, kc, :]) for kc in range(8)],
                r=[HB(kc, tt) for kc in range(8)] + [("w", sv)], w=[PS[pa]])
            op("act", act_fn(vv[:, blk, :], ps[pa][:, :], AF.Copy), r=[PS[pa]], w=[("vv", blk)])
        ws.release(sv)

        tk.eng["pool"].wait_ge(cc3, 1)
        tk.stream["pool"].append(("w", ("cc", id(cc3)), 1))
        dma("pool", f"g3_{ei}", hA, g3[ei].ap()[0:128, :], w=[("hA",)])
        op("dve", ts_(hA, hA, pcol("flag"), None, ALU.mult), r=[("hA",), ("par",)], w=[("hA",)])
        for c in range(4):
            for tt in range(NT):
                op("dve", stt(qk[:, 4 + c, tile_sl(tt)], qk[:, c, tile_sl(tt)], hA[:, c:c + 1],
                              qk[:, 4 + c, tile_sl(tt)], ALU.mult, ALU.add),
                   r=[QK(c, tt), QK(4 + c, tt), ("hA",)], w=[QK(4 + c, tt)])
        swy, _ = ws.next(f"woy{l}")
        woy = wview(swy, 4, 1024)
        for m in range(8):
            for tt in range(NT):
                pi = 6 + (m * NT + tt) % 2
                mmg(ps[pi][:, :], [(woy[:, c, m * 128:(m + 1) * 128], qk[:, 4 + c, tile_sl(tt)]) for c in range(4)],
                    r=[QK(4 + c, tt) for c in range(4)] + [("w", swy)], w=[PS[pi]])
                add_to_x(m, tt, ps[pi][:, :], pi)
        ws.release(swy)

        sk, _ = ws.next(f"k{l}")
        wk = wview(sk, 8, 512)
        proj_headnorm(wk, ("w", sk), pcol(f"kg{l}"), 4)
        ws.release(sk)
        cc2 = []
        for hd in range(4):
            h2a = dma("pool", f"b2k_{ei}_{hd}", b2[ei][hd].ap()[0:128, :], qk[:, 4 + hd, :],
                      r=[QK(4 + hd, tt) for tt in range(4)])
            h2b = dma("pool", f"b2v_{ei}_{hd}",
                      b2[ei][hd].ap()[128:256, :].rearrange("p (b f) -> p b f", b=16),
                      vv[:, :, hd * 128:(hd + 1) * 128], r=[("vv", blk) for blk in range(16)])
            cc2.append(collective(b2[ei][hd], g2[ei][hd], [h2a, h2b]))

        sq_, _ = ws.next(f"q{l}")
        wq = wview(sq_, 8, 512)
        proj_headnorm(wq, ("w", sq_), pcol(f"qg{l}"), 0)
        ws.release(sq_)
        tk.barrier()
        for f in ("pe", "act", "dve"):
            tk._wait("pool", ("eng", f, tk.cnt[f]))
        for hd in range(4):
            tk.eng["pool"].wait_ge(cc2[hd], 1)
            tk.stream["pool"].append(("w", ("cc", id(cc2[hd])), 1))
            dma("pool", f"ctxk_{ei}_{hd}", kctx[:, hd, :], g2[ei][hd].ap()[0:128, :],
                w=[("kctx", hd)] + ALLHB)
            dma("pool", f"ctxv_{ei}_{hd}", vctx[:, :, hd * 128:(hd + 1) * 128],
                g2[ei][hd].ap()[128:256, :].rearrange("p (b f) -> p b f", b=16),
                w=[("vctx", hd)] + ALLHB)

        scale = 128.0 ** -0.5
        ebs = [bt[:, 2, :], bt[:, 3, :], bt[:, 0, :], bt[:, 7, :]]
        pbs = [bt[:, 4, :], bt[:, 5, :], bt[:, 1, :], bt[:, 8, :]]
        EBS = [("eb", 0), ("eb", 1), ("sq", 0), ("bt7",)]
        PBS = [("pb", 0), ("pb", 1), ("sq", 1), ("bt8",)]
        SBK = [0, 1, 2, 7]
        items = []
        gi = 0
        for hd in range(4):
            for qt in reversed(range(NT)):
                blocks = [("own", kb) for kb in range(0, 4 * qt + 4)] + [("ctx", kb) for kb in range(4 * qt, 16)]
                for bi, (kind, kb) in enumerate(blocks):
                    items.append((hd, qt, gi, bi, len(blocks), kind, kb))
                gi += 1
        LA = 3

        def att_front(idx):
            hd, qt, g, bi, nb, kind, kb = items[idx]
            si, bi3 = SBK[idx % 4], idx % 4
            if kind == "ctx":
                kT = kctx[:, hd, kb * 128:(kb + 1) * 128]
                rk = [("kctx", hd)]
                d0 = 512 * qt + 2048 - 128 * kb
                bias = pcol("ctxbias")
            else:
                kT = qk[:, 4 + hd, kb * 128:(kb + 1) * 128]
                rk = [QK(4 + hd, kb // 4)]
                d0 = 512 * qt - 128 * kb
                bias = 0.0
            off = d0 + 384
            mmg(ps[si][:, :], [(kT, qk[:, hd, tile_sl(qt)])], r=rk + [QK(hd, qt)], w=[PS[si]])
            op("act", act_fn(ebs[bi3], ps[si][:, :], AF.Exp, scale=scale, bias=bias),
               r=[PS[si], ("par",)], w=[EBS[bi3]])
            op("dve", tt_(pbs[bi3], ebs[bi3], maskt[:, off:off + 512], ALU.mult),
               r=[EBS[bi3], ("mask",)], w=[PBS[bi3]])

        def att_back(idx):
            hd, qt, g, bi, nb, kind, kb = items[idx]
            bi3 = idx % 4
            po, pd = 3 + g % 2, 5 + g % 2
            if kind == "ctx":
                vs = vctx[:, kb, hd * 128:(hd + 1) * 128]
                rv = [("vctx", hd)]
            else:
                vs = vv[:, kb, hd * 128:(hd + 1) * 128]
                rv = [("vv", kb)]
            tk.mm1(ps[po][:, :], vs, pbs[bi3], bi == 0, bi == nb - 1, r=rv + [PBS[bi3]], w=[PS[po]])
            tk.mm1(ps[pd][:, :], ONE_1, pbs[bi3], bi == 0, bi == nb - 1, r=[("ones",), PBS[bi3]], w=[PS[pd]])
            if bi == nb - 1:
                rdi = 6 + g % 2
                rd = tf[:, rdi, 0:512]
                emit_recip(ps[pd][:, :], rd, r=[PS[pd]], w=[TF[rdi]])
                op("dve", tt_(qk[:, hd, tile_sl(qt)], ps[po][:, :], rd, ALU.mult),
                   r=[PS[po], TF[rdi]], w=[QK(hd, qt)])

        for idx in range(len(items) + LA):
            if idx < len(items):
                att_front(idx)
            if idx - LA >= 0:
                att_back(idx - LA)
        swa, _ = ws.next(f"woa{l}")
        woa = wview(swa, 4, 1024)
        for tt in range(NT):
            for m in range(8):
                pi = (tt * 8 + m) % 2
                mmg(ps[pi][:, :], [(woa[:, c, m * 128:(m + 1) * 128], qk[:, c, tile_sl(tt)]) for c in range(4)],
                    r=[QK(c, tt) for c in range(4)] + [("w", swa)], w=[PS[pi]])
                add_to_x(m, tt, ps[pi][:, :], pi)
        ws.release(swa)
        tk.barrier()

    def odd_mixer(l, li):
        halo_prep(f"mixg{l}", False)
        spw, _ = ws.next(f"pool{l}")
        pw = wsl[:, spw, 0:2048].rearrange("p (i g n) -> p i g n", i=2, g=4)
        dt_ = vv[:, 0:8, :]
        for tt in range(NT):
            norm_tile_rstd(tt, 7)
            for g in range(4):
                w = 2 ** (g + 1)
                cs = (2 * g, 2 * g + 1)
                hfs = [tf[:, 0, 0:528], tf[:, 3, 0:528]]
                HFT = [TF[0], TF[3]]
                sbufs = [[(tf[:, 1, 0:528], TF[1]), (tf[:, 2, 0:528], TF[2])],
                         [(tf[:, 4, 0:528], TF[4]), (tf[:, 5, 0:528], TF[5])]]
                for q_, c in enumerate(cs):
                    op("dve", lambda e_, c=c, hf=hfs[q_]: e_.tensor_copy(hf[:, 0:16], hprev[:, c, :]),
                       r=[("hprev", c)], w=[HFT[q_]])
                for q_, c in enumerate(cs):
                    op("dve", stt(hfs[q_][:, 16:528], xres[:, c, tile_sl(tt)], pcol(f"mixg{l}", c), tf[:, 7, 0:512],
                                  ALU.mult, ALU.mult), r=[X(c, tt), TF[7], ("par",)], w=[HFT[q_]])
                for q_, c in enumerate(cs):
                    op("dve", lambda e_, c=c, hf=hfs[q_]: e_.tensor_copy(hprev[:, c, :], hf[:, 512:528]),
                       r=[HFT[q_]], w=[("hprev", c)])
                srcs = [(hfs[0], HFT[0]), (hfs[1], HFT[1])]
                for k in range(g + 1):
                    sh = 2 ** k
                    lo = 2 ** (k + 1) - 1
                    for q_ in range(2):
                        src, srct = srcs[q_]
                        dst, dstt = sbufs[q_][k % 2]
                        op("dve", tt_(dst[:, lo:528], src[:, lo:528], src[:, lo - sh:528 - sh], ALU.add),
                           r=[srct], w=[dstt])
                        srcs[q_] = (dst, dstt)
                for q_, c in enumerate(cs):
                    src, srct = srcs[q_]
                    op("dve", stt(dt_[:, c, :], src[:, 16:528], 1.0 / w, hfs[q_][:, 16:528], ALU.mult, ALU.subtract),
                       r=[srct, HFT[q_]], w=[("dt", c)])
                if tt == 0:
                    o_, _w = POFF["invcnt"]
                    for q_, c in enumerate(cs):
                        src, srct = srcs[q_]
                        t16 = sm[:, 40:56] if q_ == 0 else sm[:, 64:80]
                        op("dve", tt_(t16, src[:, 16:32], par[:, o_ + c * 16:o_ + (c + 1) * 16], ALU.mult),
                           r=[srct, ("par",)], w=[("t16", q_)])
                    for q_, c in enumerate(cs):
                        t16 = sm[:, 40:56] if q_ == 0 else sm[:, 64:80]
                        op("dve", tt_(dt_[:, c, 0:16], t16, hfs[q_][:, 16:32], ALU.subtract),
                           r=[("t16", q_), HFT[q_]], w=[("dt", c)])
            for j in range(8):
                g, jj = j // 2, j % 2
                pi = j % 2
                mmg(ps[pi][:, :], [(pw[:, i, g, jj * 128:(jj + 1) * 128], dt_[:, 2 * g + i, :]) for i in range(2)],
                    r=[("dt", 2 * g), ("dt", 2 * g + 1), ("w", spw)], w=[PS[pi]])
                op("dve", stt(xres[:, j, tile_sl(tt)], ps[pi][:, :], pcol(f"scale{l}", j),
                              xres[:, j, tile_sl(tt)], ALU.mult, ALU.add),
                   r=[PS[pi], ("par",)], w=[X(j, tt)])
        ws.release(spw)
        tk.barrier()

    def xattn(l):
        kx = vv[:, 0:4, :].rearrange("p a (b m) -> p (a b) m", b=2)
        vx = vv[:, 4:8, :].rearrange("p (b j) f -> p b (j f)", b=2)
        for j in range(2):
            sw, _ = ws.next(f"xk{l}_{j}")
            wk = wview(sw, 8, 512)
            for hh in range(2):
                h = 2 * j + hh
                for i in range(2):
                    mmg(ps[i][:, 0:256],
                        [(wk[:, kc, (2 * hh + i) * 128:(2 * hh + i + 1) * 128], memn[:, kc, :]) for kc in range(8)],
                        r=[("memn",), ("w", sw)], w=[PS[i]])
                    op("act", act_fn(sq[i][:, 0:256], ps[i][:, 0:256], AF.Square), r=[PS[i]], w=[SQ[i]])
                mmg(ps[2][:, 0:256], [(ONE_256, sq[0][:, 0:256]), (ONE_256, sq[1][:, 0:256])],
                    r=[SQ[0], SQ[1], ("ones",)], w=[PS[2]])
                emit_rstd(ps[2][:, 0:256], tf[:, 0, 0:256], r=[PS[2]], w=[TF[0]])
                for i in range(2):
                    op("dve", stt(kx[:, 2 * h + i, :], ps[i][:, 0:256], pcol(f"xkg{l}", i), tf[:, 0, 0:256],
                                  ALU.mult, ALU.mult), r=[PS[i], TF[0], ("par",)], w=[("kx",)])
            ws.release(sw)
        for j in range(2):
            sw, _ = ws.next(f"xv{l}_{j}")
            wv = wview(sw, 8, 512)
            for blk in range(2):
                pi = 3 + blk
                mmg(ps[pi][:, :], [(memn[:, kc, blk * 128:(blk + 1) * 128], wv[:, kc, :]) for kc in range(8)],
                    r=[("memn",), ("w", sw)], w=[PS[pi]])
                op("act", act_fn(vx[:, blk, j * 512:(j + 1) * 512], ps[pi][:, :], AF.Copy),
                   r=[PS[pi]], w=[("vx",)])
            ws.release(sw)
        norm_to_hbuf(f"xag{l}")
        sqx = [[bt[:, 0, :], bt[:, 1, :]], [bt[:, 2, :], bt[:, 3, :]], [bt[:, 7, :], bt[:, 8, :]]]
        SQX = [[("sq", 0), ("sq", 1)], [("eb", 0), ("eb", 1)], [("bt7",), ("bt8",)]]
        for j in range(2):
            sw, _ = ws.next(f"xq{l}_{j}")
            wq = wview(sw, 8, 512)
            its = [(hh, tt) for hh in range(2) for tt in range(NT)]

            def qfront(n, its=its, wq=wq, sw=sw):
                hh, tt = its[n]
                p = n % 3
                for i in range(2):
                    bk = 2 * p + i
                    mmg(ps[bk][:, :],
                        [(wq[:, kc, (2 * hh + i) * 128:(2 * hh + i + 1) * 128], hslice(kc, tt)) for kc in range(8)],
                        r=[HB(kc, tt) for kc in range(8)] + [("w", sw)], w=[PS[bk]])
                    op("act", act_fn(sqx[p][i], ps[bk][:, :], AF.Square), r=[PS[bk]], w=[SQX[p][i]])

            def qback(n, its=its, j=j):
                hh, tt = its[n]
                h = 2 * j + hh
                p = n % 3
                pn = 6 + n % 2
                mmg(ps[pn][:, :], [(ONE_256, sqx[p][0]), (ONE_256, sqx[p][1])],
                    r=[SQX[p][0], SQX[p][1], ("ones",)], w=[PS[pn]])
                emit_rstd(ps[pn][:, :], tf[:, p, 0:512], r=[PS[pn]], w=[TF[p]])
                for i in range(2):
                    op("dve", stt(qk[:, 2 * h + i, tile_sl(tt)], ps[2 * p + i][:, :], pcol(f"xqg{l}", i),
                                  tf[:, p, 0:512], ALU.mult, ALU.mult),
                       r=[PS[2 * p + i], TF[p], ("par",)], w=[QK(2 * h + i, tt)])

            for n in range(len(its) + 1):
                if n < len(its):
                    qfront(n)
                if n >= 1:
                    qback(n - 1)
            ws.release(sw)
        scale = 256.0 ** -0.5
        ebx = [[bt[:, 2, :], bt[:, 3, :]], [bt[:, 4, :], bt[:, 5, :]]]
        EBX = [[("eb", 0), ("eb", 1)], [("pb", 0), ("pb", 1)]]
        aits = [(h, tt) for h in range(4) for tt in range(NT)]

        def afront(n):
            h, tt = aits[n]
            p = n % 2
            for blk in range(2):
                bk = 2 * p + blk
                mmg(ps[bk][:, :],
                    [(kx[:, 2 * h + i, blk * 128:(blk + 1) * 128], qk[:, 2 * h + i, tile_sl(tt)]) for i in range(2)],
                    r=[("kx",), QK(2 * h, tt), QK(2 * h + 1, tt)], w=[PS[bk]])
                op("act", act_fn(ebx[p][blk], ps[bk][:, :], AF.Exp, scale=scale), r=[PS[bk]], w=[EBX[p][blk]])

        def aback(n):
            h, tt = aits[n]
            p = n % 2
            pd = 6 + p
            for i in range(2):
                mmg(ps[4 + i][:, :],
                    [(vx[:, blk, h * 256 + i * 128:h * 256 + (i + 1) * 128], ebx[p][blk]) for blk in range(2)],
                    r=[("vx",), EBX[p][0], EBX[p][1]], w=[PS[4 + i]])
            mmg(ps[pd][:, :], [(ONE_1, ebx[p][0]), (ONE_1, ebx[p][1])],
                r=[EBX[p][0], EBX[p][1], ("ones",)], w=[PS[pd]])
            rd = tf[:, 2 + p, 0:512]
            emit_recip(ps[pd][:, :], rd, r=[PS[pd]], w=[TF[2 + p]])
            for i in range(2):
                op("dve", tt_(qk[:, 2 * h + i, tile_sl(tt)], ps[4 + i][:, :], rd, ALU.mult),
                   r=[PS[4 + i], TF[2 + p]], w=[QK(2 * h + i, tt)])

        for n in range(len(aits) + 1):
            if n < len(aits):
                afront(n)
            if n >= 1:
                aback(n - 1)
        s0, _ = ws.next(f"xo{l}_0")
        s1, _ = ws.next(f"xo{l}_1")
        wo = [wview(s0, 4, 1024), wview(s1, 4, 1024)]
        for tt in range(NT):
            for m in range(8):
                pi = 6 + (tt * 8 + m) % 2
                mmg(ps[pi][:, :],
                    [(wo[c // 4][:, c % 4, m * 128:(m + 1) * 128], qk[:, c, tile_sl(tt)]) for c in range(8)],
                    r=[QK(c, tt) for c in range(8)] + [("w", s0), ("w", s1)], w=[PS[pi]])
                add_to_x(m, tt, ps[pi][:, :], pi)
        ws.release(s0)
        ws.release(s1)

    def mlp(l):
        norm_to_hbuf(f"mlpg{l}")
        for g in range(8):
            s1, _ = ws.next(f"w1_{l}_{g}")
            s2, _ = ws.next(f"w2_{l}_{g}")
            w1 = wview(s1, 8, 512)
            w2 = wview(s2, 4, 1024)
            k = 0
            for j in range(4):
                for tt in range(NT):
                    pi = k % 4
                    ei_ = k % 2
                    k += 1
                    mmg(ps[pi][:, :], [(w1[:, kc, j * 128:(j + 1) * 128], hslice(kc, tt)) for kc in range(8)],
                        r=[HB(kc, tt) for kc in range(8)] + [("w", s1)], w=[PS[pi]])
                    op("act", act_fn(eb[ei_], ps[pi][:, :], AF.Relu), r=[PS[pi]], w=[EB[ei_]])
                    op("dve", tt_(qk[:, j, tile_sl(tt)], eb[ei_], eb[ei_], ALU.mult), r=[EB[ei_]], w=[QK(j, tt)])
            ws.release(s1)
            k = 0
            for tt in range(NT):
                for m in range(8):
                    pi = 4 + k % 4
                    k += 1
                    mmg(ps[pi][:, :], [(w2[:, j, m * 128:(m + 1) * 128], qk[:, j, tile_sl(tt)]) for j in range(4)],
                        r=[QK(j, tt) for j in range(4)] + [("w", s2)], w=[PS[pi]])
                    add_to_x(m, tt, ps[pi][:, :], pi)
            ws.release(s2)

    ei = 0
    for li, l in enumerate(layers):
        if li > 0:
            tk.eng["pool"].wait_ge(cc0, 1)
            tk.stream["pool"].append(("w", ("cc", id(cc0)), 1))
            dma("pool", f"g0_{li}", xhalo[:, :, :],
                g0[li].ap()[0:128, :].rearrange("p (c t) -> p c t", c=8), w=[("xhalo",)])
        if l % 2 == 0:
            even_mixer(l, li, ei)
            ei += 1
        else:
            odd_mixer(l, li)
        xattn(l)
        mlp(l)
        if li + 1 < len(layers):
            h0 = dma("pool", f"b0_{li + 1}", b0[li + 1].ap().rearrange("p (c t) -> p c t", c=8),
                     xres[:, :, T - HALO:T], r=[X(c, 3) for c in range(8)])
            cc0 = collective(b0[li + 1], g0[li + 1], [h0])

    for tt in range(NT):
        dma("sp", f"yout{tt}", y_d[:, :, tile_sl(tt)], xres[:, :, tile_sl(tt)],
            r=[X(c, tt) for c in range(8)])
    tk.wait_all("sp")
    assert ws.used == len(ws.blocks), (ws.used, len(ws.blocks))
    tk.check_deadlock()
    stack.close()
    return nc


WEIGHT_KEYS = ["ev_w_in", "ev_gate_a_w", "ev_gate_x_w", "ev_w_out", "od_pool_w",
               "xa_w_q", "xa_w_kv", "xa_w_o", "mlp_w1", "mlp_w2"]

_NC_CACHE = {}


def to_fm(a):
    t = a.shape[0]
    return np.ascontiguousarray(a.reshape(t, 8, 128).transpose(2, 1, 0))


def from_fm(a):
    t = a.shape[2]
    return np.ascontiguousarray(a.transpose(2, 1, 0).reshape(t, 1024))


def run_layers(inp, x_full, layers):
    key = tuple(layers)
    if key not in _NC_CACHE:
        _NC_CACHE[key] = build(list(layers))
    nc = _NC_CACHE[key]
    mask = make_mask()
    in_maps = []
    for core in range(8):
        b, half = core // 2, core % 2
        base = half * T
        m = {k: np.ascontiguousarray(np.asarray(inp[k], np.float32)) for k in WEIGHT_KEYS}
        m["x"] = to_fm(x_full[b, base:base + T])
        if half:
            m["xh"] = to_fm(x_full[b, base - HALO:base])
        else:
            m["xh"] = np.zeros((128, 8, HALO), np.float32)
        m["mem"] = to_fm(np.asarray(inp["mem"], np.float32)[b])
        m["params"] = pack_params(inp, half)
        m["mask"] = mask
        in_maps.append(m)
    res = run_bass_kernel_spmd(nc, in_maps, core_ids=list(range(8)))
    out = np.zeros_like(x_full)
    for core in range(8):
        b, half = core // 2, core % 2
        out[b, half * T:(half + 1) * T] = from_fm(np.asarray(res.results[core]["y"]))
    return out


FUSED = True


def kernel(**inp):
    x = np.ascontiguousarray(np.asarray(inp["x"], np.float32))
    if FUSED:
        return run_layers(inp, x, [0, 1, 2, 3])
    for l in range(4):
        x = run_layers(inp, x, [l])
    return x
```

```python
from contextlib import ExitStack
import numpy as np
import concourse.bass as bass
import concourse.mybir as mybir
from concourse.bass_utils import run_bass_kernel_spmd

F32 = mybir.dt.float32
BF16 = mybir.dt.bfloat16
AF = mybir.ActivationFunctionType
ALU = mybir.AluOpType

T = 2048
NT = 4
TT = 512
HALO = 16
HW = HALO + T
KC = 8
EPS = 1e-6
MASKW = 384 + 2048 + 512
NEG = -30000.0
GROUPS = [[0, 1], [2, 3], [4, 5], [6, 7]]
SAME_ENGINE_SYNC = True


def tile_sl(tt):
    return slice(tt * TT, (tt + 1) * TT)


def param_layout():
    off = {}
    n = 0

    def add(name, w):
        nonlocal n
        off[name] = (n, w)
        n += w

    for l in range(4):
        add(f"mixg{l}", 8)
        add(f"xag{l}", 8)
        add(f"mlpg{l}", 8)
        add(f"xqg{l}", 2)
        add(f"xkg{l}", 2)
        if l % 2 == 0:
            add(f"convw{l}", 16)
            add(f"convb{l}", 4)
            add(f"gab{l}", 4)
            add(f"gxb{l}", 4)
            add(f"lam{l}", 4)
            add(f"qg{l}", 1)
            add(f"kg{l}", 1)
        else:
            add(f"scale{l}", 8)
    add("memg", 8)
    add("flag", 1)
    add("ctxbias", 1)
    add("invcnt", 8 * 16)
    return off, n


POFF, NP = param_layout()


def pack_params(inp, half):
    P = np.zeros((128, NP), np.float32)

    def put(name, arr):
        o, w = POFF[name]
        arr = np.asarray(arr, np.float32)
        assert arr.shape == (128, w), (name, arr.shape, w)
        P[:, o:o + w] = arr

    def cols(v):
        v = np.asarray(v, np.float32)
        return v.reshape(-1, 128).T

    for l in range(4):
        put(f"mixg{l}", cols(inp["mix_norm_g"][l]))
        put(f"xag{l}", cols(inp["xattn_norm_g"][l]))
        put(f"mlpg{l}", cols(inp["mlp_norm_g"][l]))
        put(f"xqg{l}", cols(inp["xa_q_norm_g"][l]))
        put(f"xkg{l}", cols(inp["xa_k_norm_g"][l]))
        if l % 2 == 0:
            e = l // 2
            cw = np.asarray(inp["ev_conv_w"][e], np.float32)
            put(f"convw{l}", np.concatenate([cols(cw[j]) for j in range(4)], axis=1))
            put(f"convb{l}", cols(inp["ev_conv_b"][e]))
            put(f"gab{l}", cols(inp["ev_gate_a_b"][e]))
            put(f"gxb{l}", cols(inp["ev_gate_x_b"][e]))
            put(f"lam{l}", cols(inp["ev_lambda"][e]))
            put(f"qg{l}", cols(inp["ev_q_norm_g"][e]))
            put(f"kg{l}", cols(inp["ev_k_norm_g"][e]))
        else:
            put(f"scale{l}", cols(inp["od_scale"][l // 2]))
    put("memg", cols(inp["mem_norm_g"]))
    put("flag", np.full((128, 1), float(half), np.float32))
    put("ctxbias", np.full((128, 1), 0.0 if half else NEG, np.float32))
    ic = np.zeros((8, 16), np.float32)
    for c in range(8):
        w = 2 ** (c // 2 + 1)
        for t in range(16):
            ic[c, t] = 1.0 / w if half else 1.0 / min(t + 1, w)
    put("invcnt", np.broadcast_to(ic.reshape(1, 128), (128, 128)))
    return P


def make_mask():
    ki = np.arange(128)[:, None]
    x = np.arange(MASKW)[None, :]
    d = x - ki - 384
    m = ((d >= 0) & (d <= 128)).astype(np.float32)
    m += ((d >= 0) & (d % 4 == 0) & (d <= 512)).astype(np.float32)
    m += ((d >= 0) & (d % 16 == 0) & (d <= 2048)).astype(np.float32)
    return m


class Trk:
    CH = 4000

    def __init__(self, nc, stack):
        self.nc = nc
        self.stack = stack
        self.eng = dict(pe=nc.tensor, act=nc.scalar, dve=nc.vector, pool=nc.gpsimd, sp=nc.sync)
        self.cnt = {e: 0 for e in self.eng}
        self.sems = {e: [] for e in self.eng}
        self.seen = {e: {f: 0 for f in self.eng} for e in self.eng}
        self.snap = {e: {} for e in self.eng}
        self.dsem = {}
        self.dcnt = {}
        self.dseen = {e: {} for e in self.eng}
        self.lastw = {}
        self.readers = {}
        self.dma_tokens = set()
        self.nops = 0
        self.stream = {e: [] for e in self.eng}

    def _sem(self, e, seq):
        k = (seq - 1) // self.CH
        while len(self.sems[e]) <= k:
            self.sems[e].append(
                self.stack.enter_context(self.nc.semaphore(f"s_{e}_{len(self.sems[e])}")))
        return self.sems[e][k], (seq - 1) % self.CH + 1

    def _wait(self, e, h):
        if h[0] == "eng":
            _, f, s = h
            if f == e and (e == "pe" or not SAME_ENGINE_SYNC):
                return
            if self.seen[e][f] >= s:
                return
            sem, v = self._sem(f, s)
            self.eng[e].wait_ge(sem, v)
            self.stream[e].append(("w", ("eng", f, (s - 1) // self.CH), v))
            self.seen[e][f] = s
            sn = self.snap[f].get(s)
            if sn and f != e:
                for g, v2 in sn.items():
                    if g != e and v2 > self.seen[e][g]:
                        self.seen[e][g] = v2
        else:
            _, key, v = h
            if self.dseen[e].get(key, 0) >= v:
                return
            self.eng[e].wait_ge(self.dsem[key], v)
            self.stream[e].append(("w", ("dma", key), v))
            self.dseen[e][key] = v

    def _deps(self, e, r, w):
        hs = []
        for t in r:
            h = self.lastw.get(t)
            if h:
                hs.append(h)
        for t in w:
            h = self.lastw.get(t)
            if h:
                hs.append(h)
            hs.extend(self.readers.get(t, ()))
        best = {}
        for h in hs:
            k = (h[0], h[1])
            if k not in best or h[2] > best[k][2]:
                best[k] = h
        for h in best.values():
            self._wait(e, h)

    def _commit(self, h, r, w):
        for t in r:
            self.readers.setdefault(t, []).append(h)
        for t in w:
            self.lastw[t] = h
            self.readers[t] = []

    def op(self, e, fn, r=(), w=()):
        self._deps(e, r, w)
        ins = fn(self.eng[e])
        self.cnt[e] += 1
        s = self.cnt[e]
        sem, v = self._sem(e, s)
        ins.then_inc(sem, 1)
        self.stream[e].append(("i", ("eng", e, (s - 1) // self.CH), 1))
        self.snap[e][s] = dict(self.seen[e])
        self._commit(("eng", e, s), r, w)
        self.nops += 1
        return ins

    def mmg(self, out, pairs, r=(), w=()):
        self._deps("pe", r, w)
        n = len(pairs)
        ins = None
        for i, (lhsT, rhs) in enumerate(pairs):
            ins = self.nc.tensor.matmul(out, lhsT, rhs, start=(i == 0), stop=(i == n - 1))
        self.cnt["pe"] += 1
        s = self.cnt["pe"]
        sem, v = self._sem("pe", s)
        ins.then_inc(sem, 1)
        self.stream["pe"].append(("i", ("eng", "pe", (s - 1) // self.CH), 1))
        self.snap["pe"][s] = dict(self.seen["pe"])
        self._commit(("eng", "pe", s), r, w)
        self.nops += n

    def mm1(self, out, lhsT, rhs, start, stop, r=(), w=()):
        self._deps("pe", r, w)
        ins = self.nc.tensor.matmul(out, lhsT, rhs, start=start, stop=stop)
        self.cnt["pe"] += 1
        s = self.cnt["pe"]
        sem, v = self._sem("pe", s)
        ins.then_inc(sem, 1)
        self.stream["pe"].append(("i", ("eng", "pe", (s - 1) // self.CH), 1))
        self.snap["pe"][s] = dict(self.seen["pe"])
        self._commit(("eng", "pe", s), r, w)
        self.nops += 1

    def dma(self, q, key, out, in_, r=(), w=()):
        if key not in self.dsem:
            self.dsem[key] = self.stack.enter_context(self.nc.semaphore(f"d_{key}"))
            self.dcnt[key] = 0
        self._deps(q, r, w)
        self.eng[q].dma_start(out=out, in_=in_).then_inc(self.dsem[key], 16)
        self.stream[q].append(("i", ("dma", key), 16))
        self.dcnt[key] += 16
        h = ("dma", key, self.dcnt[key])
        self._commit(h, r, w)
        self.dma_tokens.update(r)
        self.dma_tokens.update(w)
        return h

    def barrier(self):
        es = ["pe", "act", "dve"]
        for e in es:
            for f in es:
                if f != e and self.cnt[f] > self.seen[e][f]:
                    self._wait(e, ("eng", f, self.cnt[f]))
        for t in list(self.lastw):
            if t in self.dma_tokens:
                continue
            del self.lastw[t]
            self.readers.pop(t, None)
        for t in list(self.readers):
            if t not in self.dma_tokens and t not in self.lastw:
                del self.readers[t]

    def check_deadlock(self):
        val = {}
        ptr = {e: 0 for e in self.eng}
        prog = True
        while prog:
            prog = False
            for e in self.eng:
                st = self.stream[e]
                while ptr[e] < len(st):
                    k, key, v = st[ptr[e]]
                    if k == "w":
                        if val.get(key, 0) < v:
                            break
                    else:
                        val[key] = val.get(key, 0) + v
                    ptr[e] += 1
                    prog = True
        stuck = {e: (ptr[e], len(self.stream[e]), self.stream[e][ptr[e]], val.get(self.stream[e][ptr[e]][1], 0))
                 for e in self.eng if ptr[e] < len(self.stream[e])}
        assert not stuck, ("DEADLOCK", stuck)

    def wait_all(self, e):
        for t, h in self.lastw.items():
            self._wait(e, h)
        for t, hs in self.readers.items():
            for h in hs:
                self._wait(e, h)


def build(layers):
    nc = bass.Bass(target_bir_lowering=False)
    stack = ExitStack()

    def din(name, shape):
        return nc.dram_tensor(name, list(shape), F32, kind="ExternalInput").ap()

    x_d = din("x", (128, 8, T))
    xh_d = din("xh", (128, 8, HALO))
    mem_d = din("mem", (128, 8, 256))
    par_d = din("params", (128, NP))
    mask_d = din("mask", (128, MASKW))
    w_in_d = din("ev_w_in", (2, 1024, 2560))
    gaw_d = din("ev_gate_a_w", (2, 4, 128, 128))
    gxw_d = din("ev_gate_x_w", (2, 4, 128, 128))
    w_out_d = din("ev_w_out", (2, 1024, 1024))
    poolw_d = din("od_pool_w", (2, 4, 256, 256))
    xwq_d = din("xa_w_q", (4, 1024, 1024))
    xwkv_d = din("xa_w_kv", (4, 1024, 2048))
    xwo_d = din("xa_w_o", (4, 1024, 1024))
    w1_d = din("mlp_w1", (4, 1024, 4096))
    w2_d = din("mlp_w2", (4, 4096, 1024))
    y_d = nc.dram_tensor("y", [128, 8, T], F32, kind="ExternalOutput").ap()

    ncc_kv = sum(1 for l in layers if l % 2 == 0)
    b0 = [nc.dram_tensor(f"b0_{i}", [128, 128], F32) for i in range(len(layers))]
    g0 = [nc.dram_tensor(f"g0_{i}", [256, 128], F32) for i in range(len(layers))]
    b2 = [[nc.dram_tensor(f"b2_{i}_{h}", [256, 2048], BF16) for h in range(4)] for i in range(ncc_kv)]
    g2 = [[nc.dram_tensor(f"g2_{i}_{h}", [512, 2048], BF16) for h in range(4)] for i in range(ncc_kv)]
    b3 = [nc.dram_tensor(f"b3_{i}", [128, 4], F32) for i in range(ncc_kv)]
    g3 = [nc.dram_tensor(f"g3_{i}", [256, 4], F32) for i in range(ncc_kv)]

    def sb(name, shape, dt):
        return stack.enter_context(nc.sbuf_tensor(name, list(shape), dt))

    xres = sb("xres", (128, 8, T), F32)
    par = sb("par", (128, NP), F32)
    xhalo = sb("xhalo", (128, 8, HALO), F32)
    hprev = sb("hprev", (128, 8, HALO), F32)
    tf = sb("tf", (128, 8, 528), F32)
    sm = sb("sm", (128, 96), F32)
    hb = sb("hb", (128, 8 * HW), BF16)
    qk = sb("qk", (128, 8, T), BF16)
    vv = sb("vv", (128, 16, 512), BF16)
    NSLOT = 3
    wsl = sb("wsl", (128, NSLOT, 4096), BF16)
    bt = sb("bt", (128, 9, 512), BF16)
    yb = bt[:, 2:6, :].rearrange("p a b -> p (a b)")
    maskt = sb("maskt", (128, MASKW), BF16)
    memn = sb("memn", (128, 8, 256), BF16)
    ones = sb("ones", (128, 4, 128), BF16)
    ps = [stack.enter_context(nc.psum_tensor(f"ps{i}", [128, 512], F32)) for i in range(8)]

    hbuf = hb[:, :].rearrange("p (c t) -> p c t", c=8)
    kctx = hb[:, 0:8192].rearrange("p (h t) -> p h t", h=4)
    vctx = hb[:, 8192:16384].rearrange("p (b f) -> p b f", b=16)

    tk = Trk(nc, stack)
    op, mmg, dma = tk.op, tk.mmg, tk.dma

    def pcol(name, i=0, n=1):
        o, w = POFF[name]
        return par[:, o + i:o + i + n]

    class WStream:
        def __init__(self):
            self.blocks = []
            self.issued = 0
            self.used = 0
            self.released = set()
            self.cur = {}

        def add(self, tag, parts):
            self.blocks.append((tag, parts))

        def _issue(self, i):
            tag, parts = self.blocks[i]
            s = i % NSLOT
            for (lo, shape, src) in parts:
                n = int(np.prod(shape))
                dst = wsl[:, s, lo:lo + n]
                if len(shape) == 1:
                    pass
                elif len(shape) == 2:
                    dst = dst.rearrange("p (a b) -> p a b", a=shape[0])
                elif len(shape) == 3:
                    dst = dst.rearrange("p (a b c) -> p a b c", a=shape[0], b=shape[1])
                dma("pool", f"w{s}", dst, src, w=[("w", s)])

        def _pump(self):
            while self.issued < len(self.blocks) and (
                    self.issued < NSLOT or (self.issued - NSLOT) in self.released):
                self._issue(self.issued)
                self.issued += 1

        def next(self, tag):
            i = self.used
            assert self.blocks[i][0] == tag, (self.blocks[i][0], tag)
            self._pump()
            assert self.issued > i, ("weight slot not released", tag)
            self.used += 1
            s = i % NSLOT
            self.cur[s] = i
            return s, wsl[:, s, :]

        def release(self, s):
            self.released.add(self.cur[s])
            self._pump()

    ws = WStream()

    def wview(s, a, b, lo=0):
        return wsl[:, s, lo:lo + a * b].rearrange("p (a b) -> p a b", a=a)

    def cols_block(wd2, c0, n=512):
        return wd2.rearrange("(kc p) n -> p kc n", p=128)[:, :, c0:c0 + n]

    def rows_block(wd2, r0, nchunks):
        return wd2[r0:r0 + nchunks * 128, :].rearrange("(j p) n -> p j n", p=128)

    def schedule_layer(l):
        if l % 2 == 0:
            e = l // 2
            w = w_in_d[e]
            for c in range(4):
                ws.add(f"rg{l}_{c}", [(0, (8, 128), cols_block(w, 1536 + c * 128, 128)),
                                      (1024, (8, 128), cols_block(w, 2048 + c * 128, 128)),
                                      (2048, (128,), gaw_d[e, c]),
                                      (2176, (128,), gxw_d[e, c])])
            ws.add(f"v{l}", [(0, (8, 512), cols_block(w, 1024))])
            ws.add(f"woy{l}", [(0, (4, 1024), rows_block(w_out_d[e], 512, 4))])
            ws.add(f"k{l}", [(0, (8, 512), cols_block(w, 512))])
            ws.add(f"q{l}", [(0, (8, 512), cols_block(w, 0))])
            ws.add(f"woa{l}", [(0, (4, 1024), rows_block(w_out_d[e], 0, 4))])
        else:
            o = l // 2
            ws.add(f"pool{l}", [(i * 1024, (4, 256),
                                 poolw_d[o][:, i * 128:(i + 1) * 128, :].rearrange("g p n -> p g n"))
                                for i in range(2)])
        kv = xwkv_d[l]
        for j in range(2):
            ws.add(f"xk{l}_{j}", [(0, (8, 512), cols_block(kv, j * 512))])
        for j in range(2):
            ws.add(f"xv{l}_{j}", [(0, (8, 512), cols_block(kv, 1024 + j * 512))])
        for j in range(2):
            ws.add(f"xq{l}_{j}", [(0, (8, 512), cols_block(xwq_d[l], j * 512))])
        for j in range(2):
            ws.add(f"xo{l}_{j}", [(0, (4, 1024), rows_block(xwo_d[l], j * 512, 4))])
        for g in range(8):
            ws.add(f"w1_{l}_{g}", [(0, (8, 512), cols_block(w1_d[l], g * 512))])
            ws.add(f"w2_{l}_{g}", [(0, (4, 1024), rows_block(w2_d[l], g * 512, 4))])

    for l in layers:
        schedule_layer(l)

    def X(c, tt):
        return ("x", c, tt)

    def HB(c, tt):
        return ("hb", c, tt)

    def QK(c, tt):
        return ("qk", c, tt)

    ALLHB = [("hb", c, tt) for c in range(8) for tt in range(4)] + [("hbh",)]

    def act_fn(out, in_, func, **kw):
        return lambda e: e.activation(out=out, in_=in_, func=func, **kw)

    def emit_rstd(psum_ap, dst, r, w):
        op("act", act_fn(dst, psum_ap, AF.Ln, bias=EPSC), r=list(r) + [("epsc",)], w=w)
        op("act", act_fn(dst, dst, AF.Exp, scale=-0.5), r=w, w=w)

    def emit_recip(psum_ap, dst, r, w):
        op("act", act_fn(dst, psum_ap, AF.Ln), r=list(r), w=w)
        op("act", act_fn(dst, dst, AF.Exp, scale=-1.0), r=w, w=w)

    def stt(out, in0, scalar, in1, op0, op1):
        return lambda e: e.scalar_tensor_tensor(out, in0, scalar, in1, op0, op1)

    def tt_(out, in0, in1, opx):
        return lambda e: e.tensor_tensor(out, in0, in1, opx)

    def ts_(out, in0, s1, s2, op0, op1=None):
        if op1 is None:
            return lambda e: e.tensor_scalar(out, in0, s1, None, op0)
        return lambda e: e.tensor_scalar(out, in0, s1, s2, op0, op1)

    EPSC = sm[:, 1:2]
    ONE_D, ONE_128, ONE_256, ONE_1 = (ones[:, i, :] for i in range(4))
    sq = [bt[:, 0, :], bt[:, 1, :]]
    eb = [bt[:, 2, :], bt[:, 3, :]]
    pb = [bt[:, 4, :], bt[:, 5, :]]
    SQ = [("sq", 0), ("sq", 1)]
    EB = [("eb", 0), ("eb", 1)]
    PB = [("pb", 0), ("pb", 1)]
    PS = [("ps", i) for i in range(8)]
    TF = [("tf", i) for i in range(8)]

    for tt in range(NT):
        dma("sp", f"xin{tt}", xres[:, :, tile_sl(tt)], x_d[:, :, tile_sl(tt)],
            w=[X(c, tt) for c in range(8)])
    dma("sp", "par", par[:, :], par_d[:, :], w=[("par",)])
    dma("sp", "xh", xhalo[:, :, :], xh_d[:, :, :], w=[("xhalo",)])
    memf = tf[:, 0:8, 0:256]
    dma("sp", "mem", memf, mem_d[:, :, :], w=TF[0:8])
    dma("pool", "mask", maskt[:, :], mask_d[:, :], w=[("mask",)])
    for i, val in enumerate([1.0 / 1024, 1.0 / 128, 1.0 / 256, 1.0]):
        op("dve", lambda e, i=i, val=val: e.memset(ones[:, i, :], val), w=[("ones",)])
    op("dve", lambda e: e.memset(bt[:, 6, :], 0.0), w=[("sm0",)])
    ZT = bt[:, 6, :]
    op("dve", lambda e: e.memset(sm[:, 1:2], EPS), w=[("epsc",)])

    sqm = qk[:, 0, 0:2048].rearrange("p (c m) -> p c m", c=8)
    for c in range(8):
        op("act", act_fn(sqm[:, c, :], memf[:, c, :], AF.Square), r=TF[0:8], w=[QK(0, 0)])
    mmg(ps[0][:, 0:256], [(ONE_D, sqm[:, c, :]) for c in range(8)],
        r=[QK(0, 0), ("ones",)], w=[PS[0]])
    emit_rstd(ps[0][:, 0:256], tf[:, 4, 256:512], r=[PS[0]], w=[("mrs",)])
    for c in range(8):
        op("dve", stt(memn[:, c, :], memf[:, c, :], pcol("memg", c), tf[:, 4, 256:512],
                      ALU.mult, ALU.mult),
           r=TF[0:8] + [("mrs",), ("par",)], w=[("memn",)])
    tk.barrier()

    def halo_prep(gname, to_hbuf):
        op("dve", ts_(xhalo[:, :, :], xhalo[:, :, :], pcol("flag"), None, ALU.mult),
           r=[("par",)], w=[("xhalo",)])
        sqh = bt[:, 0, 0:128].rearrange("p (c t) -> p c t", c=8)
        op("act", act_fn(sqh, xhalo[:, :, :], AF.Square), r=[("xhalo",)], w=[SQ[0]])
        mmg(ps[7][:, 0:HALO], [(ONE_D, sqh[:, c, :]) for c in range(8)],
            r=[SQ[0], ("ones",)], w=[PS[7]])
        rh = sm[:, 16:32]
        emit_rstd(ps[7][:, 0:HALO], rh, r=[PS[7]], w=[("rh",)])
        for c in range(8):
            dst = hbuf[:, c, 0:HALO] if to_hbuf else hprev[:, c, :]
            op("dve", stt(dst, xhalo[:, c, :], pcol(gname, c), rh, ALU.mult, ALU.mult),
               r=[("xhalo",), ("rh",), ("par",)], w=[("hbh",)] if to_hbuf else [("hprev", c)])

    def norm_tile_rstd(tt, dst_tf):
        for c in range(8):
            op("act", act_fn(sq[c % 2], xres[:, c, tile_sl(tt)], AF.Square),
               r=[X(c, tt)], w=[SQ[c % 2]])
            tk.mm1(ps[7 - tt % 2][:, :], ONE_D, sq[c % 2], c == 0, c == 7, r=[SQ[c % 2], ("ones",)],
                   w=[PS[7 - tt % 2]])
        emit_rstd(ps[7 - tt % 2][:, :], tf[:, dst_tf, 0:512], r=[PS[7 - tt % 2]], w=[TF[dst_tf]])

    def norm_to_hbuf(gname):
        for tt in range(NT):
            ri = 7 - tt % 2
            norm_tile_rstd(tt, ri)
            for c in range(8):
                op("dve", stt(hbuf[:, c, HALO + tt * TT:HALO + (tt + 1) * TT],
                              xres[:, c, tile_sl(tt)], pcol(gname, c), tf[:, ri, 0:512],
                              ALU.mult, ALU.mult),
                   r=[X(c, tt), TF[ri], ("par",)], w=[HB(c, tt)])

    def hslice(c, tt):
        return hbuf[:, c, HALO + tt * TT:HALO + (tt + 1) * TT]

    def add_to_x(m, tt, psum_ap, psi):
        op("dve", tt_(xres[:, m, tile_sl(tt)], psum_ap, xres[:, m, tile_sl(tt)], ALU.add),
           r=[PS[psi]], w=[X(m, tt)])

    cc_count = [0]

    def collective(src_d, dst_d, wait_handles):
        for h in wait_handles:
            tk._wait("pool", h)
        sem = stack.enter_context(nc.semaphore(f"cc{cc_count[0]}"))
        cc_count[0] += 1
        nc.gpsimd.collective_compute(
            "AllGather", ALU.bypass, replica_groups=GROUPS,
            ins=[src_d.ap().opt()], outs=[dst_d.ap().opt()]).then_inc(sem)
        tk.stream["pool"].append(("i", ("cc", id(sem)), 1))
        return sem

    def proj_headnorm(wt, wtok, gcol, base):
        its = [(hd, tt) for hd in range(4) for tt in range(NT)]

        def front(i):
            hd, tt = its[i]
            pa = i % 3
            mmg(ps[pa][:, :], [(wt[:, kc, hd * 128:(hd + 1) * 128], hslice(kc, tt)) for kc in range(8)],
                r=[HB(kc, tt) for kc in range(8)] + [wtok], w=[PS[pa]])
            op("act", act_fn(sq[i % 2], ps[pa][:, :], AF.Square), r=[PS[pa]], w=[SQ[i % 2]])

        def back(i):
            hd, tt = its[i]
            pa, pn, ti = i % 3, 3 + i % 2, i % 2
            mmg(ps[pn][:, :], [(ONE_128, sq[i % 2])], r=[SQ[i % 2], ("ones",)], w=[PS[pn]])
            emit_rstd(ps[pn][:, :], tf[:, ti, 0:512], r=[PS[pn]], w=[TF[ti]])
            op("dve", stt(qk[:, base + hd, tile_sl(tt)], ps[pa][:, :], gcol, tf[:, ti, 0:512],
                          ALU.mult, ALU.mult), r=[PS[pa], TF[ti], ("par",)], w=[QK(base + hd, tt)])

        for i in range(len(its) + 1):
            if i < len(its):
                front(i)
            if i >= 1:
                back(i - 1)

    def even_mixer(l, li, ei):
        e = l // 2
        halo_prep(f"mixg{l}", True)
        norm_to_hbuf(f"mixg{l}")
        sp8 = sm[:, 4:8]
        op("act", act_fn(sp8, pcol(f"lam{l}", 0, 4), AF.Exp, scale=-1.0), r=[("par",)], w=[("sp8",)])
        op("act", act_fn(sp8, sp8, AF.Ln, bias=1.0), r=[("sp8",)], w=[("sp8",)])
        op("dve", ts_(sp8, sp8, -8.0, None, ALU.mult), r=[("sp8",)], w=[("sp8",)])

        hfin = sm[:, 8:12]
        hA = sm[:, 32:36]
        tk.barrier()
        vf = vv[:, :, :].rearrange("p a b -> p (a b)").bitcast(F32)
        VR = lambda r: vf[:, r * 528:(r + 1) * 528]
        CH = [
            dict(xrt=[tf[:, 0, 0:515], tf[:, 1, 0:515]], XT=[TF[0], TF[1]],
                 gx=tf[:, 2, 0:512], GX=TF[2], g3=tf[:, 3, 0:512], G3=TF[3],
                 ra=tf[:, 4, 0:512], RA=TF[4], ri=tf[:, 5, 0:512], RI=TF[5],
                 xc=tf[:, 6, 0:512], XC=TF[6], pxr=0, pgt=1, pga=4, pgx=5, sq=0,
                 hc=sm[:, 12:13], HC=("hc", 0), pc=sm[:, 13:14], PC=("pc", 0),
                 xrh=sm[:, 40:56], XRH=("xrh", 0)),
            dict(xrt=[VR(0)[:, 0:515], VR(1)[:, 0:515]], XT=[("vf", 0), ("vf", 1)],
                 gx=VR(2)[:, 0:512], GX=("vf", 2), g3=VR(3)[:, 0:512], G3=("vf", 3),
                 ra=VR(4)[:, 0:512], RA=("vf", 4), ri=VR(5)[:, 0:512], RI=("vf", 5),
                 xc=VR(6)[:, 0:512], XC=("vf", 6), pxr=2, pgt=3, pga=6, pgx=7, sq=1,
                 hc=sm[:, 14:15], HC=("hc", 1), pc=sm[:, 15:16], PC=("pc", 1),
                 xrh=sm[:, 64:80], XRH=("xrh", 1)),
        ]

        def chain_ops(q, c, tt, srg):
            B = CH[q]
            p = tt % 2
            xt, XTp = B["xrt"][p], B["XT"][p]
            xprev, XTq = B["xrt"][1 - p], B["XT"][1 - p]
            gx, g3_, ra, ri, xc = B["gx"], B["g3"], B["ra"], B["ri"], B["xc"]
            GX, G3, RA, RI, XC = B["GX"], B["G3"], B["RA"], B["RI"], B["XC"]
            hc, pc, HC, PC = B["hc"], B["pc"], B["HC"], B["PC"]
            sqb, SQB = sq[B["sq"]], SQ[B["sq"]]
            pxr, pgt, pga, pgx = B["pxr"], B["pgt"], B["pga"], B["pgx"]
            wxr = wview(srg, 8, 128, 0)
            wgt = wview(srg, 8, 128, 1024)
            wga = wsl[:, srg, 2048:2176]
            wgx = wsl[:, srg, 2176:2304]
            W = ("w", srg)
            cw = lambda j: pcol(f"convw{l}", j * 4 + c)
            hbr = [HB(kc, tt) for kc in range(8)]
            L = []
            A = L.append
            if tt == 0:
                A(lambda: mmg(ps[pxr][:, 0:16], [(wxr[:, kc, :], hbuf[:, kc, 0:HALO]) for kc in range(8)],
                              r=[("hbh",), W], w=[PS[pxr]]))
                A(lambda: op("act", act_fn(B["xrh"], ps[pxr][:, 0:16], AF.Copy), r=[PS[pxr]], w=[B["XRH"]]))
                A(lambda: op("dve", lambda e_: e_.memset(hc, 0.0), w=[HC]))
                A(lambda: op("dve", lambda e_: e_.memset(pc, 1.0), w=[PC]))
            A(lambda: mmg(ps[pxr][:, :], [(wxr[:, kc, :], hslice(kc, tt)) for kc in range(8)], r=hbr + [W], w=[PS[pxr]]))
            A(lambda: mmg(ps[pgt][:, :], [(wgt[:, kc, :], hslice(kc, tt)) for kc in range(8)], r=hbr + [W], w=[PS[pgt]]))
            if tt == 0:
                A(lambda: op("dve", lambda e_: e_.tensor_copy(xt[:, 0:3], B["xrh"][:, 13:16]), r=[B["XRH"]], w=[XTp]))
            else:
                A(lambda: op("dve", lambda e_: e_.tensor_copy(xt[:, 0:3], xprev[:, 512:515]), r=[XTq], w=[XTp]))
            A(lambda: op("act", act_fn(xt[:, 3:515], ps[pxr][:, :], AF.Copy), r=[PS[pxr]], w=[XTp]))
            A(lambda: op("act", act_fn(gx, ps[pgt][:, :], AF.Copy), r=[PS[pgt]], w=[GX]))
            A(lambda: op("act", act_fn(g3_, gx, AF.Square), r=[GX], w=[G3]))
            A(lambda: op("act", act_fn(xc, xt[:, 3:515], AF.Identity, scale=cw(3), bias=pcol(f"convb{l}", c)),
                         r=[XTp, ("par",)], w=[XC]))
            A(lambda: op("dve", ts_(g3_, g3_, 0.044715, 1.0, ALU.mult, ALU.add), r=[G3], w=[G3]))
            A(lambda: op("dve", stt(xc, xt[:, 0:512], cw(0), xc, ALU.mult, ALU.add), r=[XTp, XC, ("par",)], w=[XC]))
            A(lambda: op("dve", tt_(g3_, g3_, gx, ALU.mult), r=[GX, G3], w=[G3]))
            A(lambda: op("dve", stt(xc, xt[:, 1:513], cw(1), xc, ALU.mult, ALU.add), r=[XTp, XC, ("par",)], w=[XC]))
            A(lambda: op("act", act_fn(g3_, g3_, AF.Sigmoid, scale=1.5957691216057308), r=[G3], w=[G3]))
            A(lambda: op("dve", stt(xc, xt[:, 2:514], cw(2), xc, ALU.mult, ALU.add), r=[XTp, XC, ("par",)], w=[XC]))
            A(lambda: op("act", act_fn(sqb, xc, AF.Copy), r=[XC], w=[SQB]))
            A(lambda: op("dve", tt_(gx, gx, g3_, ALU.mult), r=[GX, G3], w=[GX]))
            A(lambda: mmg(ps[pga][:, :], [(wga, sqb)], r=[SQB, W], w=[PS[pga]]))
            A(lambda: mmg(ps[pgx][:, :], [(wgx, sqb)], r=[SQB, W], w=[PS[pgx]]))
            A(lambda: op("act", act_fn(ra, ps[pga][:, :], AF.Sigmoid, bias=pcol(f"gab{l}", c)),
                         r=[PS[pga], ("par",)], w=[RA]))
            A(lambda: op("act", act_fn(ri, ps[pgx][:, :], AF.Sigmoid, bias=pcol(f"gxb{l}", c)),
                         r=[PS[pgx], ("par",)], w=[RI]))
            A(lambda: op("act", act_fn(ra, ra, AF.Exp, scale=sp8[:, c:c + 1]), r=[RA, ("sp8",)], w=[RA]))
            A(lambda: op("dve", tt_(ri, ri, xc, ALU.mult), r=[RI, XC], w=[RI]))
            A(lambda: op("dve", tt_(g3_, ra, ra, ALU.mult), r=[RA], w=[G3]))
            A(lambda: op("act", act_fn(g3_, g3_, AF.Sqrt, scale=-1.0, bias=1.0), r=[G3], w=[G3]))
            A(lambda: op("dve", lambda e_: e_.tensor_tensor_scan(xc, ra, ZT, pc, ALU.mult, ALU.add),
                         r=[RA, RI, PC, ("sm0",)], w=[XC]))
            A(lambda: op("dve", tt_(ri, ri, g3_, ALU.mult), r=[RI, G3], w=[RI]))
            A(lambda: op("dve", lambda e_: e_.tensor_copy(pc, xc[:, 511:512]), r=[XC], w=[PC]))
            A(lambda: op("dve", lambda e_: e_.tensor_tensor_scan(g3_, ra, ri, hc, ALU.mult, ALU.add),
                         r=[RA, RI, HC], w=[G3]))
            A(lambda: op("dve", tt_(qk[:, c, tile_sl(tt)], xc, gx, ALU.mult), r=[XC, GX], w=[QK(c, tt)]))
            A(lambda: op("dve", lambda e_: e_.tensor_copy(hc, g3_[:, 511:512]), r=[G3], w=[HC]))
            A(lambda: op("dve", tt_(qk[:, 4 + c, tile_sl(tt)], g3_, gx, ALU.mult), r=[G3, GX], w=[QK(4 + c, tt)]))
            if tt == NT - 1:
                A(lambda: op("dve", lambda e_: e_.tensor_copy(hfin[:, c:c + 1], hc), r=[HC], w=[("hfin",)]))
            return L

        for pair in ((0, 1), (2, 3)):
            srgs = [ws.next(f"rg{l}_{c}")[0] for c in pair]
            for tt in range(NT):
                lists = [chain_ops(q, c, tt, srgs[q]) for q, c in enumerate(pair)]
                for k in range(max(len(x_) for x_ in lists)):
                    for x_ in lists:
                        if k < len(x_):
                            x_[k]()
            for sg in srgs:
                ws.release(sg)
        h3 = dma("pool", f"b3_{ei}", b3[ei].ap(), hfin, r=[("hfin",)])
        cc3 = collective(b3[ei], g3[ei], [h3])
        tk.barrier()

        sv, _ = ws.next(f"v{l}")
        wv = wview(sv, 8, 512)
        for blk in range(16):
            pa = 4 + blk % 2
            tt = blk // 4
            mmg(ps[pa][:, :],
                [(hbuf[:, kc, HALO + blk * 128:HALO + (blk + 1) * 128], wv[:, kc, :]) for kc in range(8)],
                r=[HB(kc, tt) for kc in range(8)] + [("w", sv)], w=[PS[pa]])
            op("act", act_fn(vv[:, blk, :], ps[pa][:, :], AF.Copy), r=[PS[pa]], w=[("vv", blk)])
        ws.release(sv)

        tk.eng["pool"].wait_ge(cc3, 1)
        tk.stream["pool"].append(("w", ("cc", id(cc3)), 1))
        dma("pool", f"g3_{ei}", hA, g3[ei].ap()[0:128, :], w=[("hA",)])
        op("dve", ts_(hA, hA, pcol("flag"), None, ALU.mult), r=[("hA",), ("par",)], w=[("hA",)])
        for c in range(4):
            for tt in range(NT):
                op("dve", stt(qk[:, 4 + c, tile_sl(tt)], qk[:, c, tile_sl(tt)], hA[:, c:c + 1],
                              qk[:, 4 + c, tile_sl(tt)], ALU.mult, ALU.add),
                   r=[QK(c, tt), QK(4 + c, tt), ("hA",)], w=[QK(4 + c, tt)])
        swy, _ = ws.next(f"woy{l}")
        woy = wview(swy, 4, 1024)
        for m in range(8):
            for tt in range(NT):
                pi = 6 + (m * NT + tt) % 2
                mmg(ps[pi][:, :], [(woy[:, c, m * 128:(m + 1) * 128], qk[:, 4 + c, tile_sl(tt)]) for c in range(4)],
                    r=[QK(4 + c, tt) for c in range(4)] + [("w", swy)], w=[PS[pi]])
                add_to_x(m, tt, ps[pi][:, :], pi)
        ws.release(swy)

        sk, _ = ws.next(f"k{l}")
        wk = wview(sk, 8, 512)
        proj_headnorm(wk, ("w", sk), pcol(f"kg{l}"), 4)
        ws.release(sk)
        cc2 = []
        for hd in range(4):
            h2a = dma("pool", f"b2k_{ei}_{hd}", b2[ei][hd].ap()[0:128, :], qk[:, 4 + hd, :],
                      r=[QK(4 + hd, tt) for tt in range(4)])
            h2b = dma("pool", f"b2v_{ei}_{hd}",
                      b2[ei][hd].ap()[128:256, :].rearrange("p (b f) -> p b f", b=16),
                      vv[:, :, hd * 128:(hd + 1) * 128], r=[("vv", blk) for blk in range(16)])
            cc2.append(collective(b2[ei][hd], g2[ei][hd], [h2a, h2b]))

        sq_, _ = ws.next(f"q{l}")
        wq = wview(sq_, 8, 512)
        proj_headnorm(wq, ("w", sq_), pcol(f"qg{l}"), 0)
        ws.release(sq_)
        tk.barrier()
        for f in ("pe", "act", "dve"):
            tk._wait("pool", ("eng", f, tk.cnt[f]))
        for hd in range(4):
            tk.eng["pool"].wait_ge(cc2[hd], 1)
            tk.stream["pool"].append(("w", ("cc", id(cc2[hd])), 1))
            dma("pool", f"ctxk_{ei}_{hd}", kctx[:, hd, :], g2[ei][hd].ap()[0:128, :],
                w=[("kctx", hd)] + ALLHB)
            dma("pool", f"ctxv_{ei}_{hd}", vctx[:, :, hd * 128:(hd + 1) * 128],
                g2[ei][hd].ap()[128:256, :].rearrange("p (b f) -> p b f", b=16),
                w=[("vctx", hd)] + ALLHB)

        scale = 128.0 ** -0.5
        ebs = [bt[:, 2, :], bt[:, 3, :], bt[:, 0, :], bt[:, 7, :]]
        pbs = [bt[:, 4, :], bt[:, 5, :], bt[:, 1, :], bt[:, 8, :]]
        EBS = [("eb", 0), ("eb", 1), ("sq", 0), ("bt7",)]
        PBS = [("pb", 0), ("pb", 1), ("sq", 1), ("bt8",)]
        SBK = [0, 1, 2, 7]
        items = []
        gi = 0
        for hd in range(4):
            for qt in reversed(range(NT)):
                blocks = [("own", kb) for kb in range(0, 4 * qt + 4)] + [("ctx", kb) for kb in range(4 * qt, 16)]
                for bi, (kind, kb) in enumerate(blocks):
                    items.append((hd, qt, gi, bi, len(blocks), kind, kb))
                gi += 1
        LA = 3

        def att_front(idx):
            hd, qt, g, bi, nb, kind, kb = items[idx]
            si, bi3 = SBK[idx % 4], idx % 4
            if kind == "ctx":
                kT = kctx[:, hd, kb * 128:(kb + 1) * 128]
                rk = [("kctx", hd)]
                d0 = 512 * qt + 2048 - 128 * kb
                bias = pcol("ctxbias")
            else:
                kT = qk[:, 4 + hd, kb * 128:(kb + 1) * 128]
                rk = [QK(4 + hd, kb // 4)]
                d0 = 512 * qt - 128 * kb
                bias = 0.0
            off = d0 + 384
            mmg(ps[si][:, :], [(kT, qk[:, hd, tile_sl(qt)])], r=rk + [QK(hd, qt)], w=[PS[si]])
            op("act", act_fn(ebs[bi3], ps[si][:, :], AF.Exp, scale=scale, bias=bias),
               r=[PS[si], ("par",)], w=[EBS[bi3]])
            op("dve", tt_(pbs[bi3], ebs[bi3], maskt[:, off:off + 512], ALU.mult),
               r=[EBS[bi3], ("mask",)], w=[PBS[bi3]])

        def att_back(idx):
            hd, qt, g, bi, nb, kind, kb = items[idx]
            bi3 = idx % 4
            po, pd = 3 + g % 2, 5 + g % 2
            if kind == "ctx":
                vs = vctx[:, kb, hd * 128:(hd + 1) * 128]
                rv = [("vctx", hd)]
            else:
                vs = vv[:, kb, hd * 128:(hd + 1) * 128]
                rv = [("vv", kb)]
            tk.mm1(ps[po][:, :], vs, pbs[bi3], bi == 0, bi == nb - 1, r=rv + [PBS[bi3]], w=[PS[po]])
            tk.mm1(ps[pd][:, :], ONE_1, pbs[bi3], bi == 0, bi == nb - 1, r=[("ones",), PBS[bi3]], w=[PS[pd]])
            if bi == nb - 1:
                rdi = 6 + g % 2
                rd = tf[:, rdi, 0:512]
                emit_recip(ps[pd][:, :], rd, r=[PS[pd]], w=[TF[rdi]])
                op("dve", tt_(qk[:, hd, tile_sl(qt)], ps[po][:, :], rd, ALU.mult),
                   r=[PS[po], TF[rdi]], w=[QK(hd, qt)])

        for idx in range(len(items) + LA):
            if idx < len(items):
                att_front(idx)
            if idx - LA >= 0:
                att_back(idx - LA)
        swa, _ = ws.next(f"woa{l}")
        woa = wview(swa, 4, 1024)
        for tt in range(NT):
            for m in range(8):
                pi = (tt * 8 + m) % 2
                mmg(ps[pi][:, :], [(woa[:, c, m * 128:(m + 1) * 128], qk[:, c, tile_sl(tt)]) for c in range(4)],
                    r=[QK(c, tt) for c in range(4)] + [("w", swa)], w=[PS[pi]])
                add_to_x(m, tt, ps[pi][:, :], pi)
        ws.release(swa)
        tk.barrier()

    def odd_mixer(l, li):
        halo_prep(f"mixg{l}", False)
        spw, _ = ws.next(f"pool{l}")
        pw = wsl[:, spw, 0:2048].rearrange("p (i g n) -> p i g n", i=2, g=4)
        dt_ = vv[:, 0:8, :]
        for tt in range(NT):
            norm_tile_rstd(tt, 7)
            for g in range(4):
                w = 2 ** (g + 1)
                cs = (2 * g, 2 * g + 1)
                hfs = [tf[:, 0, 0:528], tf[:, 3, 0:528]]
                HFT = [TF[0], TF[3]]
                sbufs = [[(tf[:, 1, 0:528], TF[1]), (tf[:, 2, 0:528], TF[2])],
                         [(tf[:, 4, 0:528], TF[4]), (tf[:, 5, 0:528], TF[5])]]
                for q_, c in enumerate(cs):
                    op("dve", lambda e_, c=c, hf=hfs[q_]: e_.tensor_copy(hf[:, 0:16], hprev[:, c, :]),
                       r=[("hprev", c)], w=[HFT[q_]])
                for q_, c in enumerate(cs):
                    op("dve", stt(hfs[q_][:, 16:528], xres[:, c, tile_sl(tt)], pcol(f"mixg{l}", c), tf[:, 7, 0:512],
                                  ALU.mult, ALU.mult), r=[X(c, tt), TF[7], ("par",)], w=[HFT[q_]])
                for q_, c in enumerate(cs):
                    op("dve", lambda e_, c=c, hf=hfs[q_]: e_.tensor_copy(hprev[:, c, :], hf[:, 512:528]),
                       r=[HFT[q_]], w=[("hprev", c)])
                srcs = [(hfs[0], HFT[0]), (hfs[1], HFT[1])]
                for k in range(g + 1):
                    sh = 2 ** k
                    lo = 2 ** (k + 1) - 1
                    for q_ in range(2):
                        src, srct = srcs[q_]
                        dst, dstt = sbufs[q_][k % 2]
                        op("dve", tt_(dst[:, lo:528], src[:, lo:528], src[:, lo - sh:528 - sh], ALU.add),
                           r=[srct], w=[dstt])
                        srcs[q_] = (dst, dstt)
                for q_, c in enumerate(cs):
                    src, srct = srcs[q_]
                    op("dve", stt(dt_[:, c, :], src[:, 16:528], 1.0 / w, hfs[q_][:, 16:528], ALU.mult, ALU.subtract),
                       r=[srct, HFT[q_]], w=[("dt", c)])
                if tt == 0:
                    o_, _w = POFF["invcnt"]
                    for q_, c in enumerate(cs):
                        src, srct = srcs[q_]
                        t16 = sm[:, 40:56] if q_ == 0 else sm[:, 64:80]
                        op("dve", tt_(t16, src[:, 16:32], par[:, o_ + c * 16:o_ + (c + 1) * 16], ALU.mult),
                           r=[srct, ("par",)], w=[("t16", q_)])
                    for q_, c in enumerate(cs):
                        t16 = sm[:, 40:56] if q_ == 0 else sm[:, 64:80]
                        op("dve", tt_(dt_[:, c, 0:16], t16, hfs[q_][:, 16:32], ALU.subtract),
                           r=[("t16", q_), HFT[q_]], w=[("dt", c)])
            for j in range(8):
                g, jj = j // 2, j % 2
                pi = j % 2
                mmg(ps[pi][:, :], [(pw[:, i, g, jj * 128:(jj + 1) * 128], dt_[:, 2 * g + i, :]) for i in range(2)],
                    r=[("dt", 2 * g), ("dt", 2 * g + 1), ("w", spw)], w=[PS[pi]])
                op("dve", stt(xres[:, j, tile_sl(tt)], ps[pi][:, :], pcol(f"scale{l}", j),
                              xres[:, j, tile_sl(tt)], ALU.mult, ALU.add),
                   r=[PS[pi], ("par",)], w=[X(j, tt)])
        ws.release(spw)
        tk.barrier()

    def xattn(l):
        kx = vv[:, 0:4, :].rearrange("p a (b m) -> p (a b) m", b=2)
        vx = vv[:, 4:8, :].rearrange("p (b j) f -> p b (j f)", b=2)
        for j in range(2):
            sw, _ = ws.next(f"xk{l}_{j}")
            wk = wview(sw, 8, 512)
            for hh in range(2):
                h = 2 * j + hh
                for i in range(2):
                    mmg(ps[i][:, 0:256],
                        [(wk[:, kc, (2 * hh + i) * 128:(2 * hh + i + 1) * 128], memn[:, kc, :]) for kc in range(8)],
                        r=[("memn",), ("w", sw)], w=[PS[i]])
                    op("act", act_fn(sq[i][:, 0:256], ps[i][:, 0:256], AF.Square), r=[PS[i]], w=[SQ[i]])
                mmg(ps[2][:, 0:256], [(ONE_256, sq[0][:, 0:256]), (ONE_256, sq[1][:, 0:256])],
                    r=[SQ[0], SQ[1], ("ones",)], w=[PS[2]])
                emit_rstd(ps[2][:, 0:256], tf[:, 0, 0:256], r=[PS[2]], w=[TF[0]])
                for i in range(2):
                    op("dve", stt(kx[:, 2 * h + i, :], ps[i][:, 0:256], pcol(f"xkg{l}", i), tf[:, 0, 0:256],
                                  ALU.mult, ALU.mult), r=[PS[i], TF[0], ("par",)], w=[("kx",)])
            ws.release(sw)
        for j in range(2):
            sw, _ = ws.next(f"xv{l}_{j}")
            wv = wview(sw, 8, 512)
            for blk in range(2):
                pi = 3 + blk
                mmg(ps[pi][:, :], [(memn[:, kc, blk * 128:(blk + 1) * 128], wv[:, kc, :]) for kc in range(8)],
                    r=[("memn",), ("w", sw)], w=[PS[pi]])
                op("act", act_fn(vx[:, blk, j * 512:(j + 1) * 512], ps[pi][:, :], AF.Copy),
                   r=[PS[pi]], w=[("vx",)])
            ws.release(sw)
        norm_to_hbuf(f"xag{l}")
        sqx = [[bt[:, 0, :], bt[:, 1, :]], [bt[:, 2, :], bt[:, 3, :]], [bt[:, 7, :], bt[:, 8, :]]]
        SQX = [[("sq", 0), ("sq", 1)], [("eb", 0), ("eb", 1)], [("bt7",), ("bt8",)]]
        for j in range(2):
            sw, _ = ws.next(f"xq{l}_{j}")
            wq = wview(sw, 8, 512)
            its = [(hh, tt) for hh in range(2) for tt in range(NT)]

            def qfront(n, its=its, wq=wq, sw=sw):
                hh, tt = its[n]
                p = n % 3
                for i in range(2):
                    bk = 2 * p + i
                    mmg(ps[bk][:, :],
                        [(wq[:, kc, (2 * hh + i) * 128:(2 * hh + i + 1) * 128], hslice(kc, tt)) for kc in range(8)],
                        r=[HB(kc, tt) for kc in range(8)] + [("w", sw)], w=[PS[bk]])
                    op("act", act_fn(sqx[p][i], ps[bk][:, :], AF.Square), r=[PS[bk]], w=[SQX[p][i]])

            def qback(n, its=its, j=j):
                hh, tt = its[n]
                h = 2 * j + hh
                p = n % 3
                pn = 6 + n % 2
                mmg(ps[pn][:, :], [(ONE_256, sqx[p][0]), (ONE_256, sqx[p][1])],
                    r=[SQX[p][0], SQX[p][1], ("ones",)], w=[PS[pn]])
                emit_rstd(ps[pn][:, :], tf[:, p, 0:512], r=[PS[pn]], w=[TF[p]])
                for i in range(2):
                    op("dve", stt(qk[:, 2 * h + i, tile_sl(tt)], ps[2 * p + i][:, :], pcol(f"xqg{l}", i),
                                  tf[:, p, 0:512], ALU.mult, ALU.mult),
                       r=[PS[2 * p + i], TF[p], ("par",)], w=[QK(2 * h + i, tt)])

            for n in range(len(its) + 1):
                if n < len(its):
                    qfront(n)
                if n >= 1:
                    qback(n - 1)
            ws.release(sw)
        scale = 256.0 ** -0.5
        ebx = [[bt[:, 2, :], bt[:, 3, :]], [bt[:, 4, :], bt[:, 5, :]]]
        EBX = [[("eb", 0), ("eb", 1)], [("pb", 0), ("pb", 1)]]
        aits = [(h, tt) for h in range(4) for tt in range(NT)]

        def afront(n):
            h, tt = aits[n]
            p = n % 2
            for blk in range(2):
                bk = blk
                mmg(ps[bk][:, :],
                    [(kx[:, 2 * h + i, blk * 128:(blk + 1) * 128], qk[:, 2 * h + i, tile_sl(tt)]) for i in range(2)],
                    r=[("kx",), QK(2 * h, tt), QK(2 * h + 1, tt)], w=[PS[bk]])
                op("act", act_fn(ebx[p][blk], ps[bk][:, :], AF.Exp, scale=scale), r=[PS[bk]], w=[EBX[p][blk]])

        def aback(n):
            h, tt = aits[n]
            p = n % 2
            pd = 6 + p
            for i in range(2):
                mmg(ps[2 + 2 * p + i][:, :],
                    [(vx[:, blk, h * 256 + i * 128:h * 256 + (i + 1) * 128], ebx[p][blk]) for blk in range(2)],
                    r=[("vx",), EBX[p][0], EBX[p][1]], w=[PS[2 + 2 * p + i]])
            mmg(ps[pd][:, :], [(ONE_1, ebx[p][0]), (ONE_1, ebx[p][1])],
                r=[EBX[p][0], EBX[p][1], ("ones",)], w=[PS[pd]])
            rd = tf[:, 2 + p, 0:512]
            emit_recip(ps[pd][:, :], rd, r=[PS[pd]], w=[TF[2 + p]])
            for i in range(2):
                op("dve", tt_(qk[:, 2 * h + i, tile_sl(tt)], ps[2 + 2 * p + i][:, :], rd, ALU.mult),
                   r=[PS[2 + 2 * p + i], TF[2 + p]], w=[QK(2 * h + i, tt)])

        for n in range(len(aits) + 1):
            if n < len(aits):
                afront(n)
            if n >= 1:
                aback(n - 1)
        s0, _ = ws.next(f"xo{l}_0")
        s1, _ = ws.next(f"xo{l}_1")
        wo = [wview(s0, 4, 1024), wview(s1, 4, 1024)]
        for tt in range(NT):
            for m in range(8):
                pi = 6 + (tt * 8 + m) % 2
                mmg(ps[pi][:, :],
                    [(wo[c // 4][:, c % 4, m * 128:(m + 1) * 128], qk[:, c, tile_sl(tt)]) for c in range(8)],
                    r=[QK(c, tt) for c in range(8)] + [("w", s0), ("w", s1)], w=[PS[pi]])
                add_to_x(m, tt, ps[pi][:, :], pi)
        ws.release(s0)
        ws.release(s1)

    def mlp(l):
        norm_to_hbuf(f"mlpg{l}")
        for g in range(8):
            s1, _ = ws.next(f"w1_{l}_{g}")
            s2, _ = ws.next(f"w2_{l}_{g}")
            w1 = wview(s1, 8, 512)
            w2 = wview(s2, 4, 1024)
            k = 0
            for j in range(4):
                for tt in range(NT):
                    pi = k % 4
                    ei_ = k % 2
                    k += 1
                    mmg(ps[pi][:, :], [(w1[:, kc, j * 128:(j + 1) * 128], hslice(kc, tt)) for kc in range(8)],
                        r=[HB(kc, tt) for kc in range(8)] + [("w", s1)], w=[PS[pi]])
                    op("act", act_fn(eb[ei_], ps[pi][:, :], AF.Relu), r=[PS[pi]], w=[EB[ei_]])
                    op("dve", tt_(qk[:, j, tile_sl(tt)], eb[ei_], eb[ei_], ALU.mult), r=[EB[ei_]], w=[QK(j, tt)])
            ws.release(s1)
            k = 0
            for tt in range(NT):
                for m in range(8):
                    pi = 4 + k % 4
                    k += 1
                    mmg(ps[pi][:, :], [(w2[:, j, m * 128:(m + 1) * 128], qk[:, j, tile_sl(tt)]) for j in range(4)],
                        r=[QK(j, tt) for j in range(4)] + [("w", s2)], w=[PS[pi]])
                    add_to_x(m, tt, ps[pi][:, :], pi)
            ws.release(s2)

    ei = 0
    for li, l in enumerate(layers):
        if li > 0:
            tk.eng["pool"].wait_ge(cc0, 1)
            tk.stream["pool"].append(("w", ("cc", id(cc0)), 1))
            dma("pool", f"g0_{li}", xhalo[:, :, :],
                g0[li].ap()[0:128, :].rearrange("p (c t) -> p c t", c=8), w=[("xhalo",)])
        if l % 2 == 0:
            even_mixer(l, li, ei)
            ei += 1
        else:
            odd_mixer(l, li)
        xattn(l)
        mlp(l)
        if li + 1 < len(layers):
            h0 = dma("pool", f"b0_{li + 1}", b0[li + 1].ap().rearrange("p (c t) -> p c t", c=8),
                     xres[:, :, T - HALO:T], r=[X(c, 3) for c in range(8)])
            cc0 = collective(b0[li + 1], g0[li + 1], [h0])

    for tt in range(NT):
        dma("sp", f"yout{tt}", y_d[:, :, tile_sl(tt)], xres[:, :, tile_sl(tt)],
            r=[X(c, tt) for c in range(8)])
    tk.wait_all("sp")
    assert ws.used == len(ws.blocks), (ws.used, len(ws.blocks))
    tk.check_deadlock()
    stack.close()
    return nc


WEIGHT_KEYS = ["ev_w_in", "ev_gate_a_w", "ev_gate_x_w", "ev_w_out", "od_pool_w",
               "xa_w_q", "xa_w_kv", "xa_w_o", "mlp_w1", "mlp_w2"]

_NC_CACHE = {}


def to_fm(a):
    t = a.shape[0]
    return np.ascontiguousarray(a.reshape(t, 8, 128).transpose(2, 1, 0))


def from_fm(a):
    t = a.shape[2]
    return np.ascontiguousarray(a.transpose(2, 1, 0).reshape(t, 1024))


def run_layers(inp, x_full, layers):
    key = tuple(layers)
    if key not in _NC_CACHE:
        _NC_CACHE[key] = build(list(layers))
    nc = _NC_CACHE[key]
    mask = make_mask()
    in_maps = []
    for core in range(8):
        b, half = core // 2, core % 2
        base = half * T
        m = {k: np.ascontiguousarray(np.asarray(inp[k], np.float32)) for k in WEIGHT_KEYS}
        m["x"] = to_fm(x_full[b, base:base + T])
        if half:
            m["xh"] = to_fm(x_full[b, base - HALO:base])
        else:
            m["xh"] = np.zeros((128, 8, HALO), np.float32)
        m["mem"] = to_fm(np.asarray(inp["mem"], np.float32)[b])
        m["params"] = pack_params(inp, half)
        m["mask"] = mask
        in_maps.append(m)
    res = run_bass_kernel_spmd(nc, in_maps, core_ids=list(range(8)))
    out = np.zeros_like(x_full)
    for core in range(8):
        b, half = core // 2, core % 2
        out[b, half * T:(half + 1) * T] = from_fm(np.asarray(res.results[core]["y"]))
    return out


FUSED = True


def kernel(**inp):
    x = np.ascontiguousarray(np.asarray(inp["x"], np.float32))
    if FUSED:
        return run_layers(inp, x, [0, 1, 2, 3])
    for l in range(4):
        x = run_layers(inp, x, [l])
    return x
```

```python
from contextlib import ExitStack
import numpy as np
import concourse.bass as bass
import concourse.mybir as mybir
from concourse.bass_utils import run_bass_kernel_spmd

F32 = mybir.dt.float32
BF16 = mybir.dt.bfloat16
AF = mybir.ActivationFunctionType
ALU = mybir.AluOpType

T = 2048
NT = 4
TT = 512
HALO = 16
HW = HALO + T
KC = 8
EPS = 1e-6
MASKW = 384 + 2048 + 512
NEG = -30000.0
GROUPS = [[0, 1], [2, 3], [4, 5], [6, 7]]
SAME_ENGINE_SYNC = True


def tile_sl(tt):
    return slice(tt * TT, (tt + 1) * TT)


def param_layout():
    off = {}
    n = 0

    def add(name, w):
        nonlocal n
        off[name] = (n, w)
        n += w

    for l in range(4):
        add(f"mixg{l}", 8)
        add(f"xag{l}", 8)
        add(f"mlpg{l}", 8)
        add(f"xqg{l}", 2)
        add(f"xkg{l}", 2)
        if l % 2 == 0:
            add(f"convw{l}", 16)
            add(f"convb{l}", 4)
            add(f"gab{l}", 4)
            add(f"gxb{l}", 4)
            add(f"lam{l}", 4)
            add(f"qg{l}", 1)
            add(f"kg{l}", 1)
        else:
            add(f"scale{l}", 8)
    add("memg", 8)
    add("flag", 1)
    add("ctxbias", 1)
    add("invcnt", 8 * 16)
    return off, n


POFF, NP = param_layout()


def pack_params(inp, half):
    P = np.zeros((128, NP), np.float32)

    def put(name, arr):
        o, w = POFF[name]
        arr = np.asarray(arr, np.float32)
        assert arr.shape == (128, w), (name, arr.shape, w)
        P[:, o:o + w] = arr

    def cols(v):
        v = np.asarray(v, np.float32)
        return v.reshape(-1, 128).T

    for l in range(4):
        put(f"mixg{l}", cols(inp["mix_norm_g"][l]))
        put(f"xag{l}", cols(inp["xattn_norm_g"][l]))
        put(f"mlpg{l}", cols(inp["mlp_norm_g"][l]))
        put(f"xqg{l}", cols(inp["xa_q_norm_g"][l]))
        put(f"xkg{l}", cols(inp["xa_k_norm_g"][l]))
        if l % 2 == 0:
            e = l // 2
            cw = np.asarray(inp["ev_conv_w"][e], np.float32)
            put(f"convw{l}", np.concatenate([cols(cw[j]) for j in range(4)], axis=1))
            put(f"convb{l}", cols(inp["ev_conv_b"][e]))
            put(f"gab{l}", cols(inp["ev_gate_a_b"][e]))
            put(f"gxb{l}", cols(inp["ev_gate_x_b"][e]))
            put(f"lam{l}", cols(inp["ev_lambda"][e]))
            put(f"qg{l}", cols(inp["ev_q_norm_g"][e]))
            put(f"kg{l}", cols(inp["ev_k_norm_g"][e]))
        else:
            put(f"scale{l}", cols(inp["od_scale"][l // 2]))
    put("memg", cols(inp["mem_norm_g"]))
    put("flag", np.full((128, 1), float(half), np.float32))
    put("ctxbias", np.full((128, 1), 0.0 if half else NEG, np.float32))
    ic = np.zeros((8, 16), np.float32)
    for c in range(8):
        w = 2 ** (c // 2 + 1)
        for t in range(16):
            ic[c, t] = 1.0 / w if half else 1.0 / min(t + 1, w)
    put("invcnt", np.broadcast_to(ic.reshape(1, 128), (128, 128)))
    return P


def make_mask():
    ki = np.arange(128)[:, None]
    x = np.arange(MASKW)[None, :]
    d = x - ki - 384
    m = ((d >= 0) & (d <= 128)).astype(np.float32)
    m += ((d >= 0) & (d % 4 == 0) & (d <= 512)).astype(np.float32)
    m += ((d >= 0) & (d % 16 == 0) & (d <= 2048)).astype(np.float32)
    return m


class Trk:
    CH = 4000

    def __init__(self, nc, stack):
        self.nc = nc
        self.stack = stack
        self.eng = dict(pe=nc.tensor, act=nc.scalar, dve=nc.vector, pool=nc.gpsimd, sp=nc.sync)
        self.cnt = {e: 0 for e in self.eng}
        self.sems = {e: [] for e in self.eng}
        self.seen = {e: {f: 0 for f in self.eng} for e in self.eng}
        self.snap = {e: {} for e in self.eng}
        self.dsem = {}
        self.dcnt = {}
        self.dseen = {e: {} for e in self.eng}
        self.lastw = {}
        self.readers = {}
        self.dma_tokens = set()
        self.nops = 0
        self.stream = {e: [] for e in self.eng}

    def _sem(self, e, seq):
        k = (seq - 1) // self.CH
        while len(self.sems[e]) <= k:
            self.sems[e].append(
                self.stack.enter_context(self.nc.semaphore(f"s_{e}_{len(self.sems[e])}")))
        return self.sems[e][k], (seq - 1) % self.CH + 1

    def _wait(self, e, h):
        if h[0] == "eng":
            _, f, s = h
            if f == e and (e == "pe" or not SAME_ENGINE_SYNC):
                return
            if self.seen[e][f] >= s:
                return
            sem, v = self._sem(f, s)
            self.eng[e].wait_ge(sem, v)
            self.stream[e].append(("w", ("eng", f, (s - 1) // self.CH), v))
            self.seen[e][f] = s
            sn = self.snap[f].get(s)
            if sn and f != e:
                for g, v2 in sn.items():
                    if g != e and v2 > self.seen[e][g]:
                        self.seen[e][g] = v2
        else:
            _, key, v = h
            if self.dseen[e].get(key, 0) >= v:
                return
            self.eng[e].wait_ge(self.dsem[key], v)
            self.stream[e].append(("w", ("dma", key), v))
            self.dseen[e][key] = v

    def _deps(self, e, r, w):
        hs = []
        for t in r:
            h = self.lastw.get(t)
            if h:
                hs.append(h)
        for t in w:
            h = self.lastw.get(t)
            if h:
                hs.append(h)
            hs.extend(self.readers.get(t, ()))
        best = {}
        for h in hs:
            k = (h[0], h[1])
            if k not in best or h[2] > best[k][2]:
                best[k] = h
        for h in best.values():
            self._wait(e, h)

    def _commit(self, h, r, w):
        for t in r:
            self.readers.setdefault(t, []).append(h)
        for t in w:
            self.lastw[t] = h
            self.readers[t] = []

    def op(self, e, fn, r=(), w=()):
        self._deps(e, r, w)
        ins = fn(self.eng[e])
        self.cnt[e] += 1
        s = self.cnt[e]
        sem, v = self._sem(e, s)
        ins.then_inc(sem, 1)
        self.stream[e].append(("i", ("eng", e, (s - 1) // self.CH), 1))
        self.snap[e][s] = dict(self.seen[e])
        self._commit(("eng", e, s), r, w)
        self.nops += 1
        return ins

    def mmg(self, out, pairs, r=(), w=()):
        self._deps("pe", r, w)
        n = len(pairs)
        ins = None
        for i, (lhsT, rhs) in enumerate(pairs):
            ins = self.nc.tensor.matmul(out, lhsT, rhs, start=(i == 0), stop=(i == n - 1))
        self.cnt["pe"] += 1
        s = self.cnt["pe"]
        sem, v = self._sem("pe", s)
        ins.then_inc(sem, 1)
        self.stream["pe"].append(("i", ("eng", "pe", (s - 1) // self.CH), 1))
        self.snap["pe"][s] = dict(self.seen["pe"])
        self._commit(("eng", "pe", s), r, w)
        self.nops += n

    def mm1(self, out, lhsT, rhs, start, stop, r=(), w=()):
        self._deps("pe", r, w)
        ins = self.nc.tensor.matmul(out, lhsT, rhs, start=start, stop=stop)
        self.cnt["pe"] += 1
        s = self.cnt["pe"]
        sem, v = self._sem("pe", s)
        ins.then_inc(sem, 1)
        self.stream["pe"].append(("i", ("eng", "pe", (s - 1) // self.CH), 1))
        self.snap["pe"][s] = dict(self.seen["pe"])
        self._commit(("eng", "pe", s), r, w)
        self.nops += 1

    def dma(self, q, key, out, in_, r=(), w=()):
        if key not in self.dsem:
            self.dsem[key] = self.stack.enter_context(self.nc.semaphore(f"d_{key}"))
            self.dcnt[key] = 0
        self._deps(q, r, w)
        self.eng[q].dma_start(out=out, in_=in_).then_inc(self.dsem[key], 16)
        self.stream[q].append(("i", ("dma", key), 16))
        self.dcnt[key] += 16
        h = ("dma", key, self.dcnt[key])
        self._commit(h, r, w)
        self.dma_tokens.update(r)
        self.dma_tokens.update(w)
        return h

    def barrier(self):
        es = ["pe", "act", "dve"]
        for e in es:
            for f in es:
                if f != e and self.cnt[f] > self.seen[e][f]:
                    self._wait(e, ("eng", f, self.cnt[f]))
        for t in list(self.lastw):
            if t in self.dma_tokens:
                continue
            del self.lastw[t]
            self.readers.pop(t, None)
        for t in list(self.readers):
            if t not in self.dma_tokens and t not in self.lastw:
                del self.readers[t]

    def check_deadlock(self):
        val = {}
        ptr = {e: 0 for e in self.eng}
        prog = True
        while prog:
            prog = False
            for e in self.eng:
                st = self.stream[e]
                while ptr[e] < len(st):
                    k, key, v = st[ptr[e]]
                    if k == "w":
                        if val.get(key, 0) < v:
                            break
                    else:
                        val[key] = val.get(key, 0) + v
                    ptr[e] += 1
                    prog = True
        stuck = {e: (ptr[e], len(self.stream[e]), self.stream[e][ptr[e]], val.get(self.stream[e][ptr[e]][1], 0))
                 for e in self.eng if ptr[e] < len(self.stream[e])}
        assert not stuck, ("DEADLOCK", stuck)

    def wait_all(self, e):
        for t, h in self.lastw.items():
            self._wait(e, h)
        for t, hs in self.readers.items():
            for h in hs:
                self._wait(e, h)


def build(layers):
    nc = bass.Bass(target_bir_lowering=False)
    stack = ExitStack()

    def din(name, shape):
        return nc.dram_tensor(name, list(shape), F32, kind="ExternalInput").ap()

    x_d = din("x", (128, 8, T))
    xh_d = din("xh", (128, 8, HALO))
    mem_d = din("mem", (128, 8, 256))
    par_d = din("params", (128, NP))
    mask_d = din("mask", (128, MASKW))
    w_in_d = din("ev_w_in", (2, 1024, 2560))
    gaw_d = din("ev_gate_a_w", (2, 4, 128, 128))
    gxw_d = din("ev_gate_x_w", (2, 4, 128, 128))
    w_out_d = din("ev_w_out", (2, 1024, 1024))
    poolw_d = din("od_pool_w", (2, 4, 256, 256))
    xwq_d = din("xa_w_q", (4, 1024, 1024))
    xwkv_d = din("xa_w_kv", (4, 1024, 2048))
    xwo_d = din("xa_w_o", (4, 1024, 1024))
    w1_d = din("mlp_w1", (4, 1024, 4096))
    w2_d = din("mlp_w2", (4, 4096, 1024))
    y_d = nc.dram_tensor("y", [128, 8, T], F32, kind="ExternalOutput").ap()

    ncc_kv = sum(1 for l in layers if l % 2 == 0)
    b0 = [nc.dram_tensor(f"b0_{i}", [128, 128], F32) for i in range(len(layers))]
    g0 = [nc.dram_tensor(f"g0_{i}", [256, 128], F32) for i in range(len(layers))]
    b2 = [[nc.dram_tensor(f"b2_{i}_{h}", [256, 2048], BF16) for h in range(4)] for i in range(ncc_kv)]
    g2 = [[nc.dram_tensor(f"g2_{i}_{h}", [512, 2048], BF16) for h in range(4)] for i in range(ncc_kv)]
    b3 = [nc.dram_tensor(f"b3_{i}", [128, 4], F32) for i in range(ncc_kv)]
    g3 = [nc.dram_tensor(f"g3_{i}", [256, 4], F32) for i in range(ncc_kv)]

    def sb(name, shape, dt):
        return stack.enter_context(nc.sbuf_tensor(name, list(shape), dt))

    xres = sb("xres", (128, 8, T), F32)
    par = sb("par", (128, NP), F32)
    xhalo = sb("xhalo", (128, 8, HALO), F32)
    hprev = sb("hprev", (128, 8, HALO), F32)
    tf = sb("tf", (128, 8, 528), F32)
    sm = sb("sm", (128, 96), F32)
    hb = sb("hb", (128, 8 * HW), BF16)
    qk = sb("qk", (128, 8, T), BF16)
    vv = sb("vv", (128, 16, 512), BF16)
    NSLOT = 3
    wsl = sb("wsl", (128, NSLOT, 4096), BF16)
    bt = sb("bt", (128, 9, 512), BF16)
    yb = bt[:, 2:6, :].rearrange("p a b -> p (a b)")
    maskt = sb("maskt", (128, MASKW), BF16)
    memn = sb("memn", (128, 8, 256), BF16)
    ones = sb("ones", (128, 4, 128), BF16)
    ps = [stack.enter_context(nc.psum_tensor(f"ps{i}", [128, 512], F32)) for i in range(8)]

    hbuf = hb[:, :].rearrange("p (c t) -> p c t", c=8)
    kctx = hb[:, 0:8192].rearrange("p (h t) -> p h t", h=4)
    vctx = hb[:, 8192:16384].rearrange("p (b f) -> p b f", b=16)

    tk = Trk(nc, stack)
    op, mmg, dma = tk.op, tk.mmg, tk.dma

    def pcol(name, i=0, n=1):
        o, w = POFF[name]
        return par[:, o + i:o + i + n]

    class WStream:
        def __init__(self):
            self.blocks = []
            self.issued = 0
            self.used = 0
            self.released = set()
            self.cur = {}

        def add(self, tag, parts):
            self.blocks.append((tag, parts))

        def _issue(self, i):
            tag, parts = self.blocks[i]
            s = i % NSLOT
            for (lo, shape, src) in parts:
                n = int(np.prod(shape))
                dst = wsl[:, s, lo:lo + n]
                if len(shape) == 1:
                    pass
                elif len(shape) == 2:
                    dst = dst.rearrange("p (a b) -> p a b", a=shape[0])
                elif len(shape) == 3:
                    dst = dst.rearrange("p (a b c) -> p a b c", a=shape[0], b=shape[1])
                dma("pool", f"w{s}", dst, src, w=[("w", s)])

        def _pump(self):
            while self.issued < len(self.blocks) and (
                    self.issued < NSLOT or (self.issued - NSLOT) in self.released):
                self._issue(self.issued)
                self.issued += 1

        def next(self, tag):
            i = self.used
            assert self.blocks[i][0] == tag, (self.blocks[i][0], tag)
            self._pump()
            assert self.issued > i, ("weight slot not released", tag)
            self.used += 1
            s = i % NSLOT
            self.cur[s] = i
            return s, wsl[:, s, :]

        def release(self, s):
            self.released.add(self.cur[s])
            self._pump()

    ws = WStream()

    def wview(s, a, b, lo=0):
        return wsl[:, s, lo:lo + a * b].rearrange("p (a b) -> p a b", a=a)

    def cols_block(wd2, c0, n=512):
        return wd2.rearrange("(kc p) n -> p kc n", p=128)[:, :, c0:c0 + n]

    def rows_block(wd2, r0, nchunks):
        return wd2[r0:r0 + nchunks * 128, :].rearrange("(j p) n -> p j n", p=128)

    def schedule_layer(l):
        if l % 2 == 0:
            e = l // 2
            w = w_in_d[e]
            for c in range(4):
                ws.add(f"rg{l}_{c}", [(0, (8, 128), cols_block(w, 1536 + c * 128, 128)),
                                      (1024, (8, 128), cols_block(w, 2048 + c * 128, 128)),
                                      (2048, (128,), gaw_d[e, c]),
                                      (2176, (128,), gxw_d[e, c])])
            ws.add(f"v{l}", [(0, (8, 512), cols_block(w, 1024))])
            ws.add(f"woy{l}", [(0, (4, 1024), rows_block(w_out_d[e], 512, 4))])
            ws.add(f"k{l}", [(0, (8, 512), cols_block(w, 512))])
            ws.add(f"q{l}", [(0, (8, 512), cols_block(w, 0))])
            ws.add(f"woa{l}", [(0, (4, 1024), rows_block(w_out_d[e], 0, 4))])
        else:
            o = l // 2
            ws.add(f"pool{l}", [(i * 1024, (4, 256),
                                 poolw_d[o][:, i * 128:(i + 1) * 128, :].rearrange("g p n -> p g n"))
                                for i in range(2)])
        kv = xwkv_d[l]
        for j in range(2):
            ws.add(f"xk{l}_{j}", [(0, (8, 512), cols_block(kv, j * 512))])
        for j in range(2):
            ws.add(f"xv{l}_{j}", [(0, (8, 512), cols_block(kv, 1024 + j * 512))])
        for j in range(2):
            ws.add(f"xq{l}_{j}", [(0, (8, 512), cols_block(xwq_d[l], j * 512))])
        for j in range(2):
            ws.add(f"xo{l}_{j}", [(0, (4, 1024), rows_block(xwo_d[l], j * 512, 4))])
        for g in range(8):
            ws.add(f"w1_{l}_{g}", [(0, (8, 512), cols_block(w1_d[l], g * 512))])
            ws.add(f"w2_{l}_{g}", [(0, (4, 1024), rows_block(w2_d[l], g * 512, 4))])

    for l in layers:
        schedule_layer(l)

    def X(c, tt):
        return ("x", c, tt)

    def HB(c, tt):
        return ("hb", c, tt)

    def QK(c, tt):
        return ("qk", c, tt)

    ALLHB = [("hb", c, tt) for c in range(8) for tt in range(4)] + [("hbh",)]

    def act_fn(out, in_, func, **kw):
        return lambda e: e.activation(out=out, in_=in_, func=func, **kw)

    def emit_rstd(psum_ap, dst, r, w):
        op("act", act_fn(dst, psum_ap, AF.Ln, bias=EPSC), r=list(r) + [("epsc",)], w=w)
        op("act", act_fn(dst, dst, AF.Exp, scale=-0.5), r=w, w=w)

    def emit_recip(psum_ap, dst, r, w):
        op("act", act_fn(dst, psum_ap, AF.Ln), r=list(r), w=w)
        op("act", act_fn(dst, dst, AF.Exp, scale=-1.0), r=w, w=w)

    def stt(out, in0, scalar, in1, op0, op1):
        return lambda e: e.scalar_tensor_tensor(out, in0, scalar, in1, op0, op1)

    def tt_(out, in0, in1, opx):
        return lambda e: e.tensor_tensor(out, in0, in1, opx)

    def ts_(out, in0, s1, s2, op0, op1=None):
        if op1 is None:
            return lambda e: e.tensor_scalar(out, in0, s1, None, op0)
        return lambda e: e.tensor_scalar(out, in0, s1, s2, op0, op1)

    EPSC = sm[:, 1:2]
    ONE_D, ONE_128, ONE_256, ONE_1 = (ones[:, i, :] for i in range(4))
    sq = [bt[:, 0, :], bt[:, 1, :]]
    eb = [bt[:, 2, :], bt[:, 3, :]]
    pb = [bt[:, 4, :], bt[:, 5, :]]
    SQ = [("sq", 0), ("sq", 1)]
    EB = [("eb", 0), ("eb", 1)]
    PB = [("pb", 0), ("pb", 1)]
    PS = [("ps", i) for i in range(8)]
    TF = [("tf", i) for i in range(8)]

    for tt in range(NT):
        dma("sp", f"xin{tt}", xres[:, :, tile_sl(tt)], x_d[:, :, tile_sl(tt)],
            w=[X(c, tt) for c in range(8)])
    dma("sp", "par", par[:, :], par_d[:, :], w=[("par",)])
    dma("sp", "xh", xhalo[:, :, :], xh_d[:, :, :], w=[("xhalo",)])
    memf = tf[:, 0:8, 0:256]
    dma("sp", "mem", memf, mem_d[:, :, :], w=TF[0:8])
    dma("pool", "mask", maskt[:, :], mask_d[:, :], w=[("mask",)])
    for i, val in enumerate([1.0 / 1024, 1.0 / 128, 1.0 / 256, 1.0]):
        op("dve", lambda e, i=i, val=val: e.memset(ones[:, i, :], val), w=[("ones",)])
    op("dve", lambda e: e.memset(bt[:, 6, :], 0.0), w=[("sm0",)])
    ZT = bt[:, 6, :]
    op("dve", lambda e: e.memset(sm[:, 1:2], EPS), w=[("epsc",)])

    sqm = qk[:, 0, 0:2048].rearrange("p (c m) -> p c m", c=8)
    for c in range(8):
        op("act", act_fn(sqm[:, c, :], memf[:, c, :], AF.Square), r=TF[0:8], w=[QK(0, 0)])
    mmg(ps[0][:, 0:256], [(ONE_D, sqm[:, c, :]) for c in range(8)],
        r=[QK(0, 0), ("ones",)], w=[PS[0]])
    emit_rstd(ps[0][:, 0:256], tf[:, 4, 256:512], r=[PS[0]], w=[("mrs",)])
    for c in range(8):
        op("dve", stt(memn[:, c, :], memf[:, c, :], pcol("memg", c), tf[:, 4, 256:512],
                      ALU.mult, ALU.mult),
           r=TF[0:8] + [("mrs",), ("par",)], w=[("memn",)])
    tk.barrier()

    def halo_prep(gname, to_hbuf):
        op("dve", ts_(xhalo[:, :, :], xhalo[:, :, :], pcol("flag"), None, ALU.mult),
           r=[("par",)], w=[("xhalo",)])
        sqh = bt[:, 0, 0:128].rearrange("p (c t) -> p c t", c=8)
        op("act", act_fn(sqh, xhalo[:, :, :], AF.Square), r=[("xhalo",)], w=[SQ[0]])
        mmg(ps[7][:, 0:HALO], [(ONE_D, sqh[:, c, :]) for c in range(8)],
            r=[SQ[0], ("ones",)], w=[PS[7]])
        rh = sm[:, 16:32]
        emit_rstd(ps[7][:, 0:HALO], rh, r=[PS[7]], w=[("rh",)])
        for c in range(8):
            dst = hbuf[:, c, 0:HALO] if to_hbuf else hprev[:, c, :]
            op("dve", stt(dst, xhalo[:, c, :], pcol(gname, c), rh, ALU.mult, ALU.mult),
               r=[("xhalo",), ("rh",), ("par",)], w=[("hbh",)] if to_hbuf else [("hprev", c)])

    def norm_tile_rstd(tt, dst_tf):
        for c in range(8):
            op("act", act_fn(sq[c % 2], xres[:, c, tile_sl(tt)], AF.Square),
               r=[X(c, tt)], w=[SQ[c % 2]])
            tk.mm1(ps[7 - tt % 2][:, :], ONE_D, sq[c % 2], c == 0, c == 7, r=[SQ[c % 2], ("ones",)],
                   w=[PS[7 - tt % 2]])
        emit_rstd(ps[7 - tt % 2][:, :], tf[:, dst_tf, 0:512], r=[PS[7 - tt % 2]], w=[TF[dst_tf]])

    def norm_to_hbuf(gname):
        for tt in range(NT):
            ri = 7 - tt % 2
            norm_tile_rstd(tt, ri)
            for c in range(8):
                op("dve", stt(hbuf[:, c, HALO + tt * TT:HALO + (tt + 1) * TT],
                              xres[:, c, tile_sl(tt)], pcol(gname, c), tf[:, ri, 0:512],
                              ALU.mult, ALU.mult),
                   r=[X(c, tt), TF[ri], ("par",)], w=[HB(c, tt)])

    def hslice(c, tt):
        return hbuf[:, c, HALO + tt * TT:HALO + (tt + 1) * TT]

    def add_to_x(m, tt, psum_ap, psi):
        op("dve", tt_(xres[:, m, tile_sl(tt)], psum_ap, xres[:, m, tile_sl(tt)], ALU.add),
           r=[PS[psi]], w=[X(m, tt)])

    cc_count = [0]

    def collective(src_d, dst_d, wait_handles):
        for h in wait_handles:
            tk._wait("pool", h)
        sem = stack.enter_context(nc.semaphore(f"cc{cc_count[0]}"))
        cc_count[0] += 1
        nc.gpsimd.collective_compute(
            "AllGather", ALU.bypass, replica_groups=GROUPS,
            ins=[src_d.ap().opt()], outs=[dst_d.ap().opt()]).then_inc(sem)
        tk.stream["pool"].append(("i", ("cc", id(sem)), 1))
        return sem

    def proj_headnorm(wt, wtok, gcol, base):
        its = [(hd, tt) for hd in range(4) for tt in range(NT)]

        def front(i):
            hd, tt = its[i]
            pa = i % 3
            mmg(ps[pa][:, :], [(wt[:, kc, hd * 128:(hd + 1) * 128], hslice(kc, tt)) for kc in range(8)],
                r=[HB(kc, tt) for kc in range(8)] + [wtok], w=[PS[pa]])
            op("act", act_fn(sq[i % 2], ps[pa][:, :], AF.Square), r=[PS[pa]], w=[SQ[i % 2]])

        def back(i):
            hd, tt = its[i]
            pa, pn, ti = i % 3, 3 + i % 2, i % 2
            mmg(ps[pn][:, :], [(ONE_128, sq[i % 2])], r=[SQ[i % 2], ("ones",)], w=[PS[pn]])
            emit_rstd(ps[pn][:, :], tf[:, ti, 0:512], r=[PS[pn]], w=[TF[ti]])
            op("dve", stt(qk[:, base + hd, tile_sl(tt)], ps[pa][:, :], gcol, tf[:, ti, 0:512],
                          ALU.mult, ALU.mult), r=[PS[pa], TF[ti], ("par",)], w=[QK(base + hd, tt)])

        for i in range(len(its) + 1):
            if i < len(its):
                front(i)
            if i >= 1:
                back(i - 1)

    def even_mixer(l, li, ei):
        e = l // 2
        halo_prep(f"mixg{l}", True)
        norm_to_hbuf(f"mixg{l}")
        sp8 = sm[:, 4:8]
        op("act", act_fn(sp8, pcol(f"lam{l}", 0, 4), AF.Exp, scale=-1.0), r=[("par",)], w=[("sp8",)])
        op("act", act_fn(sp8, sp8, AF.Ln, bias=1.0), r=[("sp8",)], w=[("sp8",)])
        op("dve", ts_(sp8, sp8, -8.0, None, ALU.mult), r=[("sp8",)], w=[("sp8",)])

        hfin = sm[:, 8:12]
        hA = sm[:, 32:36]
        tk.barrier()
        vf = vv[:, :, :].rearrange("p a b -> p (a b)").bitcast(F32)
        VR = lambda r: vf[:, r * 528:(r + 1) * 528]
        CH = [
            dict(xrt=[tf[:, 0, 0:515], tf[:, 1, 0:515]], XT=[TF[0], TF[1]],
                 gx=tf[:, 2, 0:512], GX=TF[2], g3=tf[:, 3, 0:512], G3=TF[3],
                 ra=tf[:, 4, 0:512], RA=TF[4], ri=tf[:, 5, 0:512], RI=TF[5],
                 xc=tf[:, 6, 0:512], XC=TF[6], pxr=0, pgt=1, pga=4, pgx=5, sq=0,
                 hc=sm[:, 12:13], HC=("hc", 0), pc=sm[:, 13:14], PC=("pc", 0),
                 xrh=sm[:, 40:56], XRH=("xrh", 0)),
            dict(xrt=[VR(0)[:, 0:515], VR(1)[:, 0:515]], XT=[("vf", 0), ("vf", 1)],
                 gx=VR(2)[:, 0:512], GX=("vf", 2), g3=VR(3)[:, 0:512], G3=("vf", 3),
                 ra=VR(4)[:, 0:512], RA=("vf", 4), ri=VR(5)[:, 0:512], RI=("vf", 5),
                 xc=VR(6)[:, 0:512], XC=("vf", 6), pxr=2, pgt=3, pga=6, pgx=7, sq=1,
                 hc=sm[:, 14:15], HC=("hc", 1), pc=sm[:, 15:16], PC=("pc", 1),
                 xrh=sm[:, 64:80], XRH=("xrh", 1)),
        ]

        def chain_ops(q, c, tt, srg):
            B = CH[q]
            p = tt % 2
            xt, XTp = B["xrt"][p], B["XT"][p]
            xprev, XTq = B["xrt"][1 - p], B["XT"][1 - p]
            gx, g3_, ra, ri, xc = B["gx"], B["g3"], B["ra"], B["ri"], B["xc"]
            GX, G3, RA, RI, XC = B["GX"], B["G3"], B["RA"], B["RI"], B["XC"]
            hc, pc, HC, PC = B["hc"], B["pc"], B["HC"], B["PC"]
            sqb, SQB = sq[B["sq"]], SQ[B["sq"]]
            pxr, pgt, pga, pgx = B["pxr"], B["pgt"], B["pga"], B["pgx"]
            wxr = wview(srg, 8, 128, 0)
            wgt = wview(srg, 8, 128, 1024)
            wga = wsl[:, srg, 2048:2176]
            wgx = wsl[:, srg, 2176:2304]
            W = ("w", srg)
            cw = lambda j: pcol(f"convw{l}", j * 4 + c)
            hbr = [HB(kc, tt) for kc in range(8)]
            L = []
            A = L.append
            if tt == 0:
                A(lambda: mmg(ps[pxr][:, 0:16], [(wxr[:, kc, :], hbuf[:, kc, 0:HALO]) for kc in range(8)],
                              r=[("hbh",), W], w=[PS[pxr]]))
                A(lambda: op("act", act_fn(B["xrh"], ps[pxr][:, 0:16], AF.Copy), r=[PS[pxr]], w=[B["XRH"]]))
                A(lambda: op("dve", lambda e_: e_.memset(hc, 0.0), w=[HC]))
                A(lambda: op("dve", lambda e_: e_.memset(pc, 1.0), w=[PC]))
            A(lambda: mmg(ps[pxr][:, :], [(wxr[:, kc, :], hslice(kc, tt)) for kc in range(8)], r=hbr + [W], w=[PS[pxr]]))
            A(lambda: mmg(ps[pgt][:, :], [(wgt[:, kc, :], hslice(kc, tt)) for kc in range(8)], r=hbr + [W], w=[PS[pgt]]))
            if tt == 0:
                A(lambda: op("dve", lambda e_: e_.tensor_copy(xt[:, 0:3], B["xrh"][:, 13:16]), r=[B["XRH"]], w=[XTp]))
            else:
                A(lambda: op("dve", lambda e_: e_.tensor_copy(xt[:, 0:3], xprev[:, 512:515]), r=[XTq], w=[XTp]))
            A(lambda: op("act", act_fn(xt[:, 3:515], ps[pxr][:, :], AF.Copy), r=[PS[pxr]], w=[XTp]))
            A(lambda: op("act", act_fn(gx, ps[pgt][:, :], AF.Copy), r=[PS[pgt]], w=[GX]))
            A(lambda: op("act", act_fn(g3_, gx, AF.Square), r=[GX], w=[G3]))
            A(lambda: op("act", act_fn(xc, xt[:, 3:515], AF.Identity, scale=cw(3), bias=pcol(f"convb{l}", c)),
                         r=[XTp, ("par",)], w=[XC]))
            A(lambda: op("dve", ts_(g3_, g3_, 0.044715, 1.0, ALU.mult, ALU.add), r=[G3], w=[G3]))
            A(lambda: op("dve", stt(xc, xt[:, 0:512], cw(0), xc, ALU.mult, ALU.add), r=[XTp, XC, ("par",)], w=[XC]))
            A(lambda: op("dve", tt_(g3_, g3_, gx, ALU.mult), r=[GX, G3], w=[G3]))
            A(lambda: op("dve", stt(xc, xt[:, 1:513], cw(1), xc, ALU.mult, ALU.add), r=[XTp, XC, ("par",)], w=[XC]))
            A(lambda: op("act", act_fn(g3_, g3_, AF.Sigmoid, scale=1.5957691216057308), r=[G3], w=[G3]))
            A(lambda: op("dve", stt(xc, xt[:, 2:514], cw(2), xc, ALU.mult, ALU.add), r=[XTp, XC, ("par",)], w=[XC]))
            A(lambda: op("act", act_fn(sqb, xc, AF.Copy), r=[XC], w=[SQB]))
            A(lambda: op("dve", tt_(gx, gx, g3_, ALU.mult), r=[GX, G3], w=[GX]))
            A(lambda: mmg(ps[pga][:, :], [(wga, sqb)], r=[SQB, W], w=[PS[pga]]))
            A(lambda: mmg(ps[pgx][:, :], [(wgx, sqb)], r=[SQB, W], w=[PS[pgx]]))
            A(lambda: op("act", act_fn(ra, ps[pga][:, :], AF.Sigmoid, bias=pcol(f"gab{l}", c)),
                         r=[PS[pga], ("par",)], w=[RA]))
            A(lambda: op("act", act_fn(ri, ps[pgx][:, :], AF.Sigmoid, bias=pcol(f"gxb{l}", c)),
                         r=[PS[pgx], ("par",)], w=[RI]))
            A(lambda: op("act", act_fn(ra, ra, AF.Exp, scale=sp8[:, c:c + 1]), r=[RA, ("sp8",)], w=[RA]))
            A(lambda: op("dve", tt_(ri, ri, xc, ALU.mult), r=[RI, XC], w=[RI]))
            A(lambda: op("dve", tt_(g3_, ra, ra, ALU.mult), r=[RA], w=[G3]))
            A(lambda: op("act", act_fn(g3_, g3_, AF.Sqrt, scale=-1.0, bias=1.0), r=[G3], w=[G3]))
            A(lambda: op("dve", lambda e_: e_.tensor_tensor_scan(xc, ra, ZT, pc, ALU.mult, ALU.add),
                         r=[RA, RI, PC, ("sm0",)], w=[XC]))
            A(lambda: op("dve", tt_(ri, ri, g3_, ALU.mult), r=[RI, G3], w=[RI]))
            A(lambda: op("dve", lambda e_: e_.tensor_copy(pc, xc[:, 511:512]), r=[XC], w=[PC]))
            A(lambda: op("dve", lambda e_: e_.tensor_tensor_scan(g3_, ra, ri, hc, ALU.mult, ALU.add),
                         r=[RA, RI, HC], w=[G3]))
            A(lambda: op("dve", tt_(qk[:, c, tile_sl(tt)], xc, gx, ALU.mult), r=[XC, GX], w=[QK(c, tt)]))
            A(lambda: op("dve", lambda e_: e_.tensor_copy(hc, g3_[:, 511:512]), r=[G3], w=[HC]))
            A(lambda: op("dve", tt_(qk[:, 4 + c, tile_sl(tt)], g3_, gx, ALU.mult), r=[G3, GX], w=[QK(4 + c, tt)]))
            if tt == NT - 1:
                A(lambda: op("dve", lambda e_: e_.tensor_copy(hfin[:, c:c + 1], hc), r=[HC], w=[("hfin",)]))
            return L

        for pair in ((0, 1), (2, 3)):
            srgs = [ws.next(f"rg{l}_{c}")[0] for c in pair]
            for tt in range(NT):
                lists = [chain_ops(q, c, tt, srgs[q]) for q, c in enumerate(pair)]
                for k in range(max(len(x_) for x_ in lists)):
                    for x_ in lists:
                        if k < len(x_):
                            x_[k]()
            for sg in srgs:
                ws.release(sg)
        h3 = dma("pool", f"b3_{ei}", b3[ei].ap(), hfin, r=[("hfin",)])
        cc3 = collective(b3[ei], g3[ei], [h3])
        tk.barrier()

        sv, _ = ws.next(f"v{l}")
        wv = wview(sv, 8, 512)
        for blk in range(16):
            pa = 4 + blk % 2
            tt = blk // 4
            mmg(ps[pa][:, :],
                [(hbuf[:, kc, HALO + blk * 128:HALO + (blk + 1) * 128], wv[:, kc, :]) for kc in range(8)],
                r=[HB(kc, tt) for kc in range(8)] + [("w", sv)], w=[PS[pa]])
            op("act", act_fn(vv[:, blk, :], ps[pa][:, :], AF.Copy), r=[PS[pa]], w=[("vv", blk)])
        ws.release(sv)

        tk.eng["pool"].wait_ge(cc3, 1)
        tk.stream["pool"].append(("w", ("cc", id(cc3)), 1))
        dma("pool", f"g3_{ei}", hA, g3[ei].ap()[0:128, :], w=[("hA",)])
        op("dve", ts_(hA, hA, pcol("flag"), None, ALU.mult), r=[("hA",), ("par",)], w=[("hA",)])
        for c in range(4):
            for tt in range(NT):
                op("dve", stt(qk[:, 4 + c, tile_sl(tt)], qk[:, c, tile_sl(tt)], hA[:, c:c + 1],
                              qk[:, 4 + c, tile_sl(tt)], ALU.mult, ALU.add),
                   r=[QK(c, tt), QK(4 + c, tt), ("hA",)], w=[QK(4 + c, tt)])
        swy, _ = ws.next(f"woy{l}")
        woy = wview(swy, 4, 1024)
        for m in range(8):
            for tt in range(NT):
                pi = 6 + (m * NT + tt) % 2
                mmg(ps[pi][:, :], [(woy[:, c, m * 128:(m + 1) * 128], qk[:, 4 + c, tile_sl(tt)]) for c in range(4)],
                    r=[QK(4 + c, tt) for c in range(4)] + [("w", swy)], w=[PS[pi]])
                add_to_x(m, tt, ps[pi][:, :], pi)
        ws.release(swy)

        sk, _ = ws.next(f"k{l}")
        wk = wview(sk, 8, 512)
        proj_headnorm(wk, ("w", sk), pcol(f"kg{l}"), 4)
        ws.release(sk)
        cc2 = []
        for hd in range(4):
            h2a = dma("pool", f"b2k_{ei}_{hd}", b2[ei][hd].ap()[0:128, :], qk[:, 4 + hd, :],
                      r=[QK(4 + hd, tt) for tt in range(4)])
            h2b = dma("pool", f"b2v_{ei}_{hd}",
                      b2[ei][hd].ap()[128:256, :].rearrange("p (b f) -> p b f", b=16),
                      vv[:, :, hd * 128:(hd + 1) * 128], r=[("vv", blk) for blk in range(16)])
            cc2.append(collective(b2[ei][hd], g2[ei][hd], [h2a, h2b]))

        sq_, _ = ws.next(f"q{l}")
        wq = wview(sq_, 8, 512)
        proj_headnorm(wq, ("w", sq_), pcol(f"qg{l}"), 0)
        ws.release(sq_)
        tk.barrier()
        for f in ("pe", "act", "dve"):
            tk._wait("pool", ("eng", f, tk.cnt[f]))
        for hd in range(4):
            tk.eng["pool"].wait_ge(cc2[hd], 1)
            tk.stream["pool"].append(("w", ("cc", id(cc2[hd])), 1))
            dma("pool", f"ctxk_{ei}_{hd}", kctx[:, hd, :], g2[ei][hd].ap()[0:128, :],
                w=[("kctx", hd)] + ALLHB)
            dma("pool", f"ctxv_{ei}_{hd}", vctx[:, :, hd * 128:(hd + 1) * 128],
                g2[ei][hd].ap()[128:256, :].rearrange("p (b f) -> p b f", b=16),
                w=[("vctx", hd)] + ALLHB)

        scale = 128.0 ** -0.5
        ebs = [bt[:, 2, :], bt[:, 3, :], bt[:, 0, :], bt[:, 7, :]]
        pbs = [bt[:, 4, :], bt[:, 5, :], bt[:, 1, :], bt[:, 8, :]]
        EBS = [("eb", 0), ("eb", 1), ("sq", 0), ("bt7",)]
        PBS = [("pb", 0), ("pb", 1), ("sq", 1), ("bt8",)]
        SBK = [0, 1, 2, 7]
        items = []
        gi = 0
        for hd in range(4):
            for qt in reversed(range(NT)):
                blocks = [("own", kb) for kb in range(0, 4 * qt + 4)] + [("ctx", kb) for kb in range(4 * qt, 16)]
                for bi, (kind, kb) in enumerate(blocks):
                    items.append((hd, qt, gi, bi, len(blocks), kind, kb))
                gi += 1
        LA = 3

        def att_front(idx):
            hd, qt, g, bi, nb, kind, kb = items[idx]
            si, bi3 = SBK[idx % 4], idx % 4
            if kind == "ctx":
                kT = kctx[:, hd, kb * 128:(kb + 1) * 128]
                rk = [("kctx", hd)]
                d0 = 512 * qt + 2048 - 128 * kb
                bias = pcol("ctxbias")
            else:
                kT = qk[:, 4 + hd, kb * 128:(kb + 1) * 128]
                rk = [QK(4 + hd, kb // 4)]
                d0 = 512 * qt - 128 * kb
                bias = 0.0
            off = d0 + 384
            mmg(ps[si][:, :], [(kT, qk[:, hd, tile_sl(qt)])], r=rk + [QK(hd, qt)], w=[PS[si]])
            op("act", act_fn(ebs[bi3], ps[si][:, :], AF.Exp, scale=scale, bias=bias),
               r=[PS[si], ("par",)], w=[EBS[bi3]])
            op("dve", tt_(pbs[bi3], ebs[bi3], maskt[:, off:off + 512], ALU.mult),
               r=[EBS[bi3], ("mask",)], w=[PBS[bi3]])

        def att_back(idx):
            hd, qt, g, bi, nb, kind, kb = items[idx]
            bi3 = idx % 4
            po, pd = 3 + g % 2, 5 + g % 2
            if kind == "ctx":
                vs = vctx[:, kb, hd * 128:(hd + 1) * 128]
                rv = [("vctx", hd)]
            else:
                vs = vv[:, kb, hd * 128:(hd + 1) * 128]
                rv = [("vv", kb)]
            tk.mm1(ps[po][:, :], vs, pbs[bi3], bi == 0, bi == nb - 1, r=rv + [PBS[bi3]], w=[PS[po]])
            tk.mm1(ps[pd][:, :], ONE_1, pbs[bi3], bi == 0, bi == nb - 1, r=[("ones",), PBS[bi3]], w=[PS[pd]])
            if bi == nb - 1:
                rdi = 6 + g % 2
                rd = tf[:, rdi, 0:512]
                emit_recip(ps[pd][:, :], rd, r=[PS[pd]], w=[TF[rdi]])
                op("dve", tt_(qk[:, hd, tile_sl(qt)], ps[po][:, :], rd, ALU.mult),
                   r=[PS[po], TF[rdi]], w=[QK(hd, qt)])

        for idx in range(len(items) + LA):
            if idx < len(items):
                att_front(idx)
            if idx - LA >= 0:
                att_back(idx - LA)
        swa, _ = ws.next(f"woa{l}")
        woa = wview(swa, 4, 1024)
        for tt in range(NT):
            for m in range(8):
                pi = (tt * 8 + m) % 2
                mmg(ps[pi][:, :], [(woa[:, c, m * 128:(m + 1) * 128], qk[:, c, tile_sl(tt)]) for c in range(4)],
                    r=[QK(c, tt) for c in range(4)] + [("w", swa)], w=[PS[pi]])
                add_to_x(m, tt, ps[pi][:, :], pi)
        ws.release(swa)
        tk.barrier()

    def odd_mixer(l, li):
        halo_prep(f"mixg{l}", False)
        spw, _ = ws.next(f"pool{l}")
        pw = wsl[:, spw, 0:2048].rearrange("p (i g n) -> p i g n", i=2, g=4)
        dt_ = vv[:, 0:8, :]
        for tt in range(NT):
            norm_tile_rstd(tt, 7)
            for g in range(4):
                w = 2 ** (g + 1)
                cs = (2 * g, 2 * g + 1)
                hfs = [tf[:, 0, 0:528], tf[:, 3, 0:528]]
                HFT = [TF[0], TF[3]]
                sbufs = [[(tf[:, 1, 0:528], TF[1]), (tf[:, 2, 0:528], TF[2])],
                         [(tf[:, 4, 0:528], TF[4]), (tf[:, 5, 0:528], TF[5])]]
                for q_, c in enumerate(cs):
                    op("act", act_fn(hfs[q_][:, 0:16], hprev[:, c, :], AF.Copy),
                       r=[("hprev", c)], w=[HFT[q_]])
                for q_, c in enumerate(cs):
                    op("dve", stt(hfs[q_][:, 16:528], xres[:, c, tile_sl(tt)], pcol(f"mixg{l}", c), tf[:, 7, 0:512],
                                  ALU.mult, ALU.mult), r=[X(c, tt), TF[7], ("par",)], w=[HFT[q_]])
                for q_, c in enumerate(cs):
                    op("act", act_fn(hprev[:, c, :], hfs[q_][:, 512:528], AF.Copy),
                       r=[HFT[q_]], w=[("hprev", c)])
                srcs = [(hfs[0], HFT[0]), (hfs[1], HFT[1])]
                for k in range(g + 1):
                    sh = 2 ** k
                    lo = 2 ** (k + 1) - 1
                    for q_ in range(2):
                        src, srct = srcs[q_]
                        dst, dstt = sbufs[q_][k % 2]
                        op("dve", tt_(dst[:, lo:528], src[:, lo:528], src[:, lo - sh:528 - sh], ALU.add),
                           r=[srct], w=[dstt])
                        srcs[q_] = (dst, dstt)
                for q_, c in enumerate(cs):
                    src, srct = srcs[q_]
                    op("dve", stt(dt_[:, c, :], src[:, 16:528], 1.0 / w, hfs[q_][:, 16:528], ALU.mult, ALU.subtract),
                       r=[srct, HFT[q_]], w=[("dt", c)])
                if tt == 0:
                    o_, _w = POFF["invcnt"]
                    for q_, c in enumerate(cs):
                        src, srct = srcs[q_]
                        t16 = sm[:, 40:56] if q_ == 0 else sm[:, 64:80]
                        op("dve", tt_(t16, src[:, 16:32], par[:, o_ + c * 16:o_ + (c + 1) * 16], ALU.mult),
                           r=[srct, ("par",)], w=[("t16", q_)])
                    for q_, c in enumerate(cs):
                        t16 = sm[:, 40:56] if q_ == 0 else sm[:, 64:80]
                        op("dve", tt_(dt_[:, c, 0:16], t16, hfs[q_][:, 16:32], ALU.subtract),
                           r=[("t16", q_), HFT[q_]], w=[("dt", c)])
            for j in range(8):
                g, jj = j // 2, j % 2
                pi = j % 2
                mmg(ps[pi][:, :], [(pw[:, i, g, jj * 128:(jj + 1) * 128], dt_[:, 2 * g + i, :]) for i in range(2)],
                    r=[("dt", 2 * g), ("dt", 2 * g + 1), ("w", spw)], w=[PS[pi]])
                op("dve", stt(xres[:, j, tile_sl(tt)], ps[pi][:, :], pcol(f"scale{l}", j),
                              xres[:, j, tile_sl(tt)], ALU.mult, ALU.add),
                   r=[PS[pi], ("par",)], w=[X(j, tt)])
        ws.release(spw)
        tk.barrier()

    def xattn(l):
        kx = vv[:, 0:4, :].rearrange("p a (b m) -> p (a b) m", b=2)
        vx = vv[:, 4:8, :].rearrange("p (b j) f -> p b (j f)", b=2)
        for j in range(2):
            sw, _ = ws.next(f"xk{l}_{j}")
            wk = wview(sw, 8, 512)
            for hh in range(2):
                h = 2 * j + hh
                for i in range(2):
                    mmg(ps[i][:, 0:256],
                        [(wk[:, kc, (2 * hh + i) * 128:(2 * hh + i + 1) * 128], memn[:, kc, :]) for kc in range(8)],
                        r=[("memn",), ("w", sw)], w=[PS[i]])
                    op("act", act_fn(sq[i][:, 0:256], ps[i][:, 0:256], AF.Square), r=[PS[i]], w=[SQ[i]])
                mmg(ps[2][:, 0:256], [(ONE_256, sq[0][:, 0:256]), (ONE_256, sq[1][:, 0:256])],
                    r=[SQ[0], SQ[1], ("ones",)], w=[PS[2]])
                emit_rstd(ps[2][:, 0:256], tf[:, 0, 0:256], r=[PS[2]], w=[TF[0]])
                for i in range(2):
                    op("dve", stt(kx[:, 2 * h + i, :], ps[i][:, 0:256], pcol(f"xkg{l}", i), tf[:, 0, 0:256],
                                  ALU.mult, ALU.mult), r=[PS[i], TF[0], ("par",)], w=[("kx",)])
            ws.release(sw)
        for j in range(2):
            sw, _ = ws.next(f"xv{l}_{j}")
            wv = wview(sw, 8, 512)
            for blk in range(2):
                pi = 3 + blk
                mmg(ps[pi][:, :], [(memn[:, kc, blk * 128:(blk + 1) * 128], wv[:, kc, :]) for kc in range(8)],
                    r=[("memn",), ("w", sw)], w=[PS[pi]])
                op("act", act_fn(vx[:, blk, j * 512:(j + 1) * 512], ps[pi][:, :], AF.Copy),
                   r=[PS[pi]], w=[("vx",)])
            ws.release(sw)
        norm_to_hbuf(f"xag{l}")
        sqx = [[bt[:, 0, :], bt[:, 1, :]], [bt[:, 2, :], bt[:, 3, :]], [bt[:, 7, :], bt[:, 8, :]]]
        SQX = [[("sq", 0), ("sq", 1)], [("eb", 0), ("eb", 1)], [("bt7",), ("bt8",)]]
        for j in range(2):
            sw, _ = ws.next(f"xq{l}_{j}")
            wq = wview(sw, 8, 512)
            its = [(hh, tt) for hh in range(2) for tt in range(NT)]

            def qfront(n, its=its, wq=wq, sw=sw):
                hh, tt = its[n]
                p = n % 3
                for i in range(2):
                    bk = 2 * p + i
                    mmg(ps[bk][:, :],
                        [(wq[:, kc, (2 * hh + i) * 128:(2 * hh + i + 1) * 128], hslice(kc, tt)) for kc in range(8)],
                        r=[HB(kc, tt) for kc in range(8)] + [("w", sw)], w=[PS[bk]])
                    op("act", act_fn(sqx[p][i], ps[bk][:, :], AF.Square), r=[PS[bk]], w=[SQX[p][i]])

            def qback(n, its=its, j=j):
                hh, tt = its[n]
                h = 2 * j + hh
                p = n % 3
                pn = 6 + n % 2
                mmg(ps[pn][:, :], [(ONE_256, sqx[p][0]), (ONE_256, sqx[p][1])],
                    r=[SQX[p][0], SQX[p][1], ("ones",)], w=[PS[pn]])
                emit_rstd(ps[pn][:, :], tf[:, p, 0:512], r=[PS[pn]], w=[TF[p]])
                for i in range(2):
                    op("dve", stt(qk[:, 2 * h + i, tile_sl(tt)], ps[2 * p + i][:, :], pcol(f"xqg{l}", i),
                                  tf[:, p, 0:512], ALU.mult, ALU.mult),
                       r=[PS[2 * p + i], TF[p], ("par",)], w=[QK(2 * h + i, tt)])

            for n in range(len(its) + 1):
                if n < len(its):
                    qfront(n)
                if n >= 1:
                    qback(n - 1)
            ws.release(sw)
        scale = 256.0 ** -0.5
        ebx = [[bt[:, 2, :], bt[:, 3, :]], [bt[:, 4, :], bt[:, 5, :]]]
        EBX = [[("eb", 0), ("eb", 1)], [("pb", 0), ("pb", 1)]]
        aits = [(h, tt) for h in range(4) for tt in range(NT)]

        def afront(n):
            h, tt = aits[n]
            p = n % 2
            for blk in range(2):
                bk = blk
                mmg(ps[bk][:, :],
                    [(kx[:, 2 * h + i, blk * 128:(blk + 1) * 128], qk[:, 2 * h + i, tile_sl(tt)]) for i in range(2)],
                    r=[("kx",), QK(2 * h, tt), QK(2 * h + 1, tt)], w=[PS[bk]])
                op("act", act_fn(ebx[p][blk], ps[bk][:, :], AF.Exp, scale=scale), r=[PS[bk]], w=[EBX[p][blk]])

        def aback(n):
            h, tt = aits[n]
            p = n % 2
            pd = 6 + p
            for i in range(2):
                mmg(ps[2 + 2 * p + i][:, :],
                    [(vx[:, blk, h * 256 + i * 128:h * 256 + (i + 1) * 128], ebx[p][blk]) for blk in range(2)],
                    r=[("vx",), EBX[p][0], EBX[p][1]], w=[PS[2 + 2 * p + i]])
            mmg(ps[pd][:, :], [(ONE_1, ebx[p][0]), (ONE_1, ebx[p][1])],
                r=[EBX[p][0], EBX[p][1], ("ones",)], w=[PS[pd]])
            rd = tf[:, 2 + p, 0:512]
            emit_recip(ps[pd][:, :], rd, r=[PS[pd]], w=[TF[2 + p]])
            for i in range(2):
                op("dve", tt_(qk[:, 2 * h + i, tile_sl(tt)], ps[2 + 2 * p + i][:, :], rd, ALU.mult),
                   r=[PS[2 + 2 * p + i], TF[2 + p]], w=[QK(2 * h + i, tt)])

        for n in range(len(aits) + 1):
            if n < len(aits):
                afront(n)
            if n >= 1:
                aback(n - 1)
        s0, _ = ws.next(f"xo{l}_0")
        s1, _ = ws.next(f"xo{l}_1")
        wo = [wview(s0, 4, 1024), wview(s1, 4, 1024)]
        for tt in range(NT):
            for m in range(8):
                pi = 6 + (tt * 8 + m) % 2
                mmg(ps[pi][:, :],
                    [(wo[c // 4][:, c % 4, m * 128:(m + 1) * 128], qk[:, c, tile_sl(tt)]) for c in range(8)],
                    r=[QK(c, tt) for c in range(8)] + [("w", s0), ("w", s1)], w=[PS[pi]])
                add_to_x(m, tt, ps[pi][:, :], pi)
        ws.release(s0)
        ws.release(s1)

    def mlp(l):
        norm_to_hbuf(f"mlpg{l}")
        for g in range(8):
            s1, _ = ws.next(f"w1_{l}_{g}")
            s2, _ = ws.next(f"w2_{l}_{g}")
            w1 = wview(s1, 8, 512)
            w2 = wview(s2, 4, 1024)
            k = 0
            for j in range(4):
                for tt in range(NT):
                    pi = k % 4
                    ei_ = k % 2
                    k += 1
                    mmg(ps[pi][:, :], [(w1[:, kc, j * 128:(j + 1) * 128], hslice(kc, tt)) for kc in range(8)],
                        r=[HB(kc, tt) for kc in range(8)] + [("w", s1)], w=[PS[pi]])
                    op("act", act_fn(eb[ei_], ps[pi][:, :], AF.Relu), r=[PS[pi]], w=[EB[ei_]])
                    op("dve", tt_(qk[:, j, tile_sl(tt)], eb[ei_], eb[ei_], ALU.mult), r=[EB[ei_]], w=[QK(j, tt)])
            ws.release(s1)
            k = 0
            for tt in range(NT):
                for m in range(8):
                    pi = 4 + k % 4
                    k += 1
                    mmg(ps[pi][:, :], [(w2[:, j, m * 128:(m + 1) * 128], qk[:, j, tile_sl(tt)]) for j in range(4)],
                        r=[QK(j, tt) for j in range(4)] + [("w", s2)], w=[PS[pi]])
                    add_to_x(m, tt, ps[pi][:, :], pi)
            ws.release(s2)

    ei = 0
    for li, l in enumerate(layers):
        if li > 0:
            tk.eng["pool"].wait_ge(cc0, 1)
            tk.stream["pool"].append(("w", ("cc", id(cc0)), 1))
            dma("pool", f"g0_{li}", xhalo[:, :, :],
                g0[li].ap()[0:128, :].rearrange("p (c t) -> p c t", c=8), w=[("xhalo",)])
        if l % 2 == 0:
            even_mixer(l, li, ei)
            ei += 1
        else:
            odd_mixer(l, li)
        xattn(l)
        mlp(l)
        if li + 1 < len(layers):
            h0 = dma("pool", f"b0_{li + 1}", b0[li + 1].ap().rearrange("p (c t) -> p c t", c=8),
                     xres[:, :, T - HALO:T], r=[X(c, 3) for c in range(8)])
            cc0 = collective(b0[li + 1], g0[li + 1], [h0])

    for tt in range(NT):
        dma("sp", f"yout{tt}", y_d[:, :, tile_sl(tt)], xres[:, :, tile_sl(tt)],
            r=[X(c, tt) for c in range(8)])
    tk.wait_all("sp")
    assert ws.used == len(ws.blocks), (ws.used, len(ws.blocks))
    tk.check_deadlock()
    stack.close()
    return nc


WEIGHT_KEYS = ["ev_w_in", "ev_gate_a_w", "ev_gate_x_w", "ev_w_out", "od_pool_w",
               "xa_w_q", "xa_w_kv", "xa_w_o", "mlp_w1", "mlp_w2"]

_NC_CACHE = {}


def to_fm(a):
    t = a.shape[0]
    return np.ascontiguousarray(a.reshape(t, 8, 128).transpose(2, 1, 0))


def from_fm(a):
    t = a.shape[2]
    return np.ascontiguousarray(a.transpose(2, 1, 0).reshape(t, 1024))


def run_layers(inp, x_full, layers):
    key = tuple(layers)
    if key not in _NC_CACHE:
        _NC_CACHE[key] = build(list(layers))
    nc = _NC_CACHE[key]
    mask = make_mask()
    in_maps = []
    for core in range(8):
        b, half = core // 2, core % 2
        base = half * T
        m = {k: np.ascontiguousarray(np.asarray(inp[k], np.float32)) for k in WEIGHT_KEYS}
        m["x"] = to_fm(x_full[b, base:base + T])
        if half:
            m["xh"] = to_fm(x_full[b, base - HALO:base])
        else:
            m["xh"] = np.zeros((128, 8, HALO), np.float32)
        m["mem"] = to_fm(np.asarray(inp["mem"], np.float32)[b])
        m["params"] = pack_params(inp, half)
        m["mask"] = mask
        in_maps.append(m)
    res = run_bass_kernel_spmd(nc, in_maps, core_ids=list(range(8)))
    out = np.zeros_like(x_full)
    for core in range(8):
        b, half = core // 2, core % 2
        out[b, half * T:(half + 1) * T] = from_fm(np.asarray(res.results[core]["y"]))
    return out


FUSED = True


def kernel(**inp):
    x = np.ascontiguousarray(np.asarray(inp["x"], np.float32))
    if FUSED:
        return run_layers(inp, x, [0, 1, 2, 3])
    for l in range(4):
        x = run_layers(inp, x, [l])
    return x
```
